# Optimizing a Trainium2 kernel written in Bass

```python
import math
import jax
import jax.numpy as jnp
from jax import lax
import numpy as np

D_MODEL = 2048
BATCH = 4
SEQ = 2048
DEPTH = 1

MIX_WIDTH = D_MODEL
GDN_HEADS = 8
GDN_HEAD_DIM = 128
GDN_WIDTH = GDN_HEADS * GDN_HEAD_DIM
GDN_CONV = 4
GDN_CHUNK = 64
MOBA_HEADS = 8
MOBA_HEAD_DIM = (MIX_WIDTH - GDN_WIDTH) // MOBA_HEADS
MOBA_WIDTH = MOBA_HEADS * MOBA_HEAD_DIM
MOBA_BLOCK = 256
MOBA_TOPK = 3
MOBA_Q_CHUNK = 32
ROPE_DIM = MOBA_HEAD_DIM // 4
ROPE_THETA = 500000.0
D_FF = 256 * ((8 * D_MODEL // 3 + 255) // 256)
N_MOD = 9
NORM_EPS = 1e-6
IN_COLS = 4 * GDN_WIDTH + 2 * GDN_HEADS + 3 * MOBA_WIDTH

kernel_name = "hymba_gdn_moba_macaron_adaln_block"


def rms_norm(x, g):
    xf = x.astype(jnp.float32)
    y = xf * lax.rsqrt(jnp.mean(xf * xf, axis=-1, keepdims=True) + NORM_EPS)
    return (y * g.astype(jnp.float32)).astype(x.dtype)


def l2_normalize(x):
    return x * lax.rsqrt(jnp.sum(x * x, axis=-1, keepdims=True) + NORM_EPS)


def swiglu(h, w_gate, w_up, w_down):
    return (jax.nn.silu(h @ w_gate) * (h @ w_up)) @ w_down


def causal_depthwise_conv(x, w):
    K = w.shape[0]
    S = x.shape[1]
    xp = jnp.pad(x, ((0, 0), (K - 1, 0), (0, 0)))
    y = xp[:, 0:S] * w[0]
    for j in range(1, K):
        y = y + xp[:, j:j + S] * w[j]
    return y


def partial_rope(x, pos):
    inv_freq = ROPE_THETA ** (-jnp.arange(0, ROPE_DIM, 2, dtype=jnp.float32) / ROPE_DIM)
    ang = pos.astype(jnp.float32)[:, None] * inv_freq[None, :]
    cos, sin = jnp.cos(ang), jnp.sin(ang)
    x_rot, x_pass = x[..., :ROPE_DIM], x[..., ROPE_DIM:]
    half = ROPE_DIM // 2
    x1 = x_rot[..., :half].astype(jnp.float32)
    x2 = x_rot[..., half:].astype(jnp.float32)
    rot = jnp.concatenate([x1 * cos - x2 * sin, x2 * cos + x1 * sin], axis=-1)
    return jnp.concatenate([rot.astype(x.dtype), x_pass], axis=-1)


def chunked_gated_delta_rule(q, k, v, g, beta):
    B, H, S, dk = q.shape
    dv = v.shape[-1]
    C = GDN_CHUNK
    S_pad = -(-S // C) * C
    pad = S_pad - S
    q, k, v = [jnp.pad(t, ((0, 0), (0, 0), (0, pad), (0, 0))) for t in (q, k, v)]
    g, beta = [jnp.pad(t, ((0, 0), (0, 0), (0, pad))) for t in (g, beta)]
    N = S_pad // C
    q = q * (dk ** -0.5)
    qc = q.reshape(B, H, N, C, dk)
    kc = k.reshape(B, H, N, C, dk)
    vc = v.reshape(B, H, N, C, dv)
    bc = beta.reshape(B, H, N, C)
    G = jnp.cumsum(g.reshape(B, H, N, C), axis=-1)
    tril = jnp.tril(jnp.ones((C, C), dtype=bool))
    strict = jnp.tril(jnp.ones((C, C), dtype=bool), -1)
    decay = jnp.exp(jnp.where(tril, G[..., :, None] - G[..., None, :], -jnp.inf))
    k_beta = kc * bc[..., None]
    v_beta = vc * bc[..., None]
    L = jnp.where(strict, jnp.einsum('bhnid,bhnjd->bhnij', k_beta, kc) * decay, 0.0)
    u = lax.linalg.triangular_solve(L, v_beta, left_side=True, lower=True, unit_diagonal=True)
    w = lax.linalg.triangular_solve(L, k_beta * jnp.exp(G)[..., None], left_side=True, lower=True,
                                    unit_diagonal=True)
    attn_intra = jnp.where(tril, jnp.einsum('bhnid,bhnjd->bhnij', qc, kc) * decay, 0.0)

    def step(state, inp):
        q_i, k_i, u_i, w_i, G_i, A_i = inp
        v_new = u_i - jnp.einsum('bhck,bhkv->bhcv', w_i, state)
        o_i = (jnp.einsum('bhck,bhkv->bhcv', q_i * jnp.exp(G_i)[..., None], state)
               + jnp.einsum('bhcj,bhjv->bhcv', A_i, v_new))
        g_last = G_i[..., -1]
        k_dec = k_i * jnp.exp(g_last[..., None] - G_i)[..., None]
        state = state * jnp.exp(g_last)[..., None, None] + jnp.einsum('bhck,bhcv->bhkv', k_dec, v_new)
        return state, o_i

    xs = tuple(jnp.moveaxis(t, 2, 0) for t in (qc, kc, u, w, G, attn_intra))
    state0 = jnp.zeros((B, H, dk, dv), jnp.float32)
    _, o = lax.scan(step, state0, xs)
    o = jnp.moveaxis(o, 0, 2).reshape(B, H, S_pad, dv)
    return o[:, :, :S]


def gated_deltanet(q, k, v, z, a, b, conv_w, a_log, dt_bias, norm_g):
    B, S, _ = q.shape
    dtype = q.dtype
    qkv = jax.nn.silu(causal_depthwise_conv(jnp.concatenate([q, k, v], axis=-1), conv_w))
    q, k, v = jnp.split(qkv.astype(jnp.float32), 3, axis=-1)
    to_heads = lambda t: t.reshape(B, S, GDN_HEADS, GDN_HEAD_DIM).transpose(0, 2, 1, 3)
    q = l2_normalize(to_heads(q))
    k = l2_normalize(to_heads(k))
    v = to_heads(v)
    beta = jax.nn.sigmoid(b.astype(jnp.float32)).transpose(0, 2, 1)
    g = (-jnp.exp(a_log.astype(jnp.float32))
         * jax.nn.softplus(a.astype(jnp.float32) + dt_bias.astype(jnp.float32))).transpose(0, 2, 1)
    o = chunked_gated_delta_rule(q, k, v, g, beta).transpose(0, 2, 1, 3)
    zf = z.astype(jnp.float32).reshape(B, S, GDN_HEADS, GDN_HEAD_DIM)
    o = rms_norm(o, norm_g) * jax.nn.silu(zf)
    return o.reshape(B, S, GDN_WIDTH).astype(dtype)


def moba_attention(q, k, v):
    B, H, S, hd = q.shape
    S_pad = -(-S // MOBA_BLOCK) * MOBA_BLOCK
    pad = S_pad - S
    q, k, v = [jnp.pad(t, ((0, 0), (0, 0), (0, pad), (0, 0))) for t in (q, k, v)]
    NB = S_pad // MOBA_BLOCK
    top = min(MOBA_TOPK, NB)
    kb = k.reshape(B, H, NB, MOBA_BLOCK, hd)
    vb = v.reshape(B, H, NB, MOBA_BLOCK, hd)
    kmean = jnp.mean(kb.astype(jnp.float32), axis=3)
    NQ = S_pad // MOBA_Q_CHUNK
    q_chunks = jnp.moveaxis(q.reshape(B, H, NQ, MOBA_Q_CHUNK, hd), 2, 0)
    bi = jnp.arange(B)[:, None, None, None]
    hi = jnp.arange(H)[None, :, None, None]
    scale = hd ** -0.5

    def one_chunk(args):
        q_i, ci = args
        qpos = ci * MOBA_Q_CHUNK + jnp.arange(MOBA_Q_CHUNK)
        own = (ci * MOBA_Q_CHUNK) // MOBA_BLOCK
        gate = jnp.einsum('bhqd,bhnd->bhqn', q_i.astype(jnp.float32), kmean)
        gate = jnp.where(jnp.arange(NB) < own, gate, -jnp.inf)
        _, sel = lax.top_k(gate, top)
        valid = jnp.arange(top) < own
        k_sel = kb[bi, hi, sel]
        v_sel = vb[bi, hi, sel]
        k_own = lax.dynamic_index_in_dim(kb, own, axis=2, keepdims=False)
        v_own = lax.dynamic_index_in_dim(vb, own, axis=2, keepdims=False)
        s_sel = jnp.einsum('bhqd,bhqtkd->bhqtk', q_i, k_sel).astype(jnp.float32) * scale
        s_sel = jnp.where(valid[:, None], s_sel, -jnp.inf)
        kpos = own * MOBA_BLOCK + jnp.arange(MOBA_BLOCK)
        s_own = jnp.einsum('bhqd,bhkd->bhqk', q_i, k_own).astype(jnp.float32) * scale
        s_own = jnp.where(kpos[None, :] <= qpos[:, None], s_own, -jnp.inf)
        s = jnp.concatenate([s_sel.reshape(B, H, MOBA_Q_CHUNK, top * MOBA_BLOCK), s_own], axis=-1)
        p = jax.nn.softmax(s, axis=-1).astype(v.dtype)
        p_sel = p[..., :top * MOBA_BLOCK].reshape(B, H, MOBA_Q_CHUNK, top, MOBA_BLOCK)
        p_own = p[..., top * MOBA_BLOCK:]
        return (jnp.einsum('bhqtk,bhqtkd->bhqd', p_sel, v_sel)
                + jnp.einsum('bhqk,bhkd->bhqd', p_own, v_own))

    out = lax.map(one_chunk, (q_chunks, jnp.arange(NQ)))
    out = jnp.moveaxis(out, 0, 2).reshape(B, H, S_pad, hd)
    return out[:, :, :S]


def hybrid_mixer(h, w_in, w_out, conv_w, a_log, dt_bias, norm_g, pos):
    B, S, _ = h.shape
    proj = h @ w_in
    sizes = [GDN_WIDTH] * 4 + [GDN_HEADS] * 2 + [MOBA_WIDTH] * 3
    cuts = [int(v) for v in np.cumsum(sizes)[:-1]]
    gq, gk, gv, gz, ga, gb, mq, mk, mv = jnp.split(proj, cuts, axis=-1)
    y_gdn = gated_deltanet(gq, gk, gv, gz, ga, gb, conv_w, a_log, dt_bias, norm_g)
    to_heads = lambda t: t.reshape(B, S, MOBA_HEADS, MOBA_HEAD_DIM).transpose(0, 2, 1, 3)
    mq = partial_rope(to_heads(mq), pos)
    mk = partial_rope(to_heads(mk), pos)
    y_moba = moba_attention(mq, mk, to_heads(mv)).transpose(0, 2, 1, 3).reshape(B, S, MOBA_WIDTH)
    return jnp.concatenate([y_gdn, y_moba.astype(y_gdn.dtype)], axis=-1) @ w_out


def setup_inputs(seed: int = 0) -> dict:
    key = jax.random.key(seed)
    ks = jax.random.split(key, 24)
    f32 = jnp.float32

    def dense(k, fan_in, shape, gain=1.0):
        return jax.random.normal(k, shape, f32) * (gain * fan_in ** -0.5)

    def gain_vec(k, n):
        return 1.0 + 0.1 * jax.random.normal(k, (DEPTH, n), f32)

    x = jax.random.normal(ks[0], (BATCH, SEQ, D_MODEL), f32)
    c = jax.random.normal(ks[1], (BATCH, D_MODEL), f32)
    w_ada = dense(ks[2], D_MODEL, (DEPTH, D_MODEL, N_MOD * D_MODEL), 0.5)
    b_ada = 0.02 * jax.random.normal(ks[3], (DEPTH, N_MOD * D_MODEL), f32)
    ffn1_pre_g = gain_vec(ks[4], D_MODEL)
    ffn1_post_g = gain_vec(ks[5], D_MODEL)
    ffn1_w_gate = dense(ks[6], D_MODEL, (DEPTH, D_MODEL, D_FF))
    ffn1_w_up = dense(ks[7], D_MODEL, (DEPTH, D_MODEL, D_FF))
    ffn1_w_down = dense(ks[8], D_FF, (DEPTH, D_FF, D_MODEL))
    mix_pre_g = gain_vec(ks[9], D_MODEL)
    mix_post_g = gain_vec(ks[10], D_MODEL)
    w_in = dense(ks[11], D_MODEL, (DEPTH, D_MODEL, IN_COLS))
    gdn_conv_w = dense(ks[12], GDN_CONV, (DEPTH, GDN_CONV, 3 * GDN_WIDTH))
    gdn_a_log = jnp.log(jax.random.uniform(ks[13], (DEPTH, GDN_HEADS), f32, 1.0, 16.0))
    dt = jnp.exp(jax.random.uniform(ks[14], (DEPTH, GDN_HEADS), f32, math.log(1e-3), math.log(1e-1)))
    gdn_dt_bias = dt + jnp.log(-jnp.expm1(-dt))
    gdn_norm_g = gain_vec(ks[15], GDN_HEAD_DIM)
    w_out = dense(ks[16], MIX_WIDTH, (DEPTH, MIX_WIDTH, D_MODEL))
    ffn2_pre_g = gain_vec(ks[17], D_MODEL)
    ffn2_post_g = gain_vec(ks[18], D_MODEL)
    ffn2_w_gate = dense(ks[19], D_MODEL, (DEPTH, D_MODEL, D_FF))
    ffn2_w_up = dense(ks[20], D_MODEL, (DEPTH, D_MODEL, D_FF))
    ffn2_w_down = dense(ks[21], D_FF, (DEPTH, D_FF, D_MODEL))
    return {"x": x, "c": c, "w_ada": w_ada, "b_ada": b_ada,
            "ffn1_pre_g": ffn1_pre_g, "ffn1_post_g": ffn1_post_g,
            "ffn1_w_gate": ffn1_w_gate, "ffn1_w_up": ffn1_w_up, "ffn1_w_down": ffn1_w_down,
            "mix_pre_g": mix_pre_g, "mix_post_g": mix_post_g, "w_in": w_in,
            "gdn_conv_w": gdn_conv_w, "gdn_a_log": gdn_a_log, "gdn_dt_bias": gdn_dt_bias,
            "gdn_norm_g": gdn_norm_g, "w_out": w_out,
            "ffn2_pre_g": ffn2_pre_g, "ffn2_post_g": ffn2_post_g,
            "ffn2_w_gate": ffn2_w_gate, "ffn2_w_up": ffn2_w_up, "ffn2_w_down": ffn2_w_down}


def reference(x, c, w_ada, b_ada, ffn1_pre_g, ffn1_post_g, ffn1_w_gate, ffn1_w_up, ffn1_w_down,
              mix_pre_g, mix_post_g, w_in, gdn_conv_w, gdn_a_log, gdn_dt_bias, gdn_norm_g, w_out,
              ffn2_pre_g, ffn2_post_g, ffn2_w_gate, ffn2_w_up, ffn2_w_down):
    B, S, _ = x.shape
    pos = jnp.arange(S)
    for l in range(DEPTH):
        mod = (jax.nn.silu(c) @ w_ada[l] + b_ada[l]).reshape(B, N_MOD, D_MODEL)[:, :, None, :]
        sh1, sc1, ga1, sh2, sc2, ga2, sh3, sc3, ga3 = [mod[:, i] for i in range(N_MOD)]
        h = rms_norm(x, ffn1_pre_g[l]) * (1 + sc1) + sh1
        y = swiglu(h, ffn1_w_gate[l], ffn1_w_up[l], ffn1_w_down[l])
        x = x + 0.5 * ga1 * rms_norm(y, ffn1_post_g[l])
        h = rms_norm(x, mix_pre_g[l]) * (1 + sc2) + sh2
        y = hybrid_mixer(h, w_in[l], w_out[l], gdn_conv_w[l], gdn_a_log[l], gdn_dt_bias[l],
                         gdn_norm_g[l], pos)
        x = x + ga2 * rms_norm(y, mix_post_g[l])
        h = rms_norm(x, ffn2_pre_g[l]) * (1 + sc3) + sh3
        y = swiglu(h, ffn2_w_gate[l], ffn2_w_up[l], ffn2_w_down[l])
        x = x + 0.5 * ga3 * rms_norm(y, ffn2_post_g[l])
    return x
```

```python
import contextlib
import numpy as np
import concourse.bass as bass
import concourse.mybir as mybir
from concourse.bass_utils import run_bass_kernel_spmd

F32 = mybir.dt.float32
BF16 = mybir.dt.bfloat16
AF = mybir.ActivationFunctionType
ALU = mybir.AluOpType
AX = mybir.AxisListType

ENGS = ("pe", "act", "dve", "pool", "sp")

D = 2048
KC = 16
FF = 5632
FC = 44
NMOD = 9
INC = 7184
EPS = 1e-6
NEG = -30000.0


class Sched:
    def __init__(self, nc, n_dma_sems=48):
        self.nc = nc
        self.ops = []
        self.last_w = {}
        self.readers = {}
        self.n_dma_sems = n_dma_sems

    def add(self, eng, emit, reads=(), writes=(), dma=False):
        idx = len(self.ops)
        deps = set()
        for b in reads:
            if b in self.last_w:
                deps.add(self.last_w[b])
        for b in writes:
            if b in self.last_w:
                deps.add(self.last_w[b])
            rd = self.readers.get(b)
            if rd:
                deps.update(rd.values())
        deps.discard(idx)
        self.ops.append(dict(eng=eng, emit=emit, deps=deps, dma=dma, barrier=False))
        for b in reads:
            self.readers.setdefault(b, {})[("dma", idx) if dma else eng] = idx
        for b in writes:
            self.last_w[b] = idx
            self.readers[b] = {}
        return idx

    def barrier(self):
        self.ops.append(dict(eng=None, emit=None, deps=set(), dma=False, barrier=True))
        self.last_w = {}
        self.readers = {}

    def dma(self, q, out, in_, reads=(), writes=()):
        return self.add(q, lambda e: e.dma_start(out=out, in_=in_), reads, writes, dma=True)

    def finalize(self, stack):
        nc = self.nc
        ops = self.ops
        dma_count = 0
        for op in ops:
            op["signal"] = False
            if op["dma"]:
                op["dma_ord"] = dma_count
                dma_count += 1
        last_c = {}
        for i, op in enumerate(ops):
            if op["barrier"]:
                for e, j in last_c.items():
                    ops[j]["signal"] = True
                op["last_c"] = dict(last_c)
                continue
            for d in op["deps"]:
                od = ops[d]
                if od["dma"]:
                    continue
                if not (od["eng"] == "pe" and op["eng"] == "pe"):
                    od["signal"] = True
            if not op["dma"]:
                last_c[op["eng"]] = i
        sigcount = {e: 0 for e in ENGS}
        for op in ops:
            if op["barrier"] or op["dma"]:
                continue
            if op["signal"]:
                sigcount[op["eng"]] += 1
                op["sigval"] = sigcount[op["eng"]]
        esem = {e: stack.enter_context(nc.semaphore("s_" + e)) for e in ENGS}
        nd = min(self.n_dma_sems, max(1, dma_count))
        dsem = [stack.enter_context(nc.semaphore("d%d" % i)) for i in range(nd)]
        for op in ops:
            if op["dma"]:
                k = op["dma_ord"]
                op["dsem"] = dsem[k % nd]
                op["dval"] = 16 * (k // nd + 1)
                op["dslot"] = k % nd
        waited = {e: {} for e in ENGS}
        per_eng = {e: [] for e in ENGS}
        last_on_slot = {}
        pending = {e: [] for e in ENGS}
        for i, op in enumerate(ops):
            if op["barrier"]:
                for e in ENGS:
                    lst = []
                    for e2, j in op["last_c"].items():
                        lst.append((("e", e2), esem[e2], ops[j]["sigval"]))
                    for slot, o in last_on_slot.items():
                        lst.append((("d", slot), o["dsem"], o["dval"]))
                    pending[e] = lst
                continue
            e = op["eng"]
            waits = []

            def need(key, sem, val):
                if waited[e].get(key, 0) < val:
                    waited[e][key] = val
                    waits.append((sem, val))

            for key, sem, val in pending[e]:
                need(key, sem, val)
            pending[e] = []
            for d in sorted(op["deps"]):
                od = ops[d]
                if od["dma"]:
                    need(("d", od["dslot"]), od["dsem"], od["dval"])
                elif not (od["eng"] == "pe" and e == "pe"):
                    need(("e", od["eng"]), esem[od["eng"]], od["sigval"])
            if op["dma"]:
                prev = last_on_slot.get(op["dslot"])
                if prev is not None:
                    need(("d", op["dslot"]), prev["dsem"], prev["dval"])
                last_on_slot[op["dslot"]] = op
            op["waits"] = waits
            per_eng[e].append(op)
        self.esem = esem
        self.per_eng = per_eng
        self.final_dma_waits = [(o["dsem"], o["dval"]) for o in last_on_slot.values()]

    def emit_engine(self, e, eng, final=False):
        for op in self.per_eng[e]:
            for sem, val in op["waits"]:
                eng.wait_ge(sem, val)
            ins = op["emit"](eng)
            if op["dma"]:
                ins.then_inc(op["dsem"], 16)
            elif op["signal"]:
                ins.then_inc(self.esem[e], 1)
        if final:
            for sem, val in self.final_dma_waits:
                eng.wait_ge(sem, val)

    def run_block(self):
        nc = self.nc
        with nc.Block() as block:
            @block.tensor
            def _(eng):
                self.emit_engine("pe", eng)

            @block.scalar
            def _(eng):
                self.emit_engine("act", eng)

            @block.vector
            def _(eng):
                self.emit_engine("dve", eng)

            @block.gpsimd
            def _(eng):
                self.emit_engine("pool", eng)

            @block.sync
            def _(eng):
                self.emit_engine("sp", eng, final=True)


C_ID, C_ONE, C_TRIU, C_SGTJ, C_TRIL, C_STRICT, C_PM, C_EBLK = 0, 128, 256, 384, 512, 640, 768, 800
NCONST = 800 + 1024
V_C, V_BADA, V_G = 0, 16, 160
V_CONV = 256
V_ALOG = 352
V_DTB = 480
V_NG = 608
NVEC = 609


def build_nc(T, phases=(0, 1, 2, 3), gdn_on=True, moba_on=True, lvl=99, ngh=8, nmh=8):
    NT = T // 512
    NTL = T // 128
    NB = T // 256
    nc = bass.Bass("TRN2", target_bir_lowering=False)
    dr = lambda name, shape, dt=F32, kind="ExternalInput": nc.dram_tensor(name, shape, dt, kind=kind).ap()
    x_d = dr("x", [T, D])
    consts_d = dr("consts", [128, NCONST])
    vecs_d = dr("vecs", [128, NVEC])
    rope_d = dr("rope", [32, 2 * T])
    wada_d = dr("w_ada", [D, NMOD * D])
    w1g_d, w1u_d, w1d_d = dr("w1g", [D, FF]), dr("w1u", [D, FF]), dr("w1d", [FF, D])
    w2g_d, w2u_d, w2d_d = dr("w2g", [D, FF]), dr("w2u", [D, FF]), dr("w2d", [FF, D])
    win_d = dr("w_in", [D, INC])
    wout_d = dr("w_out", [D, D])
    out_d = dr("out", [T, D], kind="ExternalOutput")
    X1_d = dr("X1s", [D, T], F32, "Internal")
    X0_d = dr("X0s", [D, T], F32, "Internal")
    YF_d = dr("YFs", [D, 1024], F32, "Internal")
    H2_d = dr("H2s", [D, T], BF16, "Internal")
    Y_d = dr("Ys", [D, T], BF16, "Internal")

    st = contextlib.ExitStack()
    with st:
        S = Sched(nc)
        arena = st.enter_context(nc.sbuf_tensor("arena", [128, 96 * 1024], BF16))
        cst = st.enter_context(nc.sbuf_tensor("cst", [128, NCONST], F32))
        cstb = st.enter_context(nc.sbuf_tensor("cstb", [128, NCONST], BF16))
        vec = st.enter_context(nc.sbuf_tensor("vec", [128, NVEC], F32))
        modv = st.enter_context(nc.sbuf_tensor("modv", [128, 144 + 9 * 16], F32))
        PS = [st.enter_context(nc.psum_tensor("ps%d" % i, [128, 512], F32)) for i in range(8)]

        apos = [0]

        def alloc(nbytes):
            o = apos[0]
            apos[0] += (nbytes + 63) // 64 * 32
            assert apos[0] <= 96 * 1024, apos[0]
            return o

        def vf32(off, n):
            return arena[:, off:off + 2 * n].bitcast(F32)

        def vbf(off, n):
            return arena[:, off:off + n]

        def MM(out, lhsT, rhs, start=True, stop=True, r=(), w=()):
            S.add("pe", lambda e: e.matmul(out, lhsT, rhs, start=start, stop=stop), r, w)

        def TR(out, in_, r=(), w=()):
            S.add("pe", lambda e: e.transpose(out, in_, cst[:, C_ID:C_ID + 128]), r, w)

        def ACT(out, in_, func, r=(), w=(), **kw):
            S.add("act", lambda e: e.activation(out, in_, func, **kw), r, w)

        def CP(eng, out, in_, r=(), w=()):
            if eng == "act":
                S.add("act", lambda e: e.copy(out, in_), r, w)
            else:
                S.add(eng, lambda e: e.tensor_copy(out, in_), r, w)

        def TS(eng, out, in0, s1, s2, op0, op1=None, r=(), w=()):
            if op1 is None:
                S.add(eng, lambda e: e.tensor_single_scalar(out, in0, s1, op0), r, w)
            else:
                S.add(eng, lambda e: e.tensor_scalar(out, in0, s1, s2, op0, op1), r, w)

        def STT(eng, out, in0, sc, in1, op0, op1, r=(), w=()):
            S.add(eng, lambda e: e.scalar_tensor_tensor(out, in0, sc, in1, op0, op1), r, w)

        def TT(eng, out, in0, in1, op, r=(), w=()):
            S.add(eng, lambda e: e.tensor_tensor(out, in0, in1, op), r, w)

        def MEMSET(eng, ap, val, r=(), w=()):
            S.add(eng, lambda e: e.memset(ap, val), r, w)

        def PK(b, lo=0, hi=512):
            return ["ps%d" % b]

        ident = cst[:, C_ID:C_ID + 128]
        ones_f = cst[:, C_ONE:C_ONE + 128]
        triu_f = cst[:, C_TRIU:C_TRIU + 128]
        sgtj_f = cst[:, C_SGTJ:C_SGTJ + 128]
        tril_f = cst[:, C_TRIL:C_TRIL + 128]
        strict_f = cst[:, C_STRICT:C_STRICT + 128]
        ones_b = cstb[:, C_ONE:C_ONE + 128]
        triu_b = cstb[:, C_TRIU:C_TRIU + 128]

        S.dma("sp", cst[:], consts_d, writes=["cst"])
        S.dma("sp", vec[:], vecs_d, writes=["vec"])
        CP("dve", cstb[:], cst[:], r=["cst"], w=["cstb"])

        NSTG, NWB = 2, 8
        stg_off = [alloc(8192), alloc(8192)]
        wb_off = [alloc(4096) for _ in range(8)]
        ring = dict(s=0, w=0, c=0, nstg=NSTG, nwb=NWB)
        cast_engs = ("act", "pool", "dve", "pool")

        def wblock(src, nk, ncols, engs=cast_engs):
            si = ring["s"] % ring["nstg"]
            wi = ring["w"] % ring["nwb"]
            ring["s"] += 1
            ring["w"] += 1
            sv = vf32(stg_off[si], nk * ncols).rearrange("p (k c) -> p k c", k=nk)
            wv = vbf(wb_off[wi], nk * ncols).rearrange("p (k c) -> p k c", k=nk)
            S.dma("sp", sv, src.rearrange("(k p) c -> p k c", p=128), writes=["stg%d" % si])
            eng = engs[ring["c"] % len(engs)]
            ring["c"] += 1
            CP(eng, wv, sv, r=["stg%d" % si], w=["wb%d" % wi])
            return wv, "wb%d" % wi

        main_o = apos[0]
        hT_o = alloc(32768)
        actT_o = alloc(FC * 2048)
        cx_o = [alloc(4096), alloc(4096)]
        cy_o = [alloc(4096), alloc(4096)]
        rstd_o = alloc(4096)
        sq_o = alloc(2048)
        scb_o = alloc(64)
        row_o = cy_o

        hT = vbf(hT_o, 16 * 1024).rearrange("p (k t) -> p k t", k=16)
        actT = vbf(actT_o, FC * 1024).rearrange("p (k t) -> p k t", k=FC)
        xTf = vf32(actT_o, 16 * 1024).rearrange("p (k t) -> p k t", k=16)
        xin = [vf32(actT_o + 32768, 2048), vf32(actT_o + 32768 + 4096, 2048)]
        cx = [vf32(o, 1024) for o in cx_o]
        cy = [vf32(o, 1024) for o in cy_o]
        rstdB = vf32(rstd_o, 1024)
        sqb = vbf(sq_o, 1024)

        def mcol(i):
            return modv[:, i * 16:(i + 1) * 16]
        Sv = [modv[:, 144 + i * 16:144 + (i + 1) * 16] for i in range(3)]
        coef = [modv[:, 144 + 48 + i * 16:144 + 48 + (i + 1) * 16] for i in range(3)]
        tmpv = modv[:, 144 + 96:144 + 112]

        def gain(i):
            return vec[:, V_G + i * 16:V_G + (i + 1) * 16]

        def phase0():
            scb = vbf(scb_o, 16)
            ACT(scb, vec[:, V_C:V_C + 16], AF.Silu, r=["vec"], w=["scb"])
            pm = PS[7]
            for cb in range(72):
                blks = [wblock(wada_d[kh * 1024:(kh + 1) * 1024, cb * 256:(cb + 1) * 256], 8, 256) for kh in range(2)]
                pr = PS[cb % 2]
                prk = PK(cb % 2, 0, 256)
                for kc in range(16):
                    wv, wk = blks[kc // 8]
                    MM(pr[0:1, 0:256], scb[:, kc:kc + 1], wv[:, kc % 8, :], start=(kc == 0), stop=(kc == 15),
                       r=["scb", wk], w=prk)
                rw = vf32(row_o[cb % 2], 256)
                rk = "row%d" % (cb % 2)
                CP("dve", rw[0:1, :], pr[0:1, 0:256], r=prk, w=[rk])
                for j in range(2):
                    MM(pm[:, cb * 2 + j:cb * 2 + j + 1], rw[0:1, j * 128:(j + 1) * 128], cst[0:1, C_ONE:C_ONE + 1],
                       r=[rk, "cst"], w=PK(7, 0, 256))
            TT("dve", modv[:, 0:144], pm[:, 0:144], vec[:, V_BADA:V_BADA + 144], ALU.add, r=PK(7, 0, 256) + ["vec"], w=["modv"])
            for i in range(3):
                TS("dve", tmpv, mcol(3 * i + 1), 1.0, None, ALU.add, r=["modv"], w=["tmpv"])
                TT("dve", Sv[i], tmpv, gain(2 * i), ALU.mult, r=["tmpv", "vec"], w=["Sv%d" % i])
                STT("dve", coef[i], mcol(3 * i + 2), (1.0 if i == 1 else 0.5), gain(2 * i + 1), ALU.mult, ALU.mult,
                    r=["modv", "vec"], w=["coef%d" % i])

        HK = ["H%d" % k for k in range(16)]
        AK = ["A%d" % k for k in range(FC)]
        XFK = lambda kc: [AK[2 * kc], AK[2 * kc + 1]]
        X0v = X0_d.rearrange("(k p) t -> p k t", p=128)
        X1v = X1_d.rearrange("(k p) t -> p k t", p=128)
        H2v = H2_d.rearrange("(k p) t -> p k t", p=128)
        Yv = Y_d.rearrange("(k p) t -> p k t", p=128)
        YFv = YF_d.rearrange("(k p) t -> p k t", p=128)

        def stat_acc(src, srckeys, first, last):
            ACT(sqb, src, AF.Square, r=srckeys, w=["sq"])
            for hf in range(2):
                MM(PS[6 + hf][:, :], ones_b, sqb[:, hf * 512:(hf + 1) * 512], start=first, stop=last,
                   r=["cstb", "sq"], w=PK(6 + hf))

        def stat_fin():
            for hf in range(2):
                ACT(rstdB[:, hf * 512:(hf + 1) * 512], PS[6 + hf][:, :], AF.Ln, r=PK(6 + hf), w=["rstdB"],
                    scale=1.0 / D, bias=EPS)
            ACT(rstdB, rstdB, AF.Exp, r=["rstdB"], w=["rstdB"], scale=-0.5)

        def mod_to_hT(i, kc, src, srckeys):
            t = cy[kc % 2]
            STT("dve", t, src, Sv[i][:, kc:kc + 1], rstdB, ALU.mult, ALU.mult,
                r=srckeys + ["Sv%d" % i, "rstdB"], w=["cy%d" % (kc % 2)])
            ACT(hT[:, kc, :], t, AF.Identity, r=["cy%d" % (kc % 2), "modv"], w=[HK[kc]], bias=mcol(3 * i)[:, kc:kc + 1])

        def prenorm_dram(i, Xsrc, ts, xkey):
            stat_fin()
            for kc in range(16):
                S.dma("sp", cx[kc % 2], Xsrc[:, kc, ts], reads=[xkey], writes=["cx%d" % (kc % 2)])
                mod_to_hT(i, kc, cx[kc % 2], ["cx%d" % (kc % 2)])

        def y_sink(dm, buf, bkey):
            stat_acc(buf, [bkey], dm == 0, dm == 15)
            S.dma("sp", YFv[:, dm, :], buf, reads=[bkey], writes=["YF%d" % dm])

        def proj_T(inT, inkeys, nk, W):
            groups = [(g, min(8, nk - g)) for g in range(0, nk, 8)]
            for db in range(8):
                for (g0, n) in groups:
                    wv, wk = wblock(W[g0 * 128:(g0 + n) * 128, db * 256:(db + 1) * 256], n, 256)
                    for sub in range(2):
                        for hf in range(2):
                            bk = 2 + sub * 2 + hf
                            for f in range(n):
                                k = g0 + f
                                MM(PS[bk][:, :], wv[:, f, sub * 128:(sub + 1) * 128], inT[:, k, hf * 512:(hf + 1) * 512],
                                   start=(k == 0), stop=(k == nk - 1), r=[wk, inkeys[k]], w=PK(bk))
                for sub in range(2):
                    dm = db * 2 + sub
                    buf, bkey = cy[dm % 2], "cy%d" % (dm % 2)
                    for hf in range(2):
                        bk = 2 + sub * 2 + hf
                        CP("act" if bk % 2 == 0 else "dve", buf[:, hf * 512:(hf + 1) * 512], PS[bk][:, :], r=PK(bk), w=[bkey])
                    y_sink(dm, buf, bkey)

        def ffn(Wg, Wu, Wd):
            for fb in range(22):
                blk = {}
                for mi, Wm in enumerate((Wg, Wu)):
                    for kh in range(2):
                        blk[(mi, kh)] = wblock(Wm[kh * 1024:(kh + 1) * 1024, fb * 256:(fb + 1) * 256], 8, 256,
                                               engs=("act", "pool"))
                for sub in range(2):
                    fc = fb * 2 + sub
                    for hf in range(2):
                        hs = slice(hf * 512, (hf + 1) * 512)
                        for mi in range(2):
                            bk = 2 * hf + mi
                            for kc in range(16):
                                wv, wk = blk[(mi, kc // 8)]
                                MM(PS[bk][:, :], wv[:, kc % 8, sub * 128:(sub + 1) * 128], hT[:, kc, hs],
                                   start=(kc == 0), stop=(kc == 15), r=[wk, HK[kc]], w=PK(bk))
                        tb, tk = cy[hf][:, 0:512], "cy%d" % hf
                        ACT(tb, PS[2 * hf][:, :], AF.Silu, r=PK(2 * hf), w=[tk])
                        TT("dve", actT[:, fc, hs], tb, PS[2 * hf + 1][:, :], ALU.mult, r=[tk] + PK(2 * hf + 1), w=[AK[fc]])
            proj_T(actT, AK, FC, Wd)

        def post_res(i, Xsrc, ts, xkey, dst, need_stats):
            stat_fin()
            for kc in range(16):
                cb, ck = cx[kc % 2], "cx%d" % (kc % 2)
                yb, yk = cy[kc % 2], "cy%d" % (kc % 2)
                S.dma("sp", yb, YFv[:, kc, :], reads=["YF%d" % kc], writes=[yk])
                S.dma("sp", cb, Xsrc[:, kc, ts], reads=[xkey], writes=[ck])
                STT("dve", yb, yb, coef[i][:, kc:kc + 1], rstdB, ALU.mult, ALU.mult, r=[yk, "coef%d" % i, "rstdB"], w=[yk])
                TT("pool", cb, cb, yb, ALU.add, r=[ck, yk], w=[ck])
                if need_stats:
                    stat_acc(cb, [ck], kc == 0, kc == 15)
                dst(kc, cb, ck)

        def phase1():
            for t in range(T // 1024):
                ts = slice(t * 1024, (t + 1) * 1024)
                for j in range(8):
                    xb, xbk = xin[j % 2], AK[32 + 4 * (j % 2):36 + 4 * (j % 2)]
                    S.dma("sp", xb, x_d[t * 1024 + j * 128:t * 1024 + (j + 1) * 128, :], writes=xbk)
                    for q4 in range(4):
                        bk = 4 + q4 % 2
                        for c in range(4):
                            kc = q4 * 4 + c
                            TR(PS[bk][:, c * 128:(c + 1) * 128], xb[:, kc * 128:(kc + 1) * 128], r=["cst"] + xbk, w=PK(bk))
                        CP("act" if bk % 2 == 0 else "dve", xTf[:, q4 * 4:(q4 + 1) * 4, j * 128:(j + 1) * 128],
                           PS[bk][:, :].rearrange("p (c t) -> p c t", c=4), r=PK(bk), w=AK[8 * q4:8 * q4 + 8])
                S.dma("sp", X0v[:, :, ts], xTf, reads=AK[0:32], writes=["X0s"])
                for kc in range(16):
                    stat_acc(xTf[:, kc, :], XFK(kc), kc == 0, kc == 15)
                stat_fin()
                for kc in range(16):
                    mod_to_hT(0, kc, xTf[:, kc, :], XFK(kc))
                ffn(w1g_d, w1u_d, w1d_d)
                post_res(0, X0v, ts, "X0s", lambda kc, cb, ck: S.dma("sp", X1v[:, kc, ts], cb, reads=[ck], writes=["X1s"]), True)
                prenorm_dram(1, X1v, ts, "X1s")
                S.dma("sp", H2v[:, :, ts], hT, reads=HK, writes=["H2s"])

        def phase3():
            for t in range(T // 1024):
                ts = slice(t * 1024, (t + 1) * 1024)
                S.dma("sp", hT, Yv[:, :, ts], reads=["Ys"], writes=HK)
                proj_T(hT, HK, 16, wout_d)
                post_res(1, X1v, ts, "X1s", lambda kc, cb, ck: S.dma("sp", X0v[:, kc, ts], cb, reads=[ck], writes=["X0s"]), True)
                prenorm_dram(2, X0v, ts, "X0s")
                ffn(w2g_d, w2u_d, w2d_d)
                post_res(2, X0v, ts, "X0s",
                         lambda kc, cb, ck: CP("act", xTf[:, kc, :], cb, r=[ck], w=XFK(kc)), False)
                for j in range(8):
                    xb, xbk = xin[j % 2], AK[32 + 4 * (j % 2):36 + 4 * (j % 2)]
                    for q4 in range(4):
                        bk = 4 + q4 % 2
                        for c in range(4):
                            kc = q4 * 4 + c
                            TR(PS[bk][:, c * 128:(c + 1) * 128], xTf[:, kc, j * 128:(j + 1) * 128], r=["cst"] + XFK(kc), w=PK(bk))
                        CP("act" if bk % 2 == 0 else "dve", xb[:, q4 * 512:(q4 + 1) * 512], PS[bk][:, :], r=PK(bk), w=xbk)
                    S.dma("sp", out_d[t * 1024 + j * 128:t * 1024 + (j + 1) * 128, :], xb, reads=xbk, writes=["out"])

        def phase2():
            ring["nstg"], ring["nwb"] = 2, 4
            h2_o = [wb_off[4], None]
            apos[0] = main_o
            h2_o[1] = alloc(16384)
            raw_o = [alloc(4 * T), alloc(4 * T), alloc(4 * T), alloc(4 * T)]
            cb_o = alloc(4 * T)
            Kt_o, Vt_o = alloc(4 * T), alloc(4 * T)
            AT_o, wT_o, u_o = alloc(4 * T), alloc(4 * T), alloc(4 * T)
            o_o = raw_o[2]
            G = 2
            un_o = [dict(B=alloc(512), dec=alloc(512), decS=alloc(512), decT=alloc(512), N=alloc(512), A=alloc(512),
                         XY=[alloc(512) for _ in range(4)], R=[alloc(1024), alloc(1024)]) for _ in range(G)]
            S_o = [alloc(512), alloc(512)]
            vn_o = [alloc(512), alloc(512)]
            kd_o = [alloc(512), alloc(512)]
            ot_o = [alloc(512), alloc(512)]
            on_o = [alloc(512), alloc(512)]
            yh_o = [alloc(2 * T)] * 2
            ab_o = alloc(NTL * 16 * 4)
            sm_o = {k: alloc(NTL * 8 * 4) for k in ("gcol", "beta", "nbeta", "Gc", "gl", "eG", "edec", "egl", "bEG", "t1", "t2")}
            wab_o = alloc(16 * 16 * 2)
            ss_o = alloc(64 * 4)
            km_o = alloc(64)
            gate_o = alloc(NTL * 8 * 4)
            gtmp_o = [alloc(64) for _ in range(4)]
            seln_o = alloc(NTL * 8 * 4)
            pt_o = [alloc(256) for _ in range(4)]
            rl_o = [alloc(512), alloc(512)]
            qb_o, kb_o, vtk_o = AT_o, wT_o, u_o
            if T >= 2048:
                selT_o = Vt_o + 2048
                rt_o = [Vt_o, Vt_o + 1024]
            else:
                selT_o = alloc(2 * T)
                rt_o = [alloc(2048), alloc(2048)]

            h2t = [vbf(o, 16 * 512).rearrange("p (k t) -> p k t", k=16) for o in h2_o]
            raw = [vf32(o, T) for o in raw_o]
            cbuf = vf32(cb_o, T)
            Kt = vf32(Kt_o, NTL * 128).rearrange("p (n d) -> p n d", n=NTL)
            Vt = vf32(Vt_o, NTL * 128).rearrange("p (n d) -> p n d", n=NTL)
            ATa = vf32(AT_o, NTL * 128).rearrange("p (n d) -> p n d", n=NTL)
            wTa = vf32(wT_o, NTL * 128).rearrange("p (n d) -> p n d", n=NTL)
            ua = vf32(u_o, NTL * 128).rearrange("p (n d) -> p n d", n=NTL)
            oa = vf32(o_o, NTL * 128).rearrange("p (n d) -> p n d", n=NTL)
            sm = {k: vf32(o, NTL * 8).rearrange("p (n h) -> p n h", n=NTL) for k, o in sm_o.items()}
            ab = vf32(ab_o, NTL * 16).rearrange("p (n c) -> p n c", n=NTL)
            wab = vbf(wab_o, 256).rearrange("p (k c) -> p k c", k=16)
            ssv = vf32(ss_o, 64)

            def h2load(tt, slot):
                S.dma("sp", h2t[slot], H2v[:, :, tt * 512:(tt + 1) * 512], reads=["H2s"], writes=["h2t%d" % slot])

            sv = vf32(stg_off[0], 256).rearrange("p (k c) -> p k c", k=16)
            S.dma("sp", sv, win_d[:, 4096:4112].rearrange("(k p) c -> p k c", p=128), writes=["stg0"])
            CP("dve", wab, sv, r=["stg0"], w=["wab"])
            for tt in range(NT):
                h2load(tt, tt % 2)
                for sub in range(4):
                    tl = tt * 4 + sub
                    for kc in range(16):
                        MM(PS[0][:, tl * 16:(tl + 1) * 16], h2t[tt % 2][:, kc, sub * 128:(sub + 1) * 128], wab[:, kc, :],
                           start=(kc == 0), stop=(kc == 15), r=["h2t%d" % (tt % 2), "wab"], w=PK(0, 0, 256))
            CP("dve", ab.rearrange("p n c -> p (n c)"), PS[0][:, 0:NTL * 16], r=PK(0, 0, 256), w=["ab"])
            flat = lambda k: sm[k].rearrange("p n h -> p (n h)")
            alog = vec[:, V_ALOG:V_ALOG + 128].rearrange("p (n h) -> p n h", n=16)[:, 0:NTL, :]
            dtb = vec[:, V_DTB:V_DTB + 128].rearrange("p (n h) -> p n h", n=16)[:, 0:NTL, :]
            TT("dve", sm["t1"], ab[:, :, 0:8], dtb, ALU.add, r=["ab", "vec"], w=["t1"])
            ACT(flat("t1"), flat("t1"), AF.Exp, r=["t1"], w=["t1"])
            ACT(flat("t1"), flat("t1"), AF.Ln, r=["t1"], w=["t1"], bias=1.0)
            ACT(sm["t2"], alog, AF.Exp, r=["vec"], w=["t2"])
            STT("dve", flat("gcol"), flat("t1"), -1.0, flat("t2"), ALU.mult, ALU.mult, r=["t1", "t2"], w=["gcol"])
            ACT(sm["beta"], ab[:, :, 8:16], AF.Sigmoid, r=["ab"], w=["beta"])
            TS("dve", flat("nbeta"), flat("beta"), -1.0, None, ALU.mult, r=["beta"], w=["nbeta"])
            for tl in range(NTL):
                MM(PS[1][:, tl * 8:(tl + 1) * 8], triu_f, sm["gcol"][:, tl, :], r=["cst", "gcol"], w=PK(1, 0, 128))
                MM(PS[2][:, tl * 8:(tl + 1) * 8], ones_f, sm["gcol"][:, tl, :], r=["cst", "gcol"], w=PK(2, 0, 128))
            CP("dve", flat("Gc"), PS[1][:, 0:NTL * 8], r=PK(1, 0, 128), w=["Gc"])
            CP("dve", flat("gl"), PS[2][:, 0:NTL * 8], r=PK(2, 0, 128), w=["gl"])
            ACT(flat("eG"), flat("Gc"), AF.Exp, r=["Gc"], w=["eG"])
            ACT(flat("egl"), flat("gl"), AF.Exp, r=["gl"], w=["egl"])
            TT("dve", flat("t1"), flat("gl"), flat("Gc"), ALU.subtract, r=["gl", "Gc", "t1"], w=["t1"])
            ACT(flat("edec"), flat("t1"), AF.Exp, r=["t1"], w=["edec"])
            TT("dve", flat("bEG"), flat("beta"), flat("eG"), ALU.mult, r=["beta", "eG"], w=["bEG"])

            if lvl < 1:
                ring["nstg"], ring["nwb"] = NSTG, NWB
                return

            def inproj(cols, nchunk):
                blks = [wblock(win_d[:, c0:c0 + 128], 16, 128, engs=("act", "pool")) for c0 in cols]
                for tt in range(NT):
                    h2load(tt, tt % 2)
                    for ci in range(nchunk):
                        wv, wk = blks[ci]
                        for kc in range(16):
                            MM(PS[ci][:, :], wv[:, kc, :], h2t[tt % 2][:, kc, :], start=(kc == 0), stop=(kc == 15),
                               r=[wk, "h2t%d" % (tt % 2)], w=PK(ci))
                        CP("act" if ci % 2 else "dve", raw[ci][:, tt * 512:(tt + 1) * 512], PS[ci][:, :],
                           r=PK(ci), w=["raw%d" % ci])

            def to_tokmajor(src, sk, dstflat, dk_):
                for g4 in range(NTL // 4):
                    b = 6 + g4 % 2
                    for c in range(4):
                        tl = g4 * 4 + c
                        TR(PS[b][:, c * 128:(c + 1) * 128], src[:, tl * 128:(tl + 1) * 128], r=["cst", sk],
                           w=PK(b, c * 128, (c + 1) * 128))
                    CP("act" if g4 % 2 else "dve", dstflat(g4), PS[b][:, :], r=PK(b), w=[dk_])

            def gdn_head(h):
                inproj([h * 128, 1024 + h * 128, 2048 + h * 128, 3072 + h * 128], 4)
                for ci in range(3):
                    eng = "dve"
                    cw0 = V_CONV + (ci * 8 + h) * 4
                    x = raw[ci]
                    rk = "raw%d" % ci
                    TS(eng, cbuf[:, :], x[:, :], vec[:, cw0 + 3:cw0 + 4], None, ALU.mult, r=[rk, "vec"], w=["cbuf"])
                    for sft in (1, 2, 3):
                        STT(eng, cbuf[:, sft:T], x[:, 0:T - sft], vec[:, cw0 + 3 - sft:cw0 + 4 - sft], cbuf[:, sft:T],
                            ALU.mult, ALU.add, r=[rk, "vec", "cbuf"], w=["cbuf"])
                    ACT(x[:, :], cbuf[:, :], AF.Silu, r=["cbuf"], w=[rk])
                    if ci < 2:
                        ACT(cbuf[:, :], x[:, :], AF.Square, r=[rk], w=["cbuf"])
                        for tt in range(NT):
                            MM(PS[tt][:, :], ones_f, cbuf[:, tt * 512:(tt + 1) * 512], r=["cst", "cbuf"], w=PK(tt))
                        for tt in range(NT):
                            ACT(cbuf[:, tt * 512:(tt + 1) * 512], PS[tt][:, :], AF.Ln, r=PK(tt), w=["cbuf"], bias=EPS)
                        ACT(cbuf[:, :], cbuf[:, :], AF.Exp, r=["cbuf"], w=["cbuf"], scale=-0.5)
                        if ci == 0:
                            STT("dve", x[:, :], x[:, :], 128.0 ** -0.5, cbuf[:, :], ALU.mult, ALU.mult, r=[rk, "cbuf"], w=[rk])
                        else:
                            TT("dve", x[:, :], x[:, :], cbuf[:, :], ALU.mult, r=[rk, "cbuf"], w=[rk])
                ACT(raw[3][:, :], raw[3][:, :], AF.Silu, r=["raw3"], w=["raw3"])
                TS("pool", raw[3][:, :], raw[3][:, :], vec[:, V_NG:V_NG + 1], None, ALU.mult, r=["raw3", "vec"], w=["raw3"])
                qT, kT, vT, gz = raw
                if lvl < 2:
                    return
                to_tokmajor(kT, "raw1", lambda g4: Kt[:, g4 * 4:(g4 + 1) * 4, :].rearrange("p n d -> p (n d)"), "Kt")
                to_tokmajor(vT, "raw2", lambda g4: Vt[:, g4 * 4:(g4 + 1) * 4, :].rearrange("p n d -> p (n d)"), "Vt")
                if lvl < 2.5:
                    return
                for g0 in range(0, NTL, G):
                    units = list(range(g0, min(NTL, g0 + G)))
                    U = {}
                    for tl in units:
                        u = tl % G
                        o = un_o[u]
                        U[tl] = dict(
                            B=vf32(o["B"], 128), dec=vf32(o["dec"], 128), decS=vf32(o["decS"], 128), decT=vf32(o["decT"], 128),
                            N=vf32(o["N"], 128), A=vf32(o["A"], 128), XY=[vf32(q, 128) for q in o["XY"]],
                            R=[vf32(q, 256) for q in o["R"]], pa=PS[2 * u][:, 0:128], pb=PS[2 * u][:, 128:256], pc=PS[2 * u + 1][:, 0:256],
                            pak=PK(2 * u, 0, 128), pbk=PK(2 * u, 128, 256), pck=PK(2 * u + 1, 0, 256),
                            k=(lambda s_, u=u: "u%d%s" % (u, s_)), tsl=slice(tl * 128, (tl + 1) * 128))
                    for tl in units:
                        d = U[tl]; k = d["k"]
                        TS("pool", d["B"], sgtj_f, sm["gcol"][:, tl, h:h + 1], None, ALU.mult, r=["cst", "gcol"], w=[k("B")])
                        MM(d["pa"], triu_f, d["B"], r=["cst", k("B")], w=d["pak"])
                        ACT(d["dec"], d["pa"], AF.Exp, r=d["pak"], w=[k("dec")])
                        if lvl < 2.6:
                            continue
                        MM(d["pa"], kT[:, d["tsl"]], kT[:, d["tsl"]], r=["raw1"], w=d["pak"])
                        MM(d["pb"], qT[:, d["tsl"]], kT[:, d["tsl"]], r=["raw0", "raw1"], w=d["pbk"])
                        TT("pool", d["decS"], d["dec"], strict_f, ALU.mult, r=[k("dec"), "cst"], w=[k("decS")])
                        TT("pool", d["decT"], d["dec"], tril_f, ALU.mult, r=[k("dec"), "cst"], w=[k("decT")])
                        STT("dve", d["N"], d["pa"], sm["nbeta"][:, tl, h:h + 1], d["decS"], ALU.mult, ALU.mult,
                            r=d["pak"] + ["nbeta", k("decS")], w=[k("N")])
                        TT("dve", d["A"], d["pb"], d["decT"], ALU.mult, r=d["pbk"] + [k("decT")], w=[k("A")])
                        if lvl < 2.7:
                            continue
                        TR(d["pa"], d["N"], r=["cst", k("N")], w=d["pak"])
                        TR(d["pb"], d["A"], r=["cst", k("A")], w=d["pbk"])
                        CP("act", d["XY"][1], d["pa"], r=d["pak"], w=[k("XY1")])
                        CP("act", ATa[:, tl, :], d["pb"], r=d["pbk"], w=["ATa"])
                        if lvl < 2.78:
                            continue
                        TS("pool", d["R"][0][:, 0:128], Vt[:, tl, :], sm["beta"][:, tl, h:h + 1], None, ALU.mult,
                           r=["Vt", "beta"], w=[k("R0")])
                        TS("pool", d["R"][0][:, 128:256], Kt[:, tl, :], sm["bEG"][:, tl, h:h + 1], None, ALU.mult,
                           r=["Kt", "bEG", k("R0")], w=[k("R0")])
                    if lvl < 2.8:
                        continue
                    nlv = 7 if lvl >= 3 else int(round((lvl - 2.8) * 100))
                    for lv in range(nlv):
                        for tl in units:
                            d = U[tl]; k = d["k"]
                            if lv == 0:
                                X, Xk, Y, Yk = d["N"], k("N"), d["XY"][1], k("XY1")
                            else:
                                xi, yi = (0, 1) if lv % 2 == 0 else (2, 3)
                                X, Xk, Y, Yk = d["XY"][xi], k("XY%d" % xi), d["XY"][yi], k("XY%d" % yi)
                            Rc, Rck = d["R"][lv % 2], k("R%d" % (lv % 2))
                            cv = "rs"
                            if "r" in cv:
                                MM(d["pc"], Y, Rc, r=[Yk, Rck], w=d["pck"])
                            if lv < 6 or nlv < 7:
                                Rn, Rnk = d["R"][(lv + 1) % 2], k("R%d" % ((lv + 1) % 2))
                                if "r" in cv:
                                    TT("dve", Rn, Rc, d["pc"], ALU.add, r=[Rck] + d["pck"], w=[Rnk])
                                nxi, nyi = (0, 1) if (lv + 1) % 2 == 0 else (2, 3)
                                if "s" in cv:
                                    MM(d["pa"], Y, X, r=[Yk, Xk], w=d["pak"])
                                    MM(d["pb"], X, Y, r=[Yk, Xk], w=d["pbk"])
                                    CP("dve", d["XY"][nxi], d["pa"], r=d["pak"], w=[k("XY%d" % nxi)])
                                    CP("dve", d["XY"][nyi], d["pb"], r=d["pbk"], w=[k("XY%d" % nyi)])
                            else:
                                TT("dve", ua[:, tl, :], Rc[:, 0:128], d["pc"][:, 0:128], ALU.add, r=[Rck] + d["pck"], w=["ua"])
                                TT("dve", d["A"], Rc[:, 128:256], d["pc"][:, 128:256], ALU.add, r=[Rck] + d["pck"], w=[k("A")])
                                TR(d["pa"], d["A"], r=["cst", k("A")], w=d["pak"])
                                CP("act", wTa[:, tl, :], d["pa"], r=d["pak"], w=["wTa"])
                if lvl < 4:
                    return
                Sst = [vf32(o, 128) for o in S_o]
                MEMSET("pool", Sst[0], 0.0, w=["S0"])
                for tl in range(NTL):
                    cur, nxt = tl % 2, (tl + 1) % 2
                    b = 4 + tl % 2
                    pb_ = PS[b]
                    tsl = slice(tl * 128, (tl + 1) * 128)
                    vn, kd, ot = vf32(vn_o[tl % 2], 128), vf32(kd_o[tl % 2], 128), vf32(ot_o[tl % 2], 128)
                    vk, kk, ok_ = "vn%d" % (tl % 2), "kd%d" % (tl % 2), "ot%d" % (tl % 2)
                    MM(pb_[:, 0:128], wTa[:, tl, :], Sst[cur], r=["wTa", "S%d" % cur], w=PK(b, 0, 128))
                    TT("dve", vn, ua[:, tl, :], pb_[:, 0:128], ALU.subtract, r=["ua"] + PK(b, 0, 128), w=[vk])
                    MM(pb_[:, 128:256], qT[:, tsl], Sst[cur], r=["raw0", "S%d" % cur], w=PK(b, 128, 256))
                    MM(pb_[:, 256:384], ATa[:, tl, :], vn, r=["ATa", vk], w=PK(b, 256, 384))
                    TS("dve", ot, pb_[:, 128:256], sm["eG"][:, tl, h:h + 1], None, ALU.mult, r=PK(b, 128, 256) + ["eG"], w=[ok_])
                    TT("dve", oa[:, tl, :], ot, pb_[:, 256:384], ALU.add, r=[ok_] + PK(b, 256, 384), w=["raw2"])
                    TS("pool", kd, Kt[:, tl, :], sm["edec"][:, tl, h:h + 1], None, ALU.mult, r=["Kt", "edec"], w=[kk])
                    MM(pb_[:, 384:512], kd, vn, r=[kk, vk], w=PK(b, 384, 512))
                    STT("dve", Sst[nxt], Sst[cur], sm["egl"][:, tl, h:h + 1], pb_[:, 384:512], ALU.mult, ALU.add,
                        r=["S%d" % cur, "egl"] + PK(b, 384, 512), w=["S%d" % nxt])
                if lvl < 5:
                    return
                yh = vbf(yh_o[h % 2], T)
                yk = "yh"
                ACT(cbuf[:, :], oa.rearrange("p n d -> p (n d)"), AF.Square, r=["raw2"], w=["cbuf"])
                S.add("dve", lambda e: e.tensor_reduce(ssv[:, 0:NTL], cbuf[:, :].rearrange("p (n d) -> p n d", n=NTL), AX.X, ALU.add),
                      ["cbuf"], ["ssv"])
                ACT(ssv[:, 0:NTL], ssv[:, 0:NTL], AF.Ln, r=["ssv"], w=["ssv"], scale=1.0 / 128, bias=EPS)
                ACT(ssv[:, 0:NTL], ssv[:, 0:NTL], AF.Exp, r=["ssv"], w=["ssv"], scale=-0.5)
                for tl in range(NTL):
                    on = vf32(on_o[tl % 2], 128)
                    onk = "on%d" % (tl % 2)
                    b = 6 + tl % 2
                    TS("pool", on, oa[:, tl, :], ssv[:, tl:tl + 1], None, ALU.mult, r=["raw2", "ssv"], w=[onk])
                    TR(PS[b][:, 0:128], on, r=["cst", onk], w=PK(b, 0, 128))
                    TT("dve", yh[:, tl * 128:(tl + 1) * 128], PS[b][:, 0:128], gz[:, tl * 128:(tl + 1) * 128], ALU.mult,
                       r=PK(b, 0, 128) + ["raw3"], w=[yk])
                S.dma("sp", Y_d[h * 128:(h + 1) * 128, :], yh, reads=[yk], writes=["Ys"])

            def moba_head(m):
                inproj([4112 + m * 128, 5136 + m * 128, 6160 + m * 128], 3)
                rq, rk_, rv = raw[0], raw[1], raw[2]
                cosv = cbuf[0:32, 0:T]
                sinv = vf32(Kt_o, T)[0:32, :]
                pmf = cst[0:32, C_PM:C_PM + 32]
                for ci in range(2):
                    x = raw[ci]
                    xk = "raw%d" % ci
                    for tt in range(NT):
                        cs = slice(tt * 512, (tt + 1) * 512)
                        b = 4 + tt % 2
                        t1 = vf32(rt_o[0], 512)
                        t2 = vf32(rt_o[1], 512)
                        MM(PS[b][0:32, :], pmf, x[0:32, cs], r=["cst", xk], w=PK(b))
                        TT("dve", t1[0:32, :], x[0:32, cs], cosv[:, cs], ALU.mult, r=[xk, "cbuf"], w=["Vt"])
                        TT("dve", t2[0:32, :], PS[b][0:32, :], sinv[:, cs], ALU.mult, r=PK(b) + ["Kt", "Vt"], w=["Vt"])
                        TT("pool", x[0:32, cs], t1[0:32, :], t2[0:32, :], ALU.add, r=["Vt", xk], w=[xk])
                km = vf32(km_o, 8)
                S.add("dve", lambda e: e.tensor_reduce(km[:, 0:NB], rk_[:, :].rearrange("p (n t) -> p n t", n=NB), AX.X, ALU.add),
                      ["raw1"], ["km"])
                TS("dve", km[:, 0:NB], km[:, 0:NB], 1.0 / 256, None, ALU.mult, r=["km"], w=["km"])
                gate = vf32(gate_o, NTL * 8).rearrange("p (n b) -> p n b", n=NTL)
                seln = vf32(seln_o, NTL * 8).rearrange("p (n b) -> p n b", n=NTL)
                for tl in range(NTL):
                    MM(PS[5][:, tl * 8:tl * 8 + NB], rq[:, tl * 128:(tl + 1) * 128], km[:, 0:NB], r=["raw0", "km"], w=PK(5, 0, 128))
                CP("dve", gate[:, :, 0:NB], PS[5][:, 0:NTL * 8].rearrange("p (n b) -> p n b", b=8)[:, :, 0:NB], r=PK(5, 0, 128), w=["gate"])
                MEMSET("pool", seln.rearrange("p n b -> p (n b)"), 0.0, w=["seln"])
                gt = [vf32(o, 8) for o in gtmp_o]
                for tl in range(NTL):
                    own = tl // 2
                    if own <= 3:
                        continue
                    g = gate[:, tl, 0:own]
                    cur = g
                    ck = "gate"
                    for it in range(3):
                        S.add("dve", lambda e, cur=cur, it=it: e.tensor_reduce(gt[3][:, it:it + 1], cur, AX.X, ALU.max),
                              [ck, "gt3"], ["gt3"])
                        if it < 2:
                            TS("dve", gt[2][:, 0:own], cur, gt[3][:, it:it + 1], -1e30, ALU.is_ge, ALU.mult, r=[ck, "gt3"], w=["gt2"])
                            TT("dve", gt[it][:, 0:own], cur, gt[2][:, 0:own], ALU.add, r=[ck, "gt2"], w=["gt%d" % it])
                            cur = gt[it][:, 0:own]
                            ck = "gt%d" % it
                    TS("dve", seln[:, tl, 0:own], g, gt[3][:, 2:3], NEG, ALU.is_lt, ALU.mult, r=["gate", "gt3", "seln"], w=["seln"])
                selT = vbf(selT_o, T)
                for g4 in range(NTL // 4):
                    b = 6 + g4 % 2
                    for c in range(4):
                        tl = g4 * 4 + c
                        TR(PS[b][0:8, c * 128:(c + 1) * 128], seln[:, tl, :], r=["cst", "seln"], w=PK(b, c * 128, (c + 1) * 128))
                    CP("dve", selT[0:8, g4 * 512:(g4 + 1) * 512], PS[b][0:8, :], r=PK(b), w=["Vt"])
                qb, kb = vbf(qb_o, T), vbf(kb_o, T)
                CP("act", qb, rq[:, :], r=["raw0"], w=["ATa"])
                CP("pool", kb, rk_[:, :], r=["raw1"], w=["wTa"])
                vtk = vbf(vtk_o, NTL * 128).rearrange("p (n d) -> p n d", n=NTL)
                to_tokmajor(rv, "raw2", lambda g4: vtk[:, g4 * 4:(g4 + 1) * 4, :].rearrange("p n d -> p (n d)"), "ua")
                yh = vbf(yh_o[m % 2], T)
                yk = "yh"
                eb = cstb[0:8, C_EBLK:C_EBLK + 1024].rearrange("p (n k) -> p n k", n=8)
                items = []
                for i in range(NTL):
                    own = i // 2
                    keys = [(kt, "sel") for kt in range(2 * own)]
                    if i % 2:
                        keys.append((i - 1, "full"))
                    keys.append((i, "diag"))
                    for n_, (kt, kind) in enumerate(keys):
                        items.append((i, kt, kind, n_ == 0, n_ == len(keys) - 1))
                LA = 2
                ppb = (0, 1, 4)

                def stage12(idx):
                    i, kt, kind, first, last = items[idx]
                    qs = slice(i * 128, (i + 1) * 128)
                    bk = ppb[idx % 3]
                    pp = PS[bk][:, 0:128]
                    ptb, ptk = vbf(pt_o[idx % 4], 128), "pt%d" % (idx % 4)
                    MM(pp, kb[:, kt * 128:(kt + 1) * 128], qb[:, qs], start=True, stop=(kind != "sel"),
                       r=["wTa", "ATa"], w=PK(bk))
                    if kind == "sel":
                        MM(pp, eb[:, kt // 2, :], selT[0:8, qs], start=False, stop=True, r=["cstb", "Vt"], w=PK(bk))
                    ACT(ptb, pp, AF.Exp, r=PK(bk), w=[ptk], scale=128.0 ** -0.5)
                    if kind == "diag":
                        TT("pool", ptb, ptb, triu_b, ALU.mult, r=[ptk, "cstb"], w=[ptk])

                def stage3(idx):
                    i, kt, kind, first, last = items[idx]
                    qs = slice(i * 128, (i + 1) * 128)
                    ptb, ptk = vbf(pt_o[idx % 4], 128), "pt%d" % (idx % 4)
                    bo, bl = (2, 3) if i % 2 == 0 else (6, 7)
                    po, pl = PS[bo][:, 0:128], PS[bl][:, 0:128]
                    MM(po, vtk[:, kt, :], ptb, start=first, stop=last, r=["ua", ptk], w=PK(bo))
                    MM(pl, ones_b, ptb, start=first, stop=last, r=["cstb", ptk], w=PK(bl))
                    if last:
                        rl = vf32(rl_o[i % 2], 128)
                        rlk = "rl%d" % (i % 2)
                        S.add("dve", lambda e, rl=rl, pl=pl: e.reciprocal(rl, pl), PK(bl), [rlk])
                        TT("dve", yh[:, qs], po, rl, ALU.mult, r=PK(bo) + [rlk], w=[yk])

                for idx in range(len(items) + LA):
                    if idx < len(items):
                        stage12(idx)
                    if idx - LA >= 0:
                        stage3(idx - LA)
                S.dma("sp", Y_d[1024 + m * 128:1024 + (m + 1) * 128, :], yh, reads=[yk], writes=["Ys"])

            for h in range(ngh):
                if gdn_on:
                    gdn_head(h)
            if moba_on:
                S.dma("sp", cbuf[0:32, 0:T], rope_d[:, 0:T], writes=["cbuf"])
                S.dma("sp", vf32(Kt_o, T)[0:32, :], rope_d[:, T:2 * T], writes=["Kt"])
                for m in range(nmh):
                    moba_head(m)
            ring["nstg"], ring["nwb"] = NSTG, NWB

        if 0 in phases:
            phase0()
        if 1 in phases:
            phase1()
        if 2 in phases:
            S.barrier()
            phase2()
            S.barrier()
        if 3 in phases:
            phase3()
        S.finalize(st)
        S.run_block()
    return nc


def host_consts(T):
    c = np.zeros((128, NCONST), np.float32)
    i = np.arange(128)
    c[:, C_ID:C_ID + 128] = np.eye(128)
    c[:, C_ONE:C_ONE + 128] = 1.0
    c[:, C_TRIU:C_TRIU + 128] = (i[:, None] <= i[None, :])
    c[:, C_SGTJ:C_SGTJ + 128] = (i[:, None] > i[None, :])
    c[:, C_TRIL:C_TRIL + 128] = (i[:, None] >= i[None, :])
    c[:, C_STRICT:C_STRICT + 128] = (i[:, None] > i[None, :])
    for m in range(16):
        c[m + 16, C_PM + m] = -1.0
        c[m, C_PM + m + 16] = 1.0
    for n in range(8):
        c[n, C_EBLK + n * 128:C_EBLK + (n + 1) * 128] = 1.0
    inv = 500000.0 ** (-np.arange(0, 32, 2, dtype=np.float32) / 32)
    ang = np.arange(T, dtype=np.float32)[None, :] * inv[:, None].astype(np.float32)
    rope = np.zeros((32, 2 * T), np.float32)
    rope[0:16, 0:T] = np.cos(ang)
    rope[16:32, 0:T] = np.cos(ang)
    rope[0:16, T:] = np.sin(ang)
    rope[16:32, T:] = np.sin(ang)
    return c, rope


def col16(v):
    return np.ascontiguousarray(np.asarray(v, np.float32).reshape(-1, 128).T)


def host_vecs(inp, b):
    v = np.zeros((128, NVEC), np.float32)
    v[:, V_C:V_C + 16] = col16(inp["c"][b])
    v[:, V_BADA:V_BADA + 144] = col16(inp["b_ada"][0])
    for i, nm in enumerate(("ffn1_pre_g", "ffn1_post_g", "mix_pre_g", "mix_post_g", "ffn2_pre_g", "ffn2_post_g")):
        v[:, V_G + i * 16:V_G + (i + 1) * 16] = col16(inp[nm][0])
    cw = np.asarray(inp["gdn_conv_w"][0], np.float32)
    v[:, V_CONV:V_CONV + 96] = cw.T.reshape(24, 128, 4).transpose(1, 0, 2).reshape(128, 96)
    v[:, V_ALOG:V_ALOG + 128] = np.tile(np.asarray(inp["gdn_a_log"][0], np.float32), 16)[None, :]
    v[:, V_DTB:V_DTB + 128] = np.tile(np.asarray(inp["gdn_dt_bias"][0], np.float32), 16)[None, :]
    v[:, V_NG] = np.asarray(inp["gdn_norm_g"][0], np.float32)
    return v


_NC_CACHE = {}


def kernel(**inp):
    inp = {k: np.asarray(v) for k, v in inp.items()}
    B, T, _ = inp["x"].shape
    if T not in _NC_CACHE:
        _NC_CACHE[T] = build_nc(T)
    nc = _NC_CACHE[T]
    consts, rope = host_consts(T)
    shared = dict(consts=consts, rope=rope, w_ada=np.ascontiguousarray(inp["w_ada"][0]),
                  w1g=np.ascontiguousarray(inp["ffn1_w_gate"][0]), w1u=np.ascontiguousarray(inp["ffn1_w_up"][0]),
                  w1d=np.ascontiguousarray(inp["ffn1_w_down"][0]), w2g=np.ascontiguousarray(inp["ffn2_w_gate"][0]),
                  w2u=np.ascontiguousarray(inp["ffn2_w_up"][0]), w2d=np.ascontiguousarray(inp["ffn2_w_down"][0]),
                  w_in=np.ascontiguousarray(inp["w_in"][0]), w_out=np.ascontiguousarray(inp["w_out"][0]))
    in_maps = []
    for core in range(8):
        b = core // 2
        m = dict(shared)
        m["x"] = np.ascontiguousarray(inp["x"][b])
        m["vecs"] = host_vecs(inp, b)
        in_maps.append(m)
    res = run_bass_kernel_spmd(nc, in_maps, core_ids=list(range(8)))
    out = np.stack([np.asarray(res.results[2 * b]["out"], np.float32) for b in range(B)], axis=0)
    return out
```

```python
import contextlib
import numpy as np
import concourse.bass as bass
import concourse.mybir as mybir
from concourse.bass_utils import run_bass_kernel_spmd

F32 = mybir.dt.float32
BF16 = mybir.dt.bfloat16
AF = mybir.ActivationFunctionType
ALU = mybir.AluOpType
AX = mybir.AxisListType

ENGS = ("pe", "act", "dve", "pool", "sp")

D = 2048
KC = 16
FF = 5632
FC = 44
NMOD = 9
INC = 7184
EPS = 1e-6
NEG = -30000.0


class Sched:
    def __init__(self, nc, n_dma_sems=48):
        self.nc = nc
        self.ops = []
        self.last_w = {}
        self.readers = {}
        self.n_dma_sems = n_dma_sems

    def add(self, eng, emit, reads=(), writes=(), dma=False):
        idx = len(self.ops)
        deps = set()
        for b in reads:
            if b in self.last_w:
                deps.add(self.last_w[b])
        for b in writes:
            if b in self.last_w:
                deps.add(self.last_w[b])
            rd = self.readers.get(b)
            if rd:
                deps.update(rd.values())
        deps.discard(idx)
        self.ops.append(dict(eng=eng, emit=emit, deps=deps, dma=dma, barrier=False))
        for b in reads:
            self.readers.setdefault(b, {})[("dma", idx) if dma else eng] = idx
        for b in writes:
            self.last_w[b] = idx
            self.readers[b] = {}
        return idx

    def barrier(self):
        self.ops.append(dict(eng=None, emit=None, deps=set(), dma=False, barrier=True))
        self.last_w = {}
        self.readers = {}

    def dma(self, q, out, in_, reads=(), writes=()):
        return self.add(q, lambda e: e.dma_start(out=out, in_=in_), reads, writes, dma=True)

    def finalize(self, stack):
        nc = self.nc
        ops = self.ops
        dma_count = 0
        for op in ops:
            op["signal"] = False
            if op["dma"]:
                op["dma_ord"] = dma_count
                dma_count += 1
        last_c = {}
        for i, op in enumerate(ops):
            if op["barrier"]:
                for e, j in last_c.items():
                    ops[j]["signal"] = True
                op["last_c"] = dict(last_c)
                continue
            for d in op["deps"]:
                od = ops[d]
                if od["dma"]:
                    continue
                if not (od["eng"] == "pe" and op["eng"] == "pe"):
                    od["signal"] = True
            if not op["dma"]:
                last_c[op["eng"]] = i
        sigcount = {e: 0 for e in ENGS}
        for op in ops:
            if op["barrier"] or op["dma"]:
                continue
            if op["signal"]:
                sigcount[op["eng"]] += 1
                op["sigval"] = sigcount[op["eng"]]
        esem = {e: stack.enter_context(nc.semaphore("s_" + e)) for e in ENGS}
        nd = min(self.n_dma_sems, max(1, dma_count))
        dsem = [stack.enter_context(nc.semaphore("d%d" % i)) for i in range(nd)]
        for op in ops:
            if op["dma"]:
                k = op["dma_ord"]
                op["dsem"] = dsem[k % nd]
                op["dval"] = 16 * (k // nd + 1)
                op["dslot"] = k % nd
        waited = {e: {} for e in ENGS}
        per_eng = {e: [] for e in ENGS}
        last_on_slot = {}
        pending = {e: [] for e in ENGS}
        for i, op in enumerate(ops):
            if op["barrier"]:
                for e in ENGS:
                    lst = []
                    for e2, j in op["last_c"].items():
                        lst.append((("e", e2), esem[e2], ops[j]["sigval"]))
                    for slot, o in last_on_slot.items():
                        lst.append((("d", slot), o["dsem"], o["dval"]))
                    pending[e] = lst
                continue
            e = op["eng"]
            waits = []

            def need(key, sem, val):
                if waited[e].get(key, 0) < val:
                    waited[e][key] = val
                    waits.append((sem, val))

            for key, sem, val in pending[e]:
                need(key, sem, val)
            pending[e] = []
            for d in sorted(op["deps"]):
                od = ops[d]
                if od["dma"]:
                    need(("d", od["dslot"]), od["dsem"], od["dval"])
                elif not (od["eng"] == "pe" and e == "pe"):
                    need(("e", od["eng"]), esem[od["eng"]], od["sigval"])
            if op["dma"]:
                prev = last_on_slot.get(op["dslot"])
                if prev is not None:
                    need(("d", op["dslot"]), prev["dsem"], prev["dval"])
                last_on_slot[op["dslot"]] = op
            op["waits"] = waits
            per_eng[e].append(op)
        self.esem = esem
        self.per_eng = per_eng
        self.final_dma_waits = [(o["dsem"], o["dval"]) for o in last_on_slot.values()]

    def emit_engine(self, e, eng, final=False):
        for op in self.per_eng[e]:
            for sem, val in op["waits"]:
                eng.wait_ge(sem, val)
            ins = op["emit"](eng)
            if op["dma"]:
                ins.then_inc(op["dsem"], 16)
            elif op["signal"]:
                ins.then_inc(self.esem[e], 1)
        if final:
            for sem, val in self.final_dma_waits:
                eng.wait_ge(sem, val)

    def run_block(self):
        nc = self.nc
        with nc.Block() as block:
            @block.tensor
            def _(eng):
                self.emit_engine("pe", eng)

            @block.scalar
            def _(eng):
                self.emit_engine("act", eng)

            @block.vector
            def _(eng):
                self.emit_engine("dve", eng)

            @block.gpsimd
            def _(eng):
                self.emit_engine("pool", eng)

            @block.sync
            def _(eng):
                self.emit_engine("sp", eng, final=True)


C_ID, C_ONE, C_TRIU, C_SGTJ, C_TRIL, C_STRICT, C_PM, C_EBLK = 0, 128, 256, 384, 512, 640, 768, 800
NCONST = 800 + 1024
V_C, V_BADA, V_G = 0, 16, 160
V_CONV = 256
V_ALOG = 352
V_DTB = 480
V_NG = 608
V_PAR = 609
NVEC = 611


def build_nc(T, phases=(0, 1, 2, 3), gdn_on=True, moba_on=True, lvl=99, ngh=8, nmh=8):
    NT = T // 512
    NTL = T // 128
    NB = T // 256
    nc = bass.Bass("TRN2", target_bir_lowering=False)
    dr = lambda name, shape, dt=F32, kind="ExternalInput": nc.dram_tensor(name, shape, dt, kind=kind).ap()
    x_d = dr("x", [T, D])
    consts_d = dr("consts", [128, NCONST])
    vecs_d = dr("vecs", [128, NVEC])
    rope_d = dr("rope", [32, 2 * T])
    wada_d = dr("w_ada", [D, NMOD * D])
    w1g_d, w1u_d, w1d_d = dr("w1g", [D, FF]), dr("w1u", [D, FF]), dr("w1d", [FF, D])
    w2g_d, w2u_d, w2d_d = dr("w2g", [D, FF]), dr("w2u", [D, FF]), dr("w2d", [FF, D])
    win_d = dr("w_in", [D, INC])
    wout_d = dr("w_out", [D, D])
    split = T >= 2048
    TO = 1024 if split else T
    out_d = dr("out", [TO, D], kind="ExternalOutput")
    X1_d = dr("X1s", [D, T], F32, "Internal")
    X0_d = dr("X0s", [D, T], F32, "Internal")
    YF_d = dr("YFs", [D, 1024], F32, "Internal")
    H2_d = dr("H2s", [D, T], BF16, "Internal")
    Y_d = dr("Ys", [D, T], BF16, "Internal")

    st = contextlib.ExitStack()
    with st:
        S = Sched(nc)
        arena = st.enter_context(nc.sbuf_tensor("arena", [128, 96 * 1024], BF16))
        cst = st.enter_context(nc.sbuf_tensor("cst", [128, NCONST], F32))
        cstb = st.enter_context(nc.sbuf_tensor("cstb", [128, NCONST], BF16))
        vec = st.enter_context(nc.sbuf_tensor("vec", [128, NVEC], F32))
        modv = st.enter_context(nc.sbuf_tensor("modv", [128, 144 + 9 * 16], F32))
        PS = [st.enter_context(nc.psum_tensor("ps%d" % i, [128, 512], F32)) for i in range(8)]

        apos = [0]

        def alloc(nbytes):
            o = apos[0]
            apos[0] += (nbytes + 63) // 64 * 32
            assert apos[0] <= 96 * 1024, apos[0]
            return o

        def vf32(off, n):
            return arena[:, off:off + 2 * n].bitcast(F32)

        def vbf(off, n):
            return arena[:, off:off + n]

        def MM(out, lhsT, rhs, start=True, stop=True, r=(), w=()):
            S.add("pe", lambda e: e.matmul(out, lhsT, rhs, start=start, stop=stop), r, w)

        def TR(out, in_, r=(), w=()):
            S.add("pe", lambda e: e.transpose(out, in_, cst[:, C_ID:C_ID + 128]), r, w)

        def ACT(out, in_, func, r=(), w=(), **kw):
            S.add("act", lambda e: e.activation(out, in_, func, **kw), r, w)

        def CP(eng, out, in_, r=(), w=()):
            if eng == "act":
                S.add("act", lambda e: e.copy(out, in_), r, w)
            else:
                S.add(eng, lambda e: e.tensor_copy(out, in_), r, w)

        def TS(eng, out, in0, s1, s2, op0, op1=None, r=(), w=()):
            if op1 is None:
                S.add(eng, lambda e: e.tensor_single_scalar(out, in0, s1, op0), r, w)
            else:
                S.add(eng, lambda e: e.tensor_scalar(out, in0, s1, s2, op0, op1), r, w)

        def STT(eng, out, in0, sc, in1, op0, op1, r=(), w=()):
            S.add(eng, lambda e: e.scalar_tensor_tensor(out, in0, sc, in1, op0, op1), r, w)

        def TT(eng, out, in0, in1, op, r=(), w=()):
            S.add(eng, lambda e: e.tensor_tensor(out, in0, in1, op), r, w)

        def MEMSET(eng, ap, val, r=(), w=()):
            S.add(eng, lambda e: e.memset(ap, val), r, w)

        def PK(b, lo=0, hi=512):
            return ["ps%d" % b]

        ident = cst[:, C_ID:C_ID + 128]
        ones_f = cst[:, C_ONE:C_ONE + 128]
        triu_f = cst[:, C_TRIU:C_TRIU + 128]
        sgtj_f = cst[:, C_SGTJ:C_SGTJ + 128]
        tril_f = cst[:, C_TRIL:C_TRIL + 128]
        strict_f = cst[:, C_STRICT:C_STRICT + 128]
        ones_b = cstb[:, C_ONE:C_ONE + 128]
        triu_b = cstb[:, C_TRIU:C_TRIU + 128]

        S.dma("sp", cst[:], consts_d, writes=["cst"])
        S.dma("sp", vec[:], vecs_d, writes=["vec"])
        CP("dve", cstb[:], cst[:], r=["cst"], w=["cstb"])

        NSTG, NWB = 2, 8
        stg_off = [alloc(8192), alloc(8192)]
        wb_off = [alloc(4096) for _ in range(8)]
        ring = dict(s=0, w=0, c=0, nstg=NSTG, nwb=NWB)
        cast_engs = ("act", "pool", "dve", "pool")

        def wblock(src, nk, ncols, engs=cast_engs):
            si = ring["s"] % ring["nstg"]
            wi = ring["w"] % ring["nwb"]
            ring["s"] += 1
            ring["w"] += 1
            sv = vf32(stg_off[si], nk * ncols).rearrange("p (k c) -> p k c", k=nk)
            wv = vbf(wb_off[wi], nk * ncols).rearrange("p (k c) -> p k c", k=nk)
            S.dma("sp", sv, src.rearrange("(k p) c -> p k c", p=128), writes=["stg%d" % si])
            eng = engs[ring["c"] % len(engs)]
            ring["c"] += 1
            CP(eng, wv, sv, r=["stg%d" % si], w=["wb%d" % wi])
            return wv, "wb%d" % wi

        main_o = apos[0]
        hT_o = alloc(32768)
        actT_o = alloc(FC * 2048)
        cx_o = [alloc(4096), alloc(4096)]
        cy_o = [alloc(4096), alloc(4096)]
        rstd_o = alloc(4096)
        sq_o = alloc(2048)
        scb_o = alloc(64)
        row_o = cy_o

        hT = vbf(hT_o, 16 * 1024).rearrange("p (k t) -> p k t", k=16)
        actT = vbf(actT_o, FC * 1024).rearrange("p (k t) -> p k t", k=FC)
        xTf = vf32(actT_o, 16 * 1024).rearrange("p (k t) -> p k t", k=16)
        xin = [vf32(actT_o + 32768, 2048), vf32(actT_o + 32768 + 4096, 2048)]
        cx = [vf32(o, 1024) for o in cx_o]
        cy = [vf32(o, 1024) for o in cy_o]
        rstdB = vf32(rstd_o, 1024)
        sqb = vbf(sq_o, 1024)

        def mcol(i):
            return modv[:, i * 16:(i + 1) * 16]
        Sv = [modv[:, 144 + i * 16:144 + (i + 1) * 16] for i in range(3)]
        coef = [modv[:, 144 + 48 + i * 16:144 + 48 + (i + 1) * 16] for i in range(3)]
        tmpv = modv[:, 144 + 96:144 + 112]

        def gain(i):
            return vec[:, V_G + i * 16:V_G + (i + 1) * 16]

        def phase0():
            scb = vbf(scb_o, 16)
            ACT(scb, vec[:, V_C:V_C + 16], AF.Silu, r=["vec"], w=["scb"])
            pm = PS[7]
            for cb in range(72):
                blks = [wblock(wada_d[kh * 1024:(kh + 1) * 1024, cb * 256:(cb + 1) * 256], 8, 256) for kh in range(2)]
                pr = PS[cb % 2]
                prk = PK(cb % 2, 0, 256)
                for kc in range(16):
                    wv, wk = blks[kc // 8]
                    MM(pr[0:1, 0:256], scb[:, kc:kc + 1], wv[:, kc % 8, :], start=(kc == 0), stop=(kc == 15),
                       r=["scb", wk], w=prk)
                rw = vf32(row_o[cb % 2], 256)
                rk = "row%d" % (cb % 2)
                CP("dve", rw[0:1, :], pr[0:1, 0:256], r=prk, w=[rk])
                for j in range(2):
                    MM(pm[:, cb * 2 + j:cb * 2 + j + 1], rw[0:1, j * 128:(j + 1) * 128], cst[0:1, C_ONE:C_ONE + 1],
                       r=[rk, "cst"], w=PK(7, 0, 256))
            TT("dve", modv[:, 0:144], pm[:, 0:144], vec[:, V_BADA:V_BADA + 144], ALU.add, r=PK(7, 0, 256) + ["vec"], w=["modv"])
            for i in range(3):
                TS("dve", tmpv, mcol(3 * i + 1), 1.0, None, ALU.add, r=["modv"], w=["tmpv"])
                TT("dve", Sv[i], tmpv, gain(2 * i), ALU.mult, r=["tmpv", "vec"], w=["Sv%d" % i])
                STT("dve", coef[i], mcol(3 * i + 2), (1.0 if i == 1 else 0.5), gain(2 * i + 1), ALU.mult, ALU.mult,
                    r=["modv", "vec"], w=["coef%d" % i])

        HK = ["H%d" % k for k in range(16)]
        AK = ["A%d" % k for k in range(FC)]
        XFK = lambda kc: [AK[2 * kc], AK[2 * kc + 1]]
        X0v = X0_d.rearrange("(k p) t -> p k t", p=128)
        X1v = X1_d.rearrange("(k p) t -> p k t", p=128)
        H2v = H2_d.rearrange("(k p) t -> p k t", p=128)
        Yv = Y_d.rearrange("(k p) t -> p k t", p=128)
        YFv = YF_d.rearrange("(k p) t -> p k t", p=128)

        def stat_acc(src, srckeys, first, last):
            ACT(sqb, src, AF.Square, r=srckeys, w=["sq"])
            for hf in range(2):
                MM(PS[6 + hf][:, :], ones_b, sqb[:, hf * 512:(hf + 1) * 512], start=first, stop=last,
                   r=["cstb", "sq"], w=PK(6 + hf))

        def stat_fin():
            for hf in range(2):
                ACT(rstdB[:, hf * 512:(hf + 1) * 512], PS[6 + hf][:, :], AF.Ln, r=PK(6 + hf), w=["rstdB"],
                    scale=1.0 / D, bias=EPS)
            ACT(rstdB, rstdB, AF.Exp, r=["rstdB"], w=["rstdB"], scale=-0.5)

        def mod_to_hT(i, kc, src, srckeys):
            t = cy[kc % 2]
            STT("dve", t, src, Sv[i][:, kc:kc + 1], rstdB, ALU.mult, ALU.mult,
                r=srckeys + ["Sv%d" % i, "rstdB"], w=["cy%d" % (kc % 2)])
            ACT(hT[:, kc, :], t, AF.Identity, r=["cy%d" % (kc % 2), "modv"], w=[HK[kc]], bias=mcol(3 * i)[:, kc:kc + 1])

        def prenorm_dram(i, Xsrc, ts, xkey):
            stat_fin()
            for kc in range(16):
                S.dma("sp", cx[kc % 2], Xsrc[:, kc, ts], reads=[xkey], writes=["cx%d" % (kc % 2)])
                mod_to_hT(i, kc, cx[kc % 2], ["cx%d" % (kc % 2)])

        def y_sink(dm, buf, bkey):
            stat_acc(buf, [bkey], dm == 0, dm == 15)
            S.dma("sp", YFv[:, dm, :], buf, reads=[bkey], writes=["YF%d" % dm])

        def proj_T(inT, inkeys, nk, W):
            groups = [(g, min(8, nk - g)) for g in range(0, nk, 8)]
            for db in range(8):
                for (g0, n) in groups:
                    wv, wk = wblock(W[g0 * 128:(g0 + n) * 128, db * 256:(db + 1) * 256], n, 256)
                    for sub in range(2):
                        for hf in range(2):
                            bk = 2 + sub * 2 + hf
                            for f in range(n):
                                k = g0 + f
                                MM(PS[bk][:, :], wv[:, f, sub * 128:(sub + 1) * 128], inT[:, k, hf * 512:(hf + 1) * 512],
                                   start=(k == 0), stop=(k == nk - 1), r=[wk, inkeys[k]], w=PK(bk))
                for sub in range(2):
                    dm = db * 2 + sub
                    buf, bkey = cy[dm % 2], "cy%d" % (dm % 2)
                    for hf in range(2):
                        bk = 2 + sub * 2 + hf
                        CP("act" if bk % 2 == 0 else "dve", buf[:, hf * 512:(hf + 1) * 512], PS[bk][:, :], r=PK(bk), w=[bkey])
                    y_sink(dm, buf, bkey)

        def ffn(Wg, Wu, Wd):
            for fb in range(22):
                blk = {}
                for mi, Wm in enumerate((Wg, Wu)):
                    for kh in range(2):
                        blk[(mi, kh)] = wblock(Wm[kh * 1024:(kh + 1) * 1024, fb * 256:(fb + 1) * 256], 8, 256,
                                               engs=("act", "pool"))
                for sub in range(2):
                    fc = fb * 2 + sub
                    for hf in range(2):
                        hs = slice(hf * 512, (hf + 1) * 512)
                        for mi in range(2):
                            bk = 2 * hf + mi
                            for kc in range(16):
                                wv, wk = blk[(mi, kc // 8)]
                                MM(PS[bk][:, :], wv[:, kc % 8, sub * 128:(sub + 1) * 128], hT[:, kc, hs],
                                   start=(kc == 0), stop=(kc == 15), r=[wk, HK[kc]], w=PK(bk))
                        tb, tk = cy[hf][:, 0:512], "cy%d" % hf
                        ACT(tb, PS[2 * hf][:, :], AF.Silu, r=PK(2 * hf), w=[tk])
                        TT("dve", actT[:, fc, hs], tb, PS[2 * hf + 1][:, :], ALU.mult, r=[tk] + PK(2 * hf + 1), w=[AK[fc]])
            proj_T(actT, AK, FC, Wd)

        m0c, m1c = vec[:, V_PAR:V_PAR + 1], vec[:, V_PAR + 1:V_PAR + 2]

        def post_res(i, Xsrc, ts, xkey, dst, need_stats, ts2=None):
            stat_fin()
            for kc in range(16):
                cb, ck = cx[kc % 2], "cx%d" % (kc % 2)
                yb, yk = cy[kc % 2], "cy%d" % (kc % 2)
                S.dma("sp", yb, YFv[:, kc, :], reads=["YF%d" % kc], writes=[yk])
                S.dma("sp", cb, Xsrc[:, kc, ts], reads=[xkey], writes=[ck])
                if ts2 is not None:
                    c2 = vf32(actT_o + 16384 + 2048 * (kc % 2), 1024)
                    c2k = [AK[16 + 2 * (kc % 2)], AK[17 + 2 * (kc % 2)]]
                    S.dma("sp", c2, Xsrc[:, kc, ts2], reads=[xkey], writes=c2k)
                    TS("dve", c2, c2, m1c, None, ALU.mult, r=c2k + ["vec"], w=c2k)
                    STT("dve", cb, cb, m0c, c2, ALU.mult, ALU.add, r=[ck, "vec"] + c2k, w=[ck])
                STT("dve", yb, yb, coef[i][:, kc:kc + 1], rstdB, ALU.mult, ALU.mult, r=[yk, "coef%d" % i, "rstdB"], w=[yk])
                TT("pool", cb, cb, yb, ALU.add, r=[ck, yk], w=[ck])
                if need_stats:
                    stat_acc(cb, [ck], kc == 0, kc == 15)
                dst(kc, cb, ck)

        def phase1():
            for t in range(T // 1024):
                ts = slice(t * 1024, (t + 1) * 1024)
                for j in range(8):
                    xb, xbk = xin[j % 2], AK[32 + 4 * (j % 2):36 + 4 * (j % 2)]
                    S.dma("sp", xb, x_d[t * 1024 + j * 128:t * 1024 + (j + 1) * 128, :], writes=xbk)
                    for q4 in range(4):
                        bk = 4 + q4 % 2
                        for c in range(4):
                            kc = q4 * 4 + c
                            TR(PS[bk][:, c * 128:(c + 1) * 128], xb[:, kc * 128:(kc + 1) * 128], r=["cst"] + xbk, w=PK(bk))
                        CP("act" if bk % 2 == 0 else "dve", xTf[:, q4 * 4:(q4 + 1) * 4, j * 128:(j + 1) * 128],
                           PS[bk][:, :].rearrange("p (c t) -> p c t", c=4), r=PK(bk), w=AK[8 * q4:8 * q4 + 8])
                S.dma("sp", X0v[:, :, ts], xTf, reads=AK[0:32], writes=["X0s"])
                for kc in range(16):
                    stat_acc(xTf[:, kc, :], XFK(kc), kc == 0, kc == 15)
                stat_fin()
                for kc in range(16):
                    mod_to_hT(0, kc, xTf[:, kc, :], XFK(kc))
                ffn(w1g_d, w1u_d, w1d_d)
                post_res(0, X0v, ts, "X0s", lambda kc, cb, ck: S.dma("sp", X1v[:, kc, ts], cb, reads=[ck], writes=["X1s"]), True)
                prenorm_dram(1, X1v, ts, "X1s")
                S.dma("sp", H2v[:, :, ts], hT, reads=HK, writes=["H2s"])

        def phase3():
            tiles = [0] if split else list(range(T // 1024))
            for t in tiles:
                ts = slice(t * 1024, (t + 1) * 1024)
                ts2 = slice(1024, 2048) if split else None
                S.dma("sp", hT, Yv[:, :, ts], reads=["Ys"], writes=HK)
                if split:
                    hB = vbf(actT_o, 16 * 1024).rearrange("p (k t) -> p k t", k=16)
                    S.dma("sp", hB, Yv[:, :, ts2], reads=["Ys"], writes=AK[0:16])
                    for kc in range(16):
                        TS("dve", hB[:, kc, :], hB[:, kc, :], m1c, None, ALU.mult, r=[AK[kc], "vec"], w=[AK[kc]])
                        STT("dve", hT[:, kc, :], hT[:, kc, :], m0c, hB[:, kc, :], ALU.mult, ALU.add,
                            r=[HK[kc], "vec", AK[kc]], w=[HK[kc]])
                proj_T(hT, HK, 16, wout_d)
                post_res(1, X1v, ts, "X1s", lambda kc, cb, ck: S.dma("sp", X0v[:, kc, ts], cb, reads=[ck], writes=["X0s"]), True,
                         ts2=ts2)
                prenorm_dram(2, X0v, ts, "X0s")
                ffn(w2g_d, w2u_d, w2d_d)
                post_res(2, X0v, ts, "X0s",
                         lambda kc, cb, ck: CP("act", xTf[:, kc, :], cb, r=[ck], w=XFK(kc)), False)
                for j in range(8):
                    xb, xbk = xin[j % 2], AK[32 + 4 * (j % 2):36 + 4 * (j % 2)]
                    for q4 in range(4):
                        bk = 4 + q4 % 2
                        for c in range(4):
                            kc = q4 * 4 + c
                            TR(PS[bk][:, c * 128:(c + 1) * 128], xTf[:, kc, j * 128:(j + 1) * 128], r=["cst"] + XFK(kc), w=PK(bk))
                        CP("act" if bk % 2 == 0 else "dve", xb[:, q4 * 512:(q4 + 1) * 512], PS[bk][:, :], r=PK(bk), w=xbk)
                    S.dma("sp", out_d[t * 1024 + j * 128:t * 1024 + (j + 1) * 128, :], xb, reads=xbk, writes=["out"])

        def phase2():
            ring["nstg"], ring["nwb"] = 2, 4
            h2_o = [wb_off[4], None]
            apos[0] = main_o
            h2_o[1] = alloc(16384)
            raw_o = [alloc(4 * T), alloc(4 * T), alloc(4 * T), alloc(4 * T)]
            cb_o = alloc(4 * T)
            Kt_o, Vt_o = alloc(4 * T), alloc(4 * T)
            AT_o, wT_o, u_o = alloc(4 * T), alloc(4 * T), alloc(4 * T)
            o_o = raw_o[2]
            G = 2
            un_o = [dict(B=alloc(512), dec=alloc(512), decS=alloc(512), decT=alloc(512), N=alloc(512), A=alloc(512),
                         XY=[alloc(512) for _ in range(4)], R=[alloc(1024), alloc(1024)]) for _ in range(G)]
            S_o = [alloc(512), alloc(512)]
            vn_o = [alloc(512), alloc(512)]
            kd_o = [alloc(512), alloc(512)]
            ot_o = [alloc(512), alloc(512)]
            on_o = [alloc(512), alloc(512)]
            yh_o = [alloc(2 * T)] * 2
            ab_o = alloc(NTL * 16 * 4)
            sm_o = {k: alloc(NTL * 8 * 4) for k in ("gcol", "beta", "nbeta", "Gc", "gl", "eG", "edec", "egl", "bEG", "t1", "t2")}
            wab_o = alloc(16 * 16 * 2)
            ss_o = alloc(64 * 4)
            km_o = alloc(64)
            gate_o = alloc(NTL * 8 * 4)
            gtmp_o = [alloc(64) for _ in range(4)]
            seln_o = alloc(NTL * 8 * 4)
            pt_o = [alloc(256) for _ in range(4)]
            rl_o = [alloc(512), alloc(512)]
            qb_o, kb_o, vtk_o = AT_o, wT_o, u_o
            if T >= 2048:
                selT_o = Vt_o + 2048
                rt_o = [Vt_o, Vt_o + 1024]
            else:
                selT_o = alloc(2 * T)
                rt_o = [alloc(2048), alloc(2048)]

            h2t = [vbf(o, 16 * 512).rearrange("p (k t) -> p k t", k=16) for o in h2_o]
            raw = [vf32(o, T) for o in raw_o]
            cbuf = vf32(cb_o, T)
            Kt = vf32(Kt_o, NTL * 128).rearrange("p (n d) -> p n d", n=NTL)
            Vt = vf32(Vt_o, NTL * 128).rearrange("p (n d) -> p n d", n=NTL)
            ATa = vf32(AT_o, NTL * 128).rearrange("p (n d) -> p n d", n=NTL)
            wTa = vf32(wT_o, NTL * 128).rearrange("p (n d) -> p n d", n=NTL)
            ua = vf32(u_o, NTL * 128).rearrange("p (n d) -> p n d", n=NTL)
            oa = vf32(o_o, NTL * 128).rearrange("p (n d) -> p n d", n=NTL)
            sm = {k: vf32(o, NTL * 8).rearrange("p (n h) -> p n h", n=NTL) for k, o in sm_o.items()}
            ab = vf32(ab_o, NTL * 16).rearrange("p (n c) -> p n c", n=NTL)
            wab = vbf(wab_o, 256).rearrange("p (k c) -> p k c", k=16)
            ssv = vf32(ss_o, 64)

            def h2load(tt, slot):
                S.dma("sp", h2t[slot], H2v[:, :, tt * 512:(tt + 1) * 512], reads=["H2s"], writes=["h2t%d" % slot])

            sv = vf32(stg_off[0], 256).rearrange("p (k c) -> p k c", k=16)
            S.dma("sp", sv, win_d[:, 4096:4112].rearrange("(k p) c -> p k c", p=128), writes=["stg0"])
            CP("dve", wab, sv, r=["stg0"], w=["wab"])
            for tt in range(NT):
                h2load(tt, tt % 2)
                for sub in range(4):
                    tl = tt * 4 + sub
                    for kc in range(16):
                        MM(PS[0][:, tl * 16:(tl + 1) * 16], h2t[tt % 2][:, kc, sub * 128:(sub + 1) * 128], wab[:, kc, :],
                           start=(kc == 0), stop=(kc == 15), r=["h2t%d" % (tt % 2), "wab"], w=PK(0, 0, 256))
            CP("dve", ab.rearrange("p n c -> p (n c)"), PS[0][:, 0:NTL * 16], r=PK(0, 0, 256), w=["ab"])
            flat = lambda k: sm[k].rearrange("p n h -> p (n h)")
            alog = vec[:, V_ALOG:V_ALOG + 128].rearrange("p (n h) -> p n h", n=16)[:, 0:NTL, :]
            dtb = vec[:, V_DTB:V_DTB + 128].rearrange("p (n h) -> p n h", n=16)[:, 0:NTL, :]
            TT("dve", sm["t1"], ab[:, :, 0:8], dtb, ALU.add, r=["ab", "vec"], w=["t1"])
            ACT(flat("t1"), flat("t1"), AF.Exp, r=["t1"], w=["t1"])
            ACT(flat("t1"), flat("t1"), AF.Ln, r=["t1"], w=["t1"], bias=1.0)
            ACT(sm["t2"], alog, AF.Exp, r=["vec"], w=["t2"])
            STT("dve", flat("gcol"), flat("t1"), -1.0, flat("t2"), ALU.mult, ALU.mult, r=["t1", "t2"], w=["gcol"])
            ACT(sm["beta"], ab[:, :, 8:16], AF.Sigmoid, r=["ab"], w=["beta"])
            TS("dve", flat("nbeta"), flat("beta"), -1.0, None, ALU.mult, r=["beta"], w=["nbeta"])
            for tl in range(NTL):
                MM(PS[1][:, tl * 8:(tl + 1) * 8], triu_f, sm["gcol"][:, tl, :], r=["cst", "gcol"], w=PK(1, 0, 128))
                MM(PS[2][:, tl * 8:(tl + 1) * 8], ones_f, sm["gcol"][:, tl, :], r=["cst", "gcol"], w=PK(2, 0, 128))
            CP("dve", flat("Gc"), PS[1][:, 0:NTL * 8], r=PK(1, 0, 128), w=["Gc"])
            CP("dve", flat("gl"), PS[2][:, 0:NTL * 8], r=PK(2, 0, 128), w=["gl"])
            ACT(flat("eG"), flat("Gc"), AF.Exp, r=["Gc"], w=["eG"])
            ACT(flat("egl"), flat("gl"), AF.Exp, r=["gl"], w=["egl"])
            TT("dve", flat("t1"), flat("gl"), flat("Gc"), ALU.subtract, r=["gl", "Gc", "t1"], w=["t1"])
            ACT(flat("edec"), flat("t1"), AF.Exp, r=["t1"], w=["edec"])
            TT("dve", flat("bEG"), flat("beta"), flat("eG"), ALU.mult, r=["beta", "eG"], w=["bEG"])

            if lvl < 1:
                ring["nstg"], ring["nwb"] = NSTG, NWB
                return

            def inproj(cols, nchunk):
                blks = [wblock(win_d[:, c0:c0 + 128], 16, 128, engs=("act", "pool")) for c0 in cols]
                for tt in range(NT):
                    h2load(tt, tt % 2)
                    for ci in range(nchunk):
                        wv, wk = blks[ci]
                        for kc in range(16):
                            MM(PS[ci][:, :], wv[:, kc, :], h2t[tt % 2][:, kc, :], start=(kc == 0), stop=(kc == 15),
                               r=[wk, "h2t%d" % (tt % 2)], w=PK(ci))
                        CP("act" if ci % 2 else "dve", raw[ci][:, tt * 512:(tt + 1) * 512], PS[ci][:, :],
                           r=PK(ci), w=["raw%d" % ci])

            def to_tokmajor(src, sk, dstflat, dk_):
                for g4 in range(NTL // 4):
                    b = 6 + g4 % 2
                    for c in range(4):
                        tl = g4 * 4 + c
                        TR(PS[b][:, c * 128:(c + 1) * 128], src[:, tl * 128:(tl + 1) * 128], r=["cst", sk],
                           w=PK(b, c * 128, (c + 1) * 128))
                    CP("act" if g4 % 2 else "dve", dstflat(g4), PS[b][:, :], r=PK(b), w=[dk_])

            def gdn_head(h):
                inproj([h * 128, 1024 + h * 128, 2048 + h * 128, 3072 + h * 128], 4)
                for ci in range(3):
                    eng = "dve"
                    cw0 = V_CONV + (ci * 8 + h) * 4
                    x = raw[ci]
                    rk = "raw%d" % ci
                    TS(eng, cbuf[:, :], x[:, :], vec[:, cw0 + 3:cw0 + 4], None, ALU.mult, r=[rk, "vec"], w=["cbuf"])
                    for sft in (1, 2, 3):
                        STT(eng, cbuf[:, sft:T], x[:, 0:T - sft], vec[:, cw0 + 3 - sft:cw0 + 4 - sft], cbuf[:, sft:T],
                            ALU.mult, ALU.add, r=[rk, "vec", "cbuf"], w=["cbuf"])
                    ACT(x[:, :], cbuf[:, :], AF.Silu, r=["cbuf"], w=[rk])
                    if ci < 2:
                        ACT(cbuf[:, :], x[:, :], AF.Square, r=[rk], w=["cbuf"])
                        for tt in range(NT):
                            MM(PS[tt][:, :], ones_f, cbuf[:, tt * 512:(tt + 1) * 512], r=["cst", "cbuf"], w=PK(tt))
                        for tt in range(NT):
                            ACT(cbuf[:, tt * 512:(tt + 1) * 512], PS[tt][:, :], AF.Ln, r=PK(tt), w=["cbuf"], bias=EPS)
                        ACT(cbuf[:, :], cbuf[:, :], AF.Exp, r=["cbuf"], w=["cbuf"], scale=-0.5)
                        if ci == 0:
                            STT("dve", x[:, :], x[:, :], 128.0 ** -0.5, cbuf[:, :], ALU.mult, ALU.mult, r=[rk, "cbuf"], w=[rk])
                        else:
                            TT("dve", x[:, :], x[:, :], cbuf[:, :], ALU.mult, r=[rk, "cbuf"], w=[rk])
                ACT(raw[3][:, :], raw[3][:, :], AF.Silu, r=["raw3"], w=["raw3"])
                TS("pool", raw[3][:, :], raw[3][:, :], vec[:, V_NG:V_NG + 1], None, ALU.mult, r=["raw3", "vec"], w=["raw3"])
                qT, kT, vT, gz = raw
                if lvl < 2:
                    return
                to_tokmajor(kT, "raw1", lambda g4: Kt[:, g4 * 4:(g4 + 1) * 4, :].rearrange("p n d -> p (n d)"), "Kt")
                to_tokmajor(vT, "raw2", lambda g4: Vt[:, g4 * 4:(g4 + 1) * 4, :].rearrange("p n d -> p (n d)"), "Vt")
                if lvl < 2.5:
                    return
                for g0 in range(0, NTL, G):
                    units = list(range(g0, min(NTL, g0 + G)))
                    U = {}
                    for tl in units:
                        u = tl % G
                        o = un_o[u]
                        U[tl] = dict(
                            B=vf32(o["B"], 128), dec=vf32(o["dec"], 128), decS=vf32(o["decS"], 128), decT=vf32(o["decT"], 128),
                            N=vf32(o["N"], 128), A=vf32(o["A"], 128), XY=[vf32(q, 128) for q in o["XY"]],
                            R=[vf32(q, 256) for q in o["R"]], pa=PS[2 * u][:, 0:128], pb=PS[2 * u][:, 128:256], pc=PS[2 * u + 1][:, 0:256],
                            pak=PK(2 * u, 0, 128), pbk=PK(2 * u, 128, 256), pck=PK(2 * u + 1, 0, 256),
                            k=(lambda s_, u=u: "u%d%s" % (u, s_)), tsl=slice(tl * 128, (tl + 1) * 128))
                    for tl in units:
                        d = U[tl]; k = d["k"]
                        TS("pool", d["B"], sgtj_f, sm["gcol"][:, tl, h:h + 1], None, ALU.mult, r=["cst", "gcol"], w=[k("B")])
                        MM(d["pa"], triu_f, d["B"], r=["cst", k("B")], w=d["pak"])
                        ACT(d["dec"], d["pa"], AF.Exp, r=d["pak"], w=[k("dec")])
                        if lvl < 2.6:
                            continue
                        MM(d["pa"], kT[:, d["tsl"]], kT[:, d["tsl"]], r=["raw1"], w=d["pak"])
                        MM(d["pb"], qT[:, d["tsl"]], kT[:, d["tsl"]], r=["raw0", "raw1"], w=d["pbk"])
                        TT("pool", d["decS"], d["dec"], strict_f, ALU.mult, r=[k("dec"), "cst"], w=[k("decS")])
                        TT("pool", d["decT"], d["dec"], tril_f, ALU.mult, r=[k("dec"), "cst"], w=[k("decT")])
                        STT("dve", d["N"], d["pa"], sm["nbeta"][:, tl, h:h + 1], d["decS"], ALU.mult, ALU.mult,
                            r=d["pak"] + ["nbeta", k("decS")], w=[k("N")])
                        TT("dve", d["A"], d["pb"], d["decT"], ALU.mult, r=d["pbk"] + [k("decT")], w=[k("A")])
                        if lvl < 2.7:
                            continue
                        TR(d["pa"], d["N"], r=["cst", k("N")], w=d["pak"])
                        TR(d["pb"], d["A"], r=["cst", k("A")], w=d["pbk"])
                        CP("act", d["XY"][1], d["pa"], r=d["pak"], w=[k("XY1")])
                        CP("act", ATa[:, tl, :], d["pb"], r=d["pbk"], w=["ATa"])
                        if lvl < 2.78:
                            continue
                        TS("pool", d["R"][0][:, 0:128], Vt[:, tl, :], sm["beta"][:, tl, h:h + 1], None, ALU.mult,
                           r=["Vt", "beta"], w=[k("R0")])
                        TS("pool", d["R"][0][:, 128:256], Kt[:, tl, :], sm["bEG"][:, tl, h:h + 1], None, ALU.mult,
                           r=["Kt", "bEG", k("R0")], w=[k("R0")])
                    if lvl < 2.8:
                        continue
                    nlv = 7 if lvl >= 3 else int(round((lvl - 2.8) * 100))
                    for lv in range(nlv):
                        for tl in units:
                            d = U[tl]; k = d["k"]
                            if lv == 0:
                                X, Xk, Y, Yk = d["N"], k("N"), d["XY"][1], k("XY1")
                            else:
                                xi, yi = (0, 1) if lv % 2 == 0 else (2, 3)
                                X, Xk, Y, Yk = d["XY"][xi], k("XY%d" % xi), d["XY"][yi], k("XY%d" % yi)
                            Rc, Rck = d["R"][lv % 2], k("R%d" % (lv % 2))
                            cv = "rs"
                            if "r" in cv:
                                MM(d["pc"], Y, Rc, r=[Yk, Rck], w=d["pck"])
                            if lv < 6 or nlv < 7:
                                Rn, Rnk = d["R"][(lv + 1) % 2], k("R%d" % ((lv + 1) % 2))
                                if "r" in cv:
                                    TT("dve", Rn, Rc, d["pc"], ALU.add, r=[Rck] + d["pck"], w=[Rnk])
                                nxi, nyi = (0, 1) if (lv + 1) % 2 == 0 else (2, 3)
                                if "s" in cv:
                                    MM(d["pa"], Y, X, r=[Yk, Xk], w=d["pak"])
                                    MM(d["pb"], X, Y, r=[Yk, Xk], w=d["pbk"])
                                    CP("dve", d["XY"][nxi], d["pa"], r=d["pak"], w=[k("XY%d" % nxi)])
                                    CP("dve", d["XY"][nyi], d["pb"], r=d["pbk"], w=[k("XY%d" % nyi)])
                            else:
                                TT("dve", ua[:, tl, :], Rc[:, 0:128], d["pc"][:, 0:128], ALU.add, r=[Rck] + d["pck"], w=["ua"])
                                TT("dve", d["A"], Rc[:, 128:256], d["pc"][:, 128:256], ALU.add, r=[Rck] + d["pck"], w=[k("A")])
                                TR(d["pa"], d["A"], r=["cst", k("A")], w=d["pak"])
                                CP("act", wTa[:, tl, :], d["pa"], r=d["pak"], w=["wTa"])
                if lvl < 4:
                    return
                Sst = [vf32(o, 128) for o in S_o]
                MEMSET("pool", Sst[0], 0.0, w=["S0"])
                for tl in range(NTL):
                    cur, nxt = tl % 2, (tl + 1) % 2
                    b = 4 + tl % 2
                    pb_ = PS[b]
                    tsl = slice(tl * 128, (tl + 1) * 128)
                    vn, kd, ot = vf32(vn_o[tl % 2], 128), vf32(kd_o[tl % 2], 128), vf32(ot_o[tl % 2], 128)
                    vk, kk, ok_ = "vn%d" % (tl % 2), "kd%d" % (tl % 2), "ot%d" % (tl % 2)
                    MM(pb_[:, 0:128], wTa[:, tl, :], Sst[cur], r=["wTa", "S%d" % cur], w=PK(b, 0, 128))
                    TT("dve", vn, ua[:, tl, :], pb_[:, 0:128], ALU.subtract, r=["ua"] + PK(b, 0, 128), w=[vk])
                    MM(pb_[:, 128:256], qT[:, tsl], Sst[cur], r=["raw0", "S%d" % cur], w=PK(b, 128, 256))
                    MM(pb_[:, 256:384], ATa[:, tl, :], vn, r=["ATa", vk], w=PK(b, 256, 384))
                    TS("dve", ot, pb_[:, 128:256], sm["eG"][:, tl, h:h + 1], None, ALU.mult, r=PK(b, 128, 256) + ["eG"], w=[ok_])
                    TT("dve", oa[:, tl, :], ot, pb_[:, 256:384], ALU.add, r=[ok_] + PK(b, 256, 384), w=["raw2"])
                    TS("pool", kd, Kt[:, tl, :], sm["edec"][:, tl, h:h + 1], None, ALU.mult, r=["Kt", "edec"], w=[kk])
                    MM(pb_[:, 384:512], kd, vn, r=[kk, vk], w=PK(b, 384, 512))
                    STT("dve", Sst[nxt], Sst[cur], sm["egl"][:, tl, h:h + 1], pb_[:, 384:512], ALU.mult, ALU.add,
                        r=["S%d" % cur, "egl"] + PK(b, 384, 512), w=["S%d" % nxt])
                if lvl < 5:
                    return
                yh = vbf(yh_o[h % 2], T)
                yk = "yh"
                ACT(cbuf[:, :], oa.rearrange("p n d -> p (n d)"), AF.Square, r=["raw2"], w=["cbuf"])
                S.add("dve", lambda e: e.tensor_reduce(ssv[:, 0:NTL], cbuf[:, :].rearrange("p (n d) -> p n d", n=NTL), AX.X, ALU.add),
                      ["cbuf"], ["ssv"])
                ACT(ssv[:, 0:NTL], ssv[:, 0:NTL], AF.Ln, r=["ssv"], w=["ssv"], scale=1.0 / 128, bias=EPS)
                ACT(ssv[:, 0:NTL], ssv[:, 0:NTL], AF.Exp, r=["ssv"], w=["ssv"], scale=-0.5)
                for tl in range(NTL):
                    on = vf32(on_o[tl % 2], 128)
                    onk = "on%d" % (tl % 2)
                    b = 6 + tl % 2
                    TS("pool", on, oa[:, tl, :], ssv[:, tl:tl + 1], None, ALU.mult, r=["raw2", "ssv"], w=[onk])
                    TR(PS[b][:, 0:128], on, r=["cst", onk], w=PK(b, 0, 128))
                    TT("dve", yh[:, tl * 128:(tl + 1) * 128], PS[b][:, 0:128], gz[:, tl * 128:(tl + 1) * 128], ALU.mult,
                       r=PK(b, 0, 128) + ["raw3"], w=[yk])
                S.dma("sp", Y_d[h * 128:(h + 1) * 128, :], yh, reads=[yk], writes=["Ys"])

            def moba_head(m):
                inproj([4112 + m * 128, 5136 + m * 128, 6160 + m * 128], 3)
                rq, rk_, rv = raw[0], raw[1], raw[2]
                cosv = cbuf[0:32, 0:T]
                sinv = vf32(Kt_o, T)[0:32, :]
                pmf = cst[0:32, C_PM:C_PM + 32]
                for ci in range(2):
                    x = raw[ci]
                    xk = "raw%d" % ci
                    for tt in range(NT):
                        cs = slice(tt * 512, (tt + 1) * 512)
                        b = 4 + tt % 2
                        t1 = vf32(rt_o[0], 512)
                        t2 = vf32(rt_o[1], 512)
                        MM(PS[b][0:32, :], pmf, x[0:32, cs], r=["cst", xk], w=PK(b))
                        TT("dve", t1[0:32, :], x[0:32, cs], cosv[:, cs], ALU.mult, r=[xk, "cbuf"], w=["Vt"])
                        TT("dve", t2[0:32, :], PS[b][0:32, :], sinv[:, cs], ALU.mult, r=PK(b) + ["Kt", "Vt"], w=["Vt"])
                        TT("pool", x[0:32, cs], t1[0:32, :], t2[0:32, :], ALU.add, r=["Vt", xk], w=[xk])
                km = vf32(km_o, 8)
                S.add("dve", lambda e: e.tensor_reduce(km[:, 0:NB], rk_[:, :].rearrange("p (n t) -> p n t", n=NB), AX.X, ALU.add),
                      ["raw1"], ["km"])
                TS("dve", km[:, 0:NB], km[:, 0:NB], 1.0 / 256, None, ALU.mult, r=["km"], w=["km"])
                gate = vf32(gate_o, NTL * 8).rearrange("p (n b) -> p n b", n=NTL)
                seln = vf32(seln_o, NTL * 8).rearrange("p (n b) -> p n b", n=NTL)
                for tl in range(NTL):
                    MM(PS[5][:, tl * 8:tl * 8 + NB], rq[:, tl * 128:(tl + 1) * 128], km[:, 0:NB], r=["raw0", "km"], w=PK(5, 0, 128))
                CP("dve", gate[:, :, 0:NB], PS[5][:, 0:NTL * 8].rearrange("p (n b) -> p n b", b=8)[:, :, 0:NB], r=PK(5, 0, 128), w=["gate"])
                MEMSET("pool", seln.rearrange("p n b -> p (n b)"), 0.0, w=["seln"])
                gt = [vf32(o, 8) for o in gtmp_o]
                for tl in range(NTL):
                    own = tl // 2
                    if own <= 3:
                        continue
                    g = gate[:, tl, 0:own]
                    cur = g
                    ck = "gate"
                    for it in range(3):
                        S.add("dve", lambda e, cur=cur, it=it: e.tensor_reduce(gt[3][:, it:it + 1], cur, AX.X, ALU.max),
                              [ck, "gt3"], ["gt3"])
                        if it < 2:
                            TS("dve", gt[2][:, 0:own], cur, gt[3][:, it:it + 1], -1e30, ALU.is_ge, ALU.mult, r=[ck, "gt3"], w=["gt2"])
                            TT("dve", gt[it][:, 0:own], cur, gt[2][:, 0:own], ALU.add, r=[ck, "gt2"], w=["gt%d" % it])
                            cur = gt[it][:, 0:own]
                            ck = "gt%d" % it
                    TS("dve", seln[:, tl, 0:own], g, gt[3][:, 2:3], NEG, ALU.is_lt, ALU.mult, r=["gate", "gt3", "seln"], w=["seln"])
                selT = vbf(selT_o, T)
                for g4 in range(NTL // 4):
                    b = 6 + g4 % 2
                    for c in range(4):
                        tl = g4 * 4 + c
                        TR(PS[b][0:8, c * 128:(c + 1) * 128], seln[:, tl, :], r=["cst", "seln"], w=PK(b, c * 128, (c + 1) * 128))
                    CP("dve", selT[0:8, g4 * 512:(g4 + 1) * 512], PS[b][0:8, :], r=PK(b), w=["Vt"])
                qb, kb = vbf(qb_o, T), vbf(kb_o, T)
                CP("act", qb, rq[:, :], r=["raw0"], w=["ATa"])
                CP("pool", kb, rk_[:, :], r=["raw1"], w=["wTa"])
                vtk = vbf(vtk_o, NTL * 128).rearrange("p (n d) -> p n d", n=NTL)
                to_tokmajor(rv, "raw2", lambda g4: vtk[:, g4 * 4:(g4 + 1) * 4, :].rearrange("p n d -> p (n d)"), "ua")
                yh = vbf(yh_o[m % 2], T)
                yk = "yh"
                eb = cstb[0:8, C_EBLK:C_EBLK + 1024].rearrange("p (n k) -> p n k", n=8)
                items = []
                for i in range(NTL):
                    own = i // 2
                    keys = [(kt, "sel") for kt in range(2 * own)]
                    if i % 2:
                        keys.append((i - 1, "full"))
                    keys.append((i, "diag"))
                    for n_, (kt, kind) in enumerate(keys):
                        items.append((i, kt, kind, n_ == 0, n_ == len(keys) - 1))
                LA = 2
                ppb = (0, 1, 4)

                def stage12(idx):
                    i, kt, kind, first, last = items[idx]
                    qs = slice(i * 128, (i + 1) * 128)
                    bk = ppb[idx % 3]
                    pp = PS[bk][:, 0:128]
                    ptb, ptk = vbf(pt_o[idx % 4], 128), "pt%d" % (idx % 4)
                    MM(pp, kb[:, kt * 128:(kt + 1) * 128], qb[:, qs], start=True, stop=(kind != "sel"),
                       r=["wTa", "ATa"], w=PK(bk))
                    if kind == "sel":
                        MM(pp, eb[:, kt // 2, :], selT[0:8, qs], start=False, stop=True, r=["cstb", "Vt"], w=PK(bk))
                    ACT(ptb, pp, AF.Exp, r=PK(bk), w=[ptk], scale=128.0 ** -0.5)
                    if kind == "diag":
                        TT("pool", ptb, ptb, triu_b, ALU.mult, r=[ptk, "cstb"], w=[ptk])

                def stage3(idx):
                    i, kt, kind, first, last = items[idx]
                    qs = slice(i * 128, (i + 1) * 128)
                    ptb, ptk = vbf(pt_o[idx % 4], 128), "pt%d" % (idx % 4)
                    bo, bl = (2, 3) if i % 2 == 0 else (6, 7)
                    po, pl = PS[bo][:, 0:128], PS[bl][:, 0:128]
                    MM(po, vtk[:, kt, :], ptb, start=first, stop=last, r=["ua", ptk], w=PK(bo))
                    MM(pl, ones_b, ptb, start=first, stop=last, r=["cstb", ptk], w=PK(bl))
                    if last:
                        rl = vf32(rl_o[i % 2], 128)
                        rlk = "rl%d" % (i % 2)
                        S.add("dve", lambda e, rl=rl, pl=pl: e.reciprocal(rl, pl), PK(bl), [rlk])
                        TT("dve", yh[:, qs], po, rl, ALU.mult, r=PK(bo) + [rlk], w=[yk])

                for idx in range(len(items) + LA):
                    if idx < len(items):
                        stage12(idx)
                    if idx - LA >= 0:
                        stage3(idx - LA)
                S.dma("sp", Y_d[1024 + m * 128:1024 + (m + 1) * 128, :], yh, reads=[yk], writes=["Ys"])

            for h in range(ngh):
                if gdn_on:
                    gdn_head(h)
            if moba_on:
                S.dma("sp", cbuf[0:32, 0:T], rope_d[:, 0:T], writes=["cbuf"])
                S.dma("sp", vf32(Kt_o, T)[0:32, :], rope_d[:, T:2 * T], writes=["Kt"])
                for m in range(nmh):
                    moba_head(m)
            ring["nstg"], ring["nwb"] = NSTG, NWB

        if 0 in phases:
            phase0()
        if 1 in phases:
            phase1()
        if 2 in phases:
            S.barrier()
            phase2()
            S.barrier()
        if 3 in phases:
            phase3()
        S.finalize(st)
        S.run_block()
    return nc


def host_consts(T):
    c = np.zeros((128, NCONST), np.float32)
    i = np.arange(128)
    c[:, C_ID:C_ID + 128] = np.eye(128)
    c[:, C_ONE:C_ONE + 128] = 1.0
    c[:, C_TRIU:C_TRIU + 128] = (i[:, None] <= i[None, :])
    c[:, C_SGTJ:C_SGTJ + 128] = (i[:, None] > i[None, :])
    c[:, C_TRIL:C_TRIL + 128] = (i[:, None] >= i[None, :])
    c[:, C_STRICT:C_STRICT + 128] = (i[:, None] > i[None, :])
    for m in range(16):
        c[m + 16, C_PM + m] = -1.0
        c[m, C_PM + m + 16] = 1.0
    for n in range(8):
        c[n, C_EBLK + n * 128:C_EBLK + (n + 1) * 128] = 1.0
    inv = 500000.0 ** (-np.arange(0, 32, 2, dtype=np.float32) / 32)
    ang = np.arange(T, dtype=np.float32)[None, :] * inv[:, None].astype(np.float32)
    rope = np.zeros((32, 2 * T), np.float32)
    rope[0:16, 0:T] = np.cos(ang)
    rope[16:32, 0:T] = np.cos(ang)
    rope[0:16, T:] = np.sin(ang)
    rope[16:32, T:] = np.sin(ang)
    return c, rope


def col16(v):
    return np.ascontiguousarray(np.asarray(v, np.float32).reshape(-1, 128).T)


def host_vecs(inp, b, parity=0):
    v = np.zeros((128, NVEC), np.float32)
    v[:, V_C:V_C + 16] = col16(inp["c"][b])
    v[:, V_BADA:V_BADA + 144] = col16(inp["b_ada"][0])
    for i, nm in enumerate(("ffn1_pre_g", "ffn1_post_g", "mix_pre_g", "mix_post_g", "ffn2_pre_g", "ffn2_post_g")):
        v[:, V_G + i * 16:V_G + (i + 1) * 16] = col16(inp[nm][0])
    cw = np.asarray(inp["gdn_conv_w"][0], np.float32)
    v[:, V_CONV:V_CONV + 96] = cw.T.reshape(24, 128, 4).transpose(1, 0, 2).reshape(128, 96)
    v[:, V_ALOG:V_ALOG + 128] = np.tile(np.asarray(inp["gdn_a_log"][0], np.float32), 16)[None, :]
    v[:, V_DTB:V_DTB + 128] = np.tile(np.asarray(inp["gdn_dt_bias"][0], np.float32), 16)[None, :]
    v[:, V_NG] = np.asarray(inp["gdn_norm_g"][0], np.float32)
    v[:, V_PAR + parity] = 1.0
    return v


_NC_CACHE = {}


def kernel(**inp):
    inp = {k: np.asarray(v) for k, v in inp.items()}
    B, T, _ = inp["x"].shape
    if T not in _NC_CACHE:
        _NC_CACHE[T] = build_nc(T)
    nc = _NC_CACHE[T]
    consts, rope = host_consts(T)
    shared = dict(consts=consts, rope=rope, w_ada=np.ascontiguousarray(inp["w_ada"][0]),
                  w1g=np.ascontiguousarray(inp["ffn1_w_gate"][0]), w1u=np.ascontiguousarray(inp["ffn1_w_up"][0]),
                  w1d=np.ascontiguousarray(inp["ffn1_w_down"][0]), w2g=np.ascontiguousarray(inp["ffn2_w_gate"][0]),
                  w2u=np.ascontiguousarray(inp["ffn2_w_up"][0]), w2d=np.ascontiguousarray(inp["ffn2_w_down"][0]),
                  w_in=np.ascontiguousarray(inp["w_in"][0]), w_out=np.ascontiguousarray(inp["w_out"][0]))
    in_maps = []
    for core in range(8):
        b = core // 2
        m = dict(shared)
        m["x"] = np.ascontiguousarray(inp["x"][b])
        m["vecs"] = host_vecs(inp, b, core % 2)
        in_maps.append(m)
    res = run_bass_kernel_spmd(nc, in_maps, core_ids=list(range(8)))
    if T >= 2048:
        out = np.stack([np.concatenate([np.asarray(res.results[2 * b + p]["out"], np.float32) for p in range(2)], axis=0)
                        for b in range(B)], axis=0)
    else:
        out = np.stack([np.asarray(res.results[2 * b]["out"], np.float32) for b in range(B)], axis=0)
    return out
```

```python
import contextlib
import numpy as np
import concourse.bass as bass
import concourse.mybir as mybir
from concourse.bass_utils import run_bass_kernel_spmd

F32 = mybir.dt.float32
BF16 = mybir.dt.bfloat16
AF = mybir.ActivationFunctionType
ALU = mybir.AluOpType
AX = mybir.AxisListType

ENGS = ("pe", "act", "dve", "pool", "sp")

D = 2048
KC = 16
FF = 5632
FC = 44
NMOD = 9
INC = 7184
EPS = 1e-6
NEG = -30000.0


class Sched:
    def __init__(self, nc, n_dma_sems=48):
        self.nc = nc
        self.ops = []
        self.last_w = {}
        self.readers = {}
        self.n_dma_sems = n_dma_sems

    def add(self, eng, emit, reads=(), writes=(), dma=False):
        idx = len(self.ops)
        deps = set()
        for b in reads:
            if b in self.last_w:
                deps.add(self.last_w[b])
        for b in writes:
            if b in self.last_w:
                deps.add(self.last_w[b])
            rd = self.readers.get(b)
            if rd:
                deps.update(rd.values())
        deps.discard(idx)
        self.ops.append(dict(eng=eng, emit=emit, deps=deps, dma=dma, barrier=False))
        for b in reads:
            self.readers.setdefault(b, {})[("dma", idx) if dma else eng] = idx
        for b in writes:
            self.last_w[b] = idx
            self.readers[b] = {}
        return idx

    def barrier(self):
        self.ops.append(dict(eng=None, emit=None, deps=set(), dma=False, barrier=True))
        self.last_w = {}
        self.readers = {}

    def dma(self, q, out, in_, reads=(), writes=()):
        return self.add(q, lambda e: e.dma_start(out=out, in_=in_), reads, writes, dma=True)

    def finalize(self, stack):
        nc = self.nc
        ops = self.ops
        dma_count = 0
        for op in ops:
            op["signal"] = False
            if op["dma"]:
                op["dma_ord"] = dma_count
                dma_count += 1
        last_c = {}
        for i, op in enumerate(ops):
            if op["barrier"]:
                for e, j in last_c.items():
                    ops[j]["signal"] = True
                op["last_c"] = dict(last_c)
                continue
            for d in op["deps"]:
                od = ops[d]
                if od["dma"]:
                    continue
                if not (od["eng"] == "pe" and op["eng"] == "pe"):
                    od["signal"] = True
            if not op["dma"]:
                last_c[op["eng"]] = i
        sigcount = {e: 0 for e in ENGS}
        for op in ops:
            if op["barrier"] or op["dma"]:
                continue
            if op["signal"]:
                sigcount[op["eng"]] += 1
                op["sigval"] = sigcount[op["eng"]]
        esem = {e: stack.enter_context(nc.semaphore("s_" + e)) for e in ENGS}
        nd = min(self.n_dma_sems, max(1, dma_count))
        dsem = [stack.enter_context(nc.semaphore("d%d" % i)) for i in range(nd)]
        for op in ops:
            if op["dma"]:
                k = op["dma_ord"]
                op["dsem"] = dsem[k % nd]
                op["dval"] = 16 * (k // nd + 1)
                op["dslot"] = k % nd
        waited = {e: {} for e in ENGS}
        per_eng = {e: [] for e in ENGS}
        last_on_slot = {}
        pending = {e: [] for e in ENGS}
        for i, op in enumerate(ops):
            if op["barrier"]:
                for e in ENGS:
                    lst = []
                    for e2, j in op["last_c"].items():
                        lst.append((("e", e2), esem[e2], ops[j]["sigval"]))
                    for slot, o in last_on_slot.items():
                        lst.append((("d", slot), o["dsem"], o["dval"]))
                    pending[e] = lst
                continue
            e = op["eng"]
            waits = []

            def need(key, sem, val):
                if waited[e].get(key, 0) < val:
                    waited[e][key] = val
                    waits.append((sem, val))

            for key, sem, val in pending[e]:
                need(key, sem, val)
            pending[e] = []
            for d in sorted(op["deps"]):
                od = ops[d]
                if od["dma"]:
                    need(("d", od["dslot"]), od["dsem"], od["dval"])
                elif not (od["eng"] == "pe" and e == "pe"):
                    need(("e", od["eng"]), esem[od["eng"]], od["sigval"])
            if op["dma"]:
                prev = last_on_slot.get(op["dslot"])
                if prev is not None:
                    need(("d", op["dslot"]), prev["dsem"], prev["dval"])
                last_on_slot[op["dslot"]] = op
            op["waits"] = waits
            per_eng[e].append(op)
        self.esem = esem
        self.per_eng = per_eng
        self.final_dma_waits = [(o["dsem"], o["dval"]) for o in last_on_slot.values()]

    def emit_engine(self, e, eng, final=False):
        for op in self.per_eng[e]:
            for sem, val in op["waits"]:
                eng.wait_ge(sem, val)
            ins = op["emit"](eng)
            if op["dma"]:
                ins.then_inc(op["dsem"], 16)
            elif op["signal"]:
                ins.then_inc(self.esem[e], 1)
        if final:
            for sem, val in self.final_dma_waits:
                eng.wait_ge(sem, val)

    def run_block(self):
        nc = self.nc
        with nc.Block() as block:
            @block.tensor
            def _(eng):
                self.emit_engine("pe", eng)

            @block.scalar
            def _(eng):
                self.emit_engine("act", eng)

            @block.vector
            def _(eng):
                self.emit_engine("dve", eng)

            @block.gpsimd
            def _(eng):
                self.emit_engine("pool", eng)

            @block.sync
            def _(eng):
                self.emit_engine("sp", eng, final=True)


C_ID, C_ONE, C_TRIU, C_SGTJ, C_TRIL, C_STRICT, C_PM, C_EBLK = 0, 128, 256, 384, 512, 640, 768, 800
NCONST = 800 + 1024
V_C, V_BADA, V_G = 0, 16, 160
V_CONV = 256
V_ALOG = 352
V_DTB = 480
V_NG = 608
V_PAR = 609
NVEC = 611


def build_nc(T, phases=(0, 1, 2, 3), gdn_on=True, moba_on=True, lvl=99, ngh=8, nmh=8):
    NT = T // 512
    NTL = T // 128
    NB = T // 256
    nc = bass.Bass("TRN2", target_bir_lowering=False)
    dr = lambda name, shape, dt=F32, kind="ExternalInput": nc.dram_tensor(name, shape, dt, kind=kind).ap()
    x_d = dr("x", [T, D])
    consts_d = dr("consts", [128, NCONST])
    vecs_d = dr("vecs", [128, NVEC])
    rope_d = dr("rope", [32, 2 * T])
    wada_d = dr("w_ada", [D, NMOD * D])
    w1g_d, w1u_d, w1d_d = dr("w1g", [D, FF]), dr("w1u", [D, FF]), dr("w1d", [FF, D])
    w2g_d, w2u_d, w2d_d = dr("w2g", [D, FF]), dr("w2u", [D, FF]), dr("w2d", [FF, D])
    win_d = dr("w_in", [D, INC])
    wout_d = dr("w_out", [D, D])
    split = T >= 2048
    TO = 1024 if split else T
    out_d = dr("out", [TO, D], kind="ExternalOutput")
    X1_d = dr("X1s", [D, T], F32, "Internal")
    X0_d = dr("X0s", [D, T], F32, "Internal")
    YF_d = dr("YFs", [D, 1024], F32, "Internal")
    H2_d = dr("H2s", [D, T], BF16, "Internal")
    Y_d = dr("Ys", [D, T], BF16, "Internal")

    st = contextlib.ExitStack()
    with st:
        S = Sched(nc)
        arena = st.enter_context(nc.sbuf_tensor("arena", [128, 96 * 1024], BF16))
        cst = st.enter_context(nc.sbuf_tensor("cst", [128, NCONST], F32))
        cstb = st.enter_context(nc.sbuf_tensor("cstb", [128, NCONST], BF16))
        vec = st.enter_context(nc.sbuf_tensor("vec", [128, NVEC], F32))
        modv = st.enter_context(nc.sbuf_tensor("modv", [128, 144 + 9 * 16], F32))
        PS = [st.enter_context(nc.psum_tensor("ps%d" % i, [128, 512], F32)) for i in range(8)]

        apos = [0]

        def alloc(nbytes):
            o = apos[0]
            apos[0] += (nbytes + 63) // 64 * 32
            assert apos[0] <= 96 * 1024, apos[0]
            return o

        def vf32(off, n):
            return arena[:, off:off + 2 * n].bitcast(F32)

        def vbf(off, n):
            return arena[:, off:off + n]

        def MM(out, lhsT, rhs, start=True, stop=True, r=(), w=()):
            S.add("pe", lambda e: e.matmul(out, lhsT, rhs, start=start, stop=stop), r, w)

        def TR(out, in_, r=(), w=()):
            S.add("pe", lambda e: e.transpose(out, in_, cst[:, C_ID:C_ID + 128]), r, w)

        def ACT(out, in_, func, r=(), w=(), **kw):
            S.add("act", lambda e: e.activation(out, in_, func, **kw), r, w)

        def CP(eng, out, in_, r=(), w=()):
            if eng == "act":
                S.add("act", lambda e: e.copy(out, in_), r, w)
            else:
                S.add(eng, lambda e: e.tensor_copy(out, in_), r, w)

        def TS(eng, out, in0, s1, s2, op0, op1=None, r=(), w=()):
            if op1 is None:
                S.add(eng, lambda e: e.tensor_single_scalar(out, in0, s1, op0), r, w)
            else:
                S.add(eng, lambda e: e.tensor_scalar(out, in0, s1, s2, op0, op1), r, w)

        def STT(eng, out, in0, sc, in1, op0, op1, r=(), w=()):
            S.add(eng, lambda e: e.scalar_tensor_tensor(out, in0, sc, in1, op0, op1), r, w)

        def TT(eng, out, in0, in1, op, r=(), w=()):
            S.add(eng, lambda e: e.tensor_tensor(out, in0, in1, op), r, w)

        def MEMSET(eng, ap, val, r=(), w=()):
            S.add(eng, lambda e: e.memset(ap, val), r, w)

        def PK(b, lo=0, hi=512):
            return ["ps%d" % b]

        ident = cst[:, C_ID:C_ID + 128]
        ones_f = cst[:, C_ONE:C_ONE + 128]
        triu_f = cst[:, C_TRIU:C_TRIU + 128]
        sgtj_f = cst[:, C_SGTJ:C_SGTJ + 128]
        tril_f = cst[:, C_TRIL:C_TRIL + 128]
        strict_f = cst[:, C_STRICT:C_STRICT + 128]
        ones_b = cstb[:, C_ONE:C_ONE + 128]
        triu_b = cstb[:, C_TRIU:C_TRIU + 128]

        S.dma("sp", cst[:], consts_d, writes=["cst"])
        S.dma("sp", vec[:], vecs_d, writes=["vec"])
        CP("dve", cstb[:], cst[:], r=["cst"], w=["cstb"])

        NSTG, NWB = 2, 8
        stg_off = [alloc(8192), alloc(8192)]
        wb_off = [alloc(4096) for _ in range(8)]
        ring = dict(s=0, w=0, c=0, nstg=NSTG, nwb=NWB)
        cast_engs = ("act", "dve")

        def wblock(src, nk, ncols, engs=cast_engs):
            si = ring["s"] % ring["nstg"]
            wi = ring["w"] % ring["nwb"]
            ring["s"] += 1
            ring["w"] += 1
            sv = vf32(stg_off[si], nk * ncols).rearrange("p (k c) -> p k c", k=nk)
            wv = vbf(wb_off[wi], nk * ncols).rearrange("p (k c) -> p k c", k=nk)
            S.dma("sp", sv, src.rearrange("(k p) c -> p k c", p=128), writes=["stg%d" % si])
            eng = engs[ring["c"] % len(engs)]
            ring["c"] += 1
            CP(eng, wv, sv, r=["stg%d" % si], w=["wb%d" % wi])
            return wv, "wb%d" % wi

        main_o = apos[0]
        hT_o = alloc(32768)
        actT_o = alloc(FC * 2048)
        cx_o = [alloc(4096), alloc(4096)]
        cy_o = [alloc(4096), alloc(4096)]
        rstd_o = alloc(4096)
        sq_o = alloc(2048)
        scb_o = alloc(64)
        row_o = cy_o

        hT = vbf(hT_o, 16 * 1024).rearrange("p (k t) -> p k t", k=16)
        actT = vbf(actT_o, FC * 1024).rearrange("p (k t) -> p k t", k=FC)
        xTf = vf32(actT_o, 16 * 1024).rearrange("p (k t) -> p k t", k=16)
        xin = [vf32(actT_o + 32768, 2048), vf32(actT_o + 32768 + 4096, 2048)]
        cx = [vf32(o, 1024) for o in cx_o]
        cy = [vf32(o, 1024) for o in cy_o]
        rstdB = vf32(rstd_o, 1024)
        sqb = vbf(sq_o, 1024)

        def mcol(i):
            return modv[:, i * 16:(i + 1) * 16]
        Sv = [modv[:, 144 + i * 16:144 + (i + 1) * 16] for i in range(3)]
        coef = [modv[:, 144 + 48 + i * 16:144 + 48 + (i + 1) * 16] for i in range(3)]
        tmpv = modv[:, 144 + 96:144 + 112]

        def gain(i):
            return vec[:, V_G + i * 16:V_G + (i + 1) * 16]

        def phase0():
            scb = vbf(scb_o, 16)
            ACT(scb, vec[:, V_C:V_C + 16], AF.Silu, r=["vec"], w=["scb"])
            pm = PS[7]
            for cb in range(72):
                blks = [wblock(wada_d[kh * 1024:(kh + 1) * 1024, cb * 256:(cb + 1) * 256], 8, 256) for kh in range(2)]
                pr = PS[cb % 2]
                prk = PK(cb % 2, 0, 256)
                for kc in range(16):
                    wv, wk = blks[kc // 8]
                    MM(pr[0:1, 0:256], scb[:, kc:kc + 1], wv[:, kc % 8, :], start=(kc == 0), stop=(kc == 15),
                       r=["scb", wk], w=prk)
                rw = vf32(row_o[cb % 2], 256)
                rk = "row%d" % (cb % 2)
                CP("dve", rw[0:1, :], pr[0:1, 0:256], r=prk, w=[rk])
                for j in range(2):
                    MM(pm[:, cb * 2 + j:cb * 2 + j + 1], rw[0:1, j * 128:(j + 1) * 128], cst[0:1, C_ONE:C_ONE + 1],
                       r=[rk, "cst"], w=PK(7, 0, 256))
            TT("dve", modv[:, 0:144], pm[:, 0:144], vec[:, V_BADA:V_BADA + 144], ALU.add, r=PK(7, 0, 256) + ["vec"], w=["modv"])
            for i in range(3):
                TS("dve", tmpv, mcol(3 * i + 1), 1.0, None, ALU.add, r=["modv"], w=["tmpv"])
                TT("dve", Sv[i], tmpv, gain(2 * i), ALU.mult, r=["tmpv", "vec"], w=["Sv%d" % i])
                STT("dve", coef[i], mcol(3 * i + 2), (1.0 if i == 1 else 0.5), gain(2 * i + 1), ALU.mult, ALU.mult,
                    r=["modv", "vec"], w=["coef%d" % i])

        HK = ["H%d" % k for k in range(16)]
        AK = ["A%d" % k for k in range(FC)]
        XFK = lambda kc: [AK[2 * kc], AK[2 * kc + 1]]
        X0v = X0_d.rearrange("(k p) t -> p k t", p=128)
        X1v = X1_d.rearrange("(k p) t -> p k t", p=128)
        H2v = H2_d.rearrange("(k p) t -> p k t", p=128)
        Yv = Y_d.rearrange("(k p) t -> p k t", p=128)
        YFv = YF_d.rearrange("(k p) t -> p k t", p=128)

        def stat_acc(src, srckeys, first, last):
            ACT(sqb, src, AF.Square, r=srckeys, w=["sq"])
            for hf in range(2):
                MM(PS[6 + hf][:, :], ones_b, sqb[:, hf * 512:(hf + 1) * 512], start=first, stop=last,
                   r=["cstb", "sq"], w=PK(6 + hf))

        def stat_fin():
            for hf in range(2):
                ACT(rstdB[:, hf * 512:(hf + 1) * 512], PS[6 + hf][:, :], AF.Ln, r=PK(6 + hf), w=["rstdB"],
                    scale=1.0 / D, bias=EPS)
            ACT(rstdB, rstdB, AF.Exp, r=["rstdB"], w=["rstdB"], scale=-0.5)

        def mod_to_hT(i, kc, src, srckeys):
            t = cy[kc % 2]
            STT("dve", t, src, Sv[i][:, kc:kc + 1], rstdB, ALU.mult, ALU.mult,
                r=srckeys + ["Sv%d" % i, "rstdB"], w=["cy%d" % (kc % 2)])
            ACT(hT[:, kc, :], t, AF.Identity, r=["cy%d" % (kc % 2), "modv"], w=[HK[kc]], bias=mcol(3 * i)[:, kc:kc + 1])

        def prenorm_dram(i, Xsrc, ts, xkey):
            stat_fin()
            for kc in range(16):
                S.dma("sp", cx[kc % 2], Xsrc[:, kc, ts], reads=[xkey], writes=["cx%d" % (kc % 2)])
                mod_to_hT(i, kc, cx[kc % 2], ["cx%d" % (kc % 2)])

        def y_sink(dm, buf, bkey):
            stat_acc(buf, [bkey], dm == 0, dm == 15)
            S.dma("sp", YFv[:, dm, :], buf, reads=[bkey], writes=["YF%d" % dm])

        def proj_T(inT, inkeys, nk, W):
            groups = [(g, min(8, nk - g)) for g in range(0, nk, 8)]
            for db in range(8):
                for (g0, n) in groups:
                    wv, wk = wblock(W[g0 * 128:(g0 + n) * 128, db * 256:(db + 1) * 256], n, 256)
                    for sub in range(2):
                        for hf in range(2):
                            bk = 2 + sub * 2 + hf
                            for f in range(n):
                                k = g0 + f
                                MM(PS[bk][:, :], wv[:, f, sub * 128:(sub + 1) * 128], inT[:, k, hf * 512:(hf + 1) * 512],
                                   start=(k == 0), stop=(k == nk - 1), r=[wk, inkeys[k]], w=PK(bk))
                for sub in range(2):
                    dm = db * 2 + sub
                    buf, bkey = cy[dm % 2], "cy%d" % (dm % 2)
                    for hf in range(2):
                        bk = 2 + sub * 2 + hf
                        CP("act" if bk % 2 == 0 else "dve", buf[:, hf * 512:(hf + 1) * 512], PS[bk][:, :], r=PK(bk), w=[bkey])
                    y_sink(dm, buf, bkey)

        def ffn(Wg, Wu, Wd):
            for fb in range(22):
                blk = {}
                for mi, Wm in enumerate((Wg, Wu)):
                    for kh in range(2):
                        blk[(mi, kh)] = wblock(Wm[kh * 1024:(kh + 1) * 1024, fb * 256:(fb + 1) * 256], 8, 256,
                                               engs=("act", "dve"))
                for sub in range(2):
                    fc = fb * 2 + sub
                    for hf in range(2):
                        hs = slice(hf * 512, (hf + 1) * 512)
                        for mi in range(2):
                            bk = 2 * hf + mi
                            for kc in range(16):
                                wv, wk = blk[(mi, kc // 8)]
                                MM(PS[bk][:, :], wv[:, kc % 8, sub * 128:(sub + 1) * 128], hT[:, kc, hs],
                                   start=(kc == 0), stop=(kc == 15), r=[wk, HK[kc]], w=PK(bk))
                        tb, tk = cy[hf][:, 0:512], "cy%d" % hf
                        ACT(tb, PS[2 * hf][:, :], AF.Silu, r=PK(2 * hf), w=[tk])
                        TT("dve", actT[:, fc, hs], tb, PS[2 * hf + 1][:, :], ALU.mult, r=[tk] + PK(2 * hf + 1), w=[AK[fc]])
            proj_T(actT, AK, FC, Wd)

        m0c, m1c = vec[:, V_PAR:V_PAR + 1], vec[:, V_PAR + 1:V_PAR + 2]

        def post_res(i, Xsrc, ts, xkey, dst, need_stats, ts2=None):
            stat_fin()
            for kc in range(16):
                cb, ck = cx[kc % 2], "cx%d" % (kc % 2)
                yb, yk = cy[kc % 2], "cy%d" % (kc % 2)
                S.dma("sp", yb, YFv[:, kc, :], reads=["YF%d" % kc], writes=[yk])
                S.dma("sp", cb, Xsrc[:, kc, ts], reads=[xkey], writes=[ck])
                if ts2 is not None:
                    c2 = vf32(actT_o + 16384 + 2048 * (kc % 2), 1024)
                    c2k = [AK[16 + 2 * (kc % 2)], AK[17 + 2 * (kc % 2)]]
                    S.dma("sp", c2, Xsrc[:, kc, ts2], reads=[xkey], writes=c2k)
                    TS("dve", c2, c2, m1c, None, ALU.mult, r=c2k + ["vec"], w=c2k)
                    STT("dve", cb, cb, m0c, c2, ALU.mult, ALU.add, r=[ck, "vec"] + c2k, w=[ck])
                STT("dve", yb, yb, coef[i][:, kc:kc + 1], rstdB, ALU.mult, ALU.mult, r=[yk, "coef%d" % i, "rstdB"], w=[yk])
                TT("pool", cb, cb, yb, ALU.add, r=[ck, yk], w=[ck])
                if need_stats:
                    stat_acc(cb, [ck], kc == 0, kc == 15)
                dst(kc, cb, ck)

        def phase1():
            for t in range(T // 1024):
                ts = slice(t * 1024, (t + 1) * 1024)
                for j in range(8):
                    xb, xbk = xin[j % 2], AK[32 + 4 * (j % 2):36 + 4 * (j % 2)]
                    S.dma("sp", xb, x_d[t * 1024 + j * 128:t * 1024 + (j + 1) * 128, :], writes=xbk)
                    for q4 in range(4):
                        bk = 4 + q4 % 2
                        for c in range(4):
                            kc = q4 * 4 + c
                            TR(PS[bk][:, c * 128:(c + 1) * 128], xb[:, kc * 128:(kc + 1) * 128], r=["cst"] + xbk, w=PK(bk))
                        CP("act" if bk % 2 == 0 else "dve", xTf[:, q4 * 4:(q4 + 1) * 4, j * 128:(j + 1) * 128],
                           PS[bk][:, :].rearrange("p (c t) -> p c t", c=4), r=PK(bk), w=AK[8 * q4:8 * q4 + 8])
                S.dma("sp", X0v[:, :, ts], xTf, reads=AK[0:32], writes=["X0s"])
                for kc in range(16):
                    stat_acc(xTf[:, kc, :], XFK(kc), kc == 0, kc == 15)
                stat_fin()
                for kc in range(16):
                    mod_to_hT(0, kc, xTf[:, kc, :], XFK(kc))
                ffn(w1g_d, w1u_d, w1d_d)
                post_res(0, X0v, ts, "X0s", lambda kc, cb, ck: S.dma("sp", X1v[:, kc, ts], cb, reads=[ck], writes=["X1s"]), True)
                prenorm_dram(1, X1v, ts, "X1s")
                S.dma("sp", H2v[:, :, ts], hT, reads=HK, writes=["H2s"])

        def phase3():
            tiles = [0] if split else list(range(T // 1024))
            for t in tiles:
                ts = slice(t * 1024, (t + 1) * 1024)
                ts2 = slice(1024, 2048) if split else None
                S.dma("sp", hT, Yv[:, :, ts], reads=["Ys"], writes=HK)
                if split:
                    hB = vbf(actT_o, 16 * 1024).rearrange("p (k t) -> p k t", k=16)
                    S.dma("sp", hB, Yv[:, :, ts2], reads=["Ys"], writes=AK[0:16])
                    for kc in range(16):
                        TS("dve", hB[:, kc, :], hB[:, kc, :], m1c, None, ALU.mult, r=[AK[kc], "vec"], w=[AK[kc]])
                        STT("dve", hT[:, kc, :], hT[:, kc, :], m0c, hB[:, kc, :], ALU.mult, ALU.add,
                            r=[HK[kc], "vec", AK[kc]], w=[HK[kc]])
                proj_T(hT, HK, 16, wout_d)
                post_res(1, X1v, ts, "X1s", lambda kc, cb, ck: S.dma("sp", X0v[:, kc, ts], cb, reads=[ck], writes=["X0s"]), True,
                         ts2=ts2)
                prenorm_dram(2, X0v, ts, "X0s")
                ffn(w2g_d, w2u_d, w2d_d)
                post_res(2, X0v, ts, "X0s",
                         lambda kc, cb, ck: CP("act", xTf[:, kc, :], cb, r=[ck], w=XFK(kc)), False)
                for j in range(8):
                    xb, xbk = xin[j % 2], AK[32 + 4 * (j % 2):36 + 4 * (j % 2)]
                    for q4 in range(4):
                        bk = 4 + q4 % 2
                        for c in range(4):
                            kc = q4 * 4 + c
                            TR(PS[bk][:, c * 128:(c + 1) * 128], xTf[:, kc, j * 128:(j + 1) * 128], r=["cst"] + XFK(kc), w=PK(bk))
                        CP("act" if bk % 2 == 0 else "dve", xb[:, q4 * 512:(q4 + 1) * 512], PS[bk][:, :], r=PK(bk), w=xbk)
                    S.dma("sp", out_d[t * 1024 + j * 128:t * 1024 + (j + 1) * 128, :], xb, reads=xbk, writes=["out"])

        def phase2():
            ring["nstg"], ring["nwb"] = 2, 4
            h2_o = [wb_off[4], None]
            apos[0] = main_o
            h2_o[1] = alloc(16384)
            raw_o = [alloc(4 * T), alloc(4 * T), alloc(4 * T), alloc(4 * T)]
            cb_o = alloc(4 * T)
            Kt_o, Vt_o = alloc(4 * T), alloc(4 * T)
            AT_o, wT_o, u_o = alloc(4 * T), alloc(4 * T), alloc(4 * T)
            o_o = raw_o[2]
            G = 2
            un_o = [dict(B=alloc(512), dec=alloc(512), decS=alloc(512), decT=alloc(512), N=alloc(512), A=alloc(512),
                         XY=[alloc(512) for _ in range(4)], R=[alloc(1024), alloc(1024)]) for _ in range(G)]
            S_o = [alloc(512), alloc(512)]
            vn_o = [alloc(512), alloc(512)]
            kd_o = [alloc(512), alloc(512)]
            ot_o = [alloc(512), alloc(512)]
            on_o = [alloc(512), alloc(512)]
            yh_o = [alloc(2 * T)] * 2
            ab_o = alloc(NTL * 16 * 4)
            sm_o = {k: alloc(NTL * 8 * 4) for k in ("gcol", "beta", "nbeta", "Gc", "gl", "eG", "edec", "egl", "bEG", "t1", "t2")}
            wab_o = alloc(16 * 16 * 2)
            ss_o = alloc(64 * 4)
            km_o = alloc(64)
            gate_o = alloc(NTL * 8 * 4)
            gtmp_o = [alloc(64) for _ in range(4)]
            seln_o = alloc(NTL * 8 * 4)
            pt_o = [alloc(256) for _ in range(4)]
            rl_o = [alloc(512), alloc(512)]
            qb_o, kb_o, vtk_o = AT_o, wT_o, u_o
            if T >= 2048:
                selT_o = Vt_o + 2048
                rt_o = [Vt_o, Vt_o + 1024]
            else:
                selT_o = alloc(2 * T)
                rt_o = [alloc(2048), alloc(2048)]

            h2t = [vbf(o, 16 * 512).rearrange("p (k t) -> p k t", k=16) for o in h2_o]
            raw = [vf32(o, T) for o in raw_o]
            cbuf = vf32(cb_o, T)
            Kt = vf32(Kt_o, NTL * 128).rearrange("p (n d) -> p n d", n=NTL)
            Vt = vf32(Vt_o, NTL * 128).rearrange("p (n d) -> p n d", n=NTL)
            ATa = vf32(AT_o, NTL * 128).rearrange("p (n d) -> p n d", n=NTL)
            wTa = vf32(wT_o, NTL * 128).rearrange("p (n d) -> p n d", n=NTL)
            ua = vf32(u_o, NTL * 128).rearrange("p (n d) -> p n d", n=NTL)
            oa = vf32(o_o, NTL * 128).rearrange("p (n d) -> p n d", n=NTL)
            sm = {k: vf32(o, NTL * 8).rearrange("p (n h) -> p n h", n=NTL) for k, o in sm_o.items()}
            ab = vf32(ab_o, NTL * 16).rearrange("p (n c) -> p n c", n=NTL)
            wab = vbf(wab_o, 256).rearrange("p (k c) -> p k c", k=16)
            ssv = vf32(ss_o, 64)

            def h2load(tt, slot):
                S.dma("sp", h2t[slot], H2v[:, :, tt * 512:(tt + 1) * 512], reads=["H2s"], writes=["h2t%d" % slot])

            sv = vf32(stg_off[0], 256).rearrange("p (k c) -> p k c", k=16)
            S.dma("sp", sv, win_d[:, 4096:4112].rearrange("(k p) c -> p k c", p=128), writes=["stg0"])
            CP("dve", wab, sv, r=["stg0"], w=["wab"])
            for tt in range(NT):
                h2load(tt, tt % 2)
                for sub in range(4):
                    tl = tt * 4 + sub
                    for kc in range(16):
                        MM(PS[0][:, tl * 16:(tl + 1) * 16], h2t[tt % 2][:, kc, sub * 128:(sub + 1) * 128], wab[:, kc, :],
                           start=(kc == 0), stop=(kc == 15), r=["h2t%d" % (tt % 2), "wab"], w=PK(0, 0, 256))
            CP("dve", ab.rearrange("p n c -> p (n c)"), PS[0][:, 0:NTL * 16], r=PK(0, 0, 256), w=["ab"])
            flat = lambda k: sm[k].rearrange("p n h -> p (n h)")
            alog = vec[:, V_ALOG:V_ALOG + 128].rearrange("p (n h) -> p n h", n=16)[:, 0:NTL, :]
            dtb = vec[:, V_DTB:V_DTB + 128].rearrange("p (n h) -> p n h", n=16)[:, 0:NTL, :]
            TT("dve", sm["t1"], ab[:, :, 0:8], dtb, ALU.add, r=["ab", "vec"], w=["t1"])
            ACT(flat("t1"), flat("t1"), AF.Exp, r=["t1"], w=["t1"])
            ACT(flat("t1"), flat("t1"), AF.Ln, r=["t1"], w=["t1"], bias=1.0)
            ACT(sm["t2"], alog, AF.Exp, r=["vec"], w=["t2"])
            STT("dve", flat("gcol"), flat("t1"), -1.0, flat("t2"), ALU.mult, ALU.mult, r=["t1", "t2"], w=["gcol"])
            ACT(sm["beta"], ab[:, :, 8:16], AF.Sigmoid, r=["ab"], w=["beta"])
            TS("dve", flat("nbeta"), flat("beta"), -1.0, None, ALU.mult, r=["beta"], w=["nbeta"])
            for tl in range(NTL):
                MM(PS[1][:, tl * 8:(tl + 1) * 8], triu_f, sm["gcol"][:, tl, :], r=["cst", "gcol"], w=PK(1, 0, 128))
                MM(PS[2][:, tl * 8:(tl + 1) * 8], ones_f, sm["gcol"][:, tl, :], r=["cst", "gcol"], w=PK(2, 0, 128))
            CP("dve", flat("Gc"), PS[1][:, 0:NTL * 8], r=PK(1, 0, 128), w=["Gc"])
            CP("dve", flat("gl"), PS[2][:, 0:NTL * 8], r=PK(2, 0, 128), w=["gl"])
            ACT(flat("eG"), flat("Gc"), AF.Exp, r=["Gc"], w=["eG"])
            ACT(flat("egl"), flat("gl"), AF.Exp, r=["gl"], w=["egl"])
            TT("dve", flat("t1"), flat("gl"), flat("Gc"), ALU.subtract, r=["gl", "Gc", "t1"], w=["t1"])
            ACT(flat("edec"), flat("t1"), AF.Exp, r=["t1"], w=["edec"])
            TT("dve", flat("bEG"), flat("beta"), flat("eG"), ALU.mult, r=["beta", "eG"], w=["bEG"])

            if lvl < 1:
                ring["nstg"], ring["nwb"] = NSTG, NWB
                return

            def inproj(cols, nchunk):
                blks = [wblock(win_d[:, c0:c0 + 128], 16, 128, engs=("act",)) for c0 in cols]
                for tt in range(NT):
                    h2load(tt, tt % 2)
                    for ci in range(nchunk):
                        wv, wk = blks[ci]
                        for kc in range(16):
                            MM(PS[ci][:, :], wv[:, kc, :], h2t[tt % 2][:, kc, :], start=(kc == 0), stop=(kc == 15),
                               r=[wk, "h2t%d" % (tt % 2)], w=PK(ci))
                        CP("act" if ci % 2 else "dve", raw[ci][:, tt * 512:(tt + 1) * 512], PS[ci][:, :],
                           r=PK(ci), w=["raw%d" % ci])

            def to_tokmajor(src, sk, dstflat, dk_):
                for g4 in range(NTL // 4):
                    b = 6 + g4 % 2
                    for c in range(4):
                        tl = g4 * 4 + c
                        TR(PS[b][:, c * 128:(c + 1) * 128], src[:, tl * 128:(tl + 1) * 128], r=["cst", sk],
                           w=PK(b, c * 128, (c + 1) * 128))
                    CP("act" if g4 % 2 else "dve", dstflat(g4), PS[b][:, :], r=PK(b), w=[dk_])

            def gdn_head(h):
                inproj([h * 128, 1024 + h * 128, 2048 + h * 128, 3072 + h * 128], 4)
                for ci in range(3):
                    eng = "dve"
                    cw0 = V_CONV + (ci * 8 + h) * 4
                    x = raw[ci]
                    rk = "raw%d" % ci
                    TS(eng, cbuf[:, :], x[:, :], vec[:, cw0 + 3:cw0 + 4], None, ALU.mult, r=[rk, "vec"], w=["cbuf"])
                    for sft in (1, 2, 3):
                        STT(eng, cbuf[:, sft:T], x[:, 0:T - sft], vec[:, cw0 + 3 - sft:cw0 + 4 - sft], cbuf[:, sft:T],
                            ALU.mult, ALU.add, r=[rk, "vec", "cbuf"], w=["cbuf"])
                    ACT(x[:, :], cbuf[:, :], AF.Silu, r=["cbuf"], w=[rk])
                    if ci < 2:
                        ACT(cbuf[:, :], x[:, :], AF.Square, r=[rk], w=["cbuf"])
                        for tt in range(NT):
                            MM(PS[tt][:, :], ones_f, cbuf[:, tt * 512:(tt + 1) * 512], r=["cst", "cbuf"], w=PK(tt))
                        for tt in range(NT):
                            ACT(cbuf[:, tt * 512:(tt + 1) * 512], PS[tt][:, :], AF.Ln, r=PK(tt), w=["cbuf"], bias=EPS)
                        ACT(cbuf[:, :], cbuf[:, :], AF.Exp, r=["cbuf"], w=["cbuf"], scale=-0.5)
                        if ci == 0:
                            STT("dve", x[:, :], x[:, :], 128.0 ** -0.5, cbuf[:, :], ALU.mult, ALU.mult, r=[rk, "cbuf"], w=[rk])
                        else:
                            TT("dve", x[:, :], x[:, :], cbuf[:, :], ALU.mult, r=[rk, "cbuf"], w=[rk])
                ACT(raw[3][:, :], raw[3][:, :], AF.Silu, r=["raw3"], w=["raw3"])
                TS("pool", raw[3][:, :], raw[3][:, :], vec[:, V_NG:V_NG + 1], None, ALU.mult, r=["raw3", "vec"], w=["raw3"])
                qT, kT, vT, gz = raw
                if lvl < 2:
                    return
                to_tokmajor(kT, "raw1", lambda g4: Kt[:, g4 * 4:(g4 + 1) * 4, :].rearrange("p n d -> p (n d)"), "Kt")
                to_tokmajor(vT, "raw2", lambda g4: Vt[:, g4 * 4:(g4 + 1) * 4, :].rearrange("p n d -> p (n d)"), "Vt")
                if lvl < 2.5:
                    return
                for g0 in range(0, NTL, G):
                    units = list(range(g0, min(NTL, g0 + G)))
                    U = {}
                    for tl in units:
                        u = tl % G
                        o = un_o[u]
                        U[tl] = dict(
                            B=vf32(o["B"], 128), dec=vf32(o["dec"], 128), decS=vf32(o["decS"], 128), decT=vf32(o["decT"], 128),
                            N=vf32(o["N"], 128), A=vf32(o["A"], 128), XY=[vf32(q, 128) for q in o["XY"]],
                            R=[vf32(q, 256) for q in o["R"]], pa=PS[2 * u][:, 0:128], pb=PS[2 * u][:, 128:256], pc=PS[2 * u + 1][:, 0:256],
                            pak=PK(2 * u, 0, 128), pbk=PK(2 * u, 128, 256), pck=PK(2 * u + 1, 0, 256),
                            k=(lambda s_, u=u: "u%d%s" % (u, s_)), tsl=slice(tl * 128, (tl + 1) * 128))
                    for tl in units:
                        d = U[tl]; k = d["k"]
                        TS("pool", d["B"], sgtj_f, sm["gcol"][:, tl, h:h + 1], None, ALU.mult, r=["cst", "gcol"], w=[k("B")])
                        MM(d["pa"], triu_f, d["B"], r=["cst", k("B")], w=d["pak"])
                        ACT(d["dec"], d["pa"], AF.Exp, r=d["pak"], w=[k("dec")])
                        if lvl < 2.6:
                            continue
                        MM(d["pa"], kT[:, d["tsl"]], kT[:, d["tsl"]], r=["raw1"], w=d["pak"])
                        MM(d["pb"], qT[:, d["tsl"]], kT[:, d["tsl"]], r=["raw0", "raw1"], w=d["pbk"])
                        TT("pool", d["decS"], d["dec"], strict_f, ALU.mult, r=[k("dec"), "cst"], w=[k("decS")])
                        TT("pool", d["decT"], d["dec"], tril_f, ALU.mult, r=[k("dec"), "cst"], w=[k("decT")])
                        STT("dve", d["N"], d["pa"], sm["nbeta"][:, tl, h:h + 1], d["decS"], ALU.mult, ALU.mult,
                            r=d["pak"] + ["nbeta", k("decS")], w=[k("N")])
                        TT("dve", d["A"], d["pb"], d["decT"], ALU.mult, r=d["pbk"] + [k("decT")], w=[k("A")])
                        if lvl < 2.7:
                            continue
                        TR(d["pa"], d["N"], r=["cst", k("N")], w=d["pak"])
                        TR(d["pb"], d["A"], r=["cst", k("A")], w=d["pbk"])
                        CP("act", d["XY"][1], d["pa"], r=d["pak"], w=[k("XY1")])
                        CP("act", ATa[:, tl, :], d["pb"], r=d["pbk"], w=["ATa"])
                        if lvl < 2.78:
                            continue
                        TS("pool", d["R"][0][:, 0:128], Vt[:, tl, :], sm["beta"][:, tl, h:h + 1], None, ALU.mult,
                           r=["Vt", "beta"], w=[k("R0")])
                        TS("pool", d["R"][0][:, 128:256], Kt[:, tl, :], sm["bEG"][:, tl, h:h + 1], None, ALU.mult,
                           r=["Kt", "bEG", k("R0")], w=[k("R0")])
                    if lvl < 2.8:
                        continue
                    nlv = 7 if lvl >= 3 else int(round((lvl - 2.8) * 100))
                    for lv in range(nlv):
                        for tl in units:
                            d = U[tl]; k = d["k"]
                            if lv == 0:
                                X, Xk, Y, Yk = d["N"], k("N"), d["XY"][1], k("XY1")
                            else:
                                xi, yi = (0, 1) if lv % 2 == 0 else (2, 3)
                                X, Xk, Y, Yk = d["XY"][xi], k("XY%d" % xi), d["XY"][yi], k("XY%d" % yi)
                            Rc, Rck = d["R"][lv % 2], k("R%d" % (lv % 2))
                            cv = "rs"
                            if "r" in cv:
                                MM(d["pc"], Y, Rc, r=[Yk, Rck], w=d["pck"])
                            if lv < 6 or nlv < 7:
                                Rn, Rnk = d["R"][(lv + 1) % 2], k("R%d" % ((lv + 1) % 2))
                                if "r" in cv:
                                    TT("dve", Rn, Rc, d["pc"], ALU.add, r=[Rck] + d["pck"], w=[Rnk])
                                nxi, nyi = (0, 1) if (lv + 1) % 2 == 0 else (2, 3)
                                if "s" in cv:
                                    MM(d["pa"], Y, X, r=[Yk, Xk], w=d["pak"])
                                    MM(d["pb"], X, Y, r=[Yk, Xk], w=d["pbk"])
                                    CP("dve", d["XY"][nxi], d["pa"], r=d["pak"], w=[k("XY%d" % nxi)])
                                    CP("dve", d["XY"][nyi], d["pb"], r=d["pbk"], w=[k("XY%d" % nyi)])
                            else:
                                TT("dve", ua[:, tl, :], Rc[:, 0:128], d["pc"][:, 0:128], ALU.add, r=[Rck] + d["pck"], w=["ua"])
                                TT("dve", d["A"], Rc[:, 128:256], d["pc"][:, 128:256], ALU.add, r=[Rck] + d["pck"], w=[k("A")])
                                TR(d["pa"], d["A"], r=["cst", k("A")], w=d["pak"])
                                CP("act", wTa[:, tl, :], d["pa"], r=d["pak"], w=["wTa"])
                if lvl < 4:
                    return
                Sst = [vf32(o, 128) for o in S_o]
                MEMSET("pool", Sst[0], 0.0, w=["S0"])
                for tl in range(NTL):
                    cur, nxt = tl % 2, (tl + 1) % 2
                    b = 4 + tl % 2
                    pb_ = PS[b]
                    tsl = slice(tl * 128, (tl + 1) * 128)
                    vn, kd, ot = vf32(vn_o[tl % 2], 128), vf32(kd_o[tl % 2], 128), vf32(ot_o[tl % 2], 128)
                    vk, kk, ok_ = "vn%d" % (tl % 2), "kd%d" % (tl % 2), "ot%d" % (tl % 2)
                    MM(pb_[:, 0:128], wTa[:, tl, :], Sst[cur], r=["wTa", "S%d" % cur], w=PK(b, 0, 128))
                    TT("dve", vn, ua[:, tl, :], pb_[:, 0:128], ALU.subtract, r=["ua"] + PK(b, 0, 128), w=[vk])
                    MM(pb_[:, 128:256], qT[:, tsl], Sst[cur], r=["raw0", "S%d" % cur], w=PK(b, 128, 256))
                    MM(pb_[:, 256:384], ATa[:, tl, :], vn, r=["ATa", vk], w=PK(b, 256, 384))
                    TS("dve", ot, pb_[:, 128:256], sm["eG"][:, tl, h:h + 1], None, ALU.mult, r=PK(b, 128, 256) + ["eG"], w=[ok_])
                    TT("dve", oa[:, tl, :], ot, pb_[:, 256:384], ALU.add, r=[ok_] + PK(b, 256, 384), w=["raw2"])
                    TS("pool", kd, Kt[:, tl, :], sm["edec"][:, tl, h:h + 1], None, ALU.mult, r=["Kt", "edec"], w=[kk])
                    MM(pb_[:, 384:512], kd, vn, r=[kk, vk], w=PK(b, 384, 512))
                    STT("dve", Sst[nxt], Sst[cur], sm["egl"][:, tl, h:h + 1], pb_[:, 384:512], ALU.mult, ALU.add,
                        r=["S%d" % cur, "egl"] + PK(b, 384, 512), w=["S%d" % nxt])
                if lvl < 5:
                    return
                yh = vbf(yh_o[h % 2], T)
                yk = "yh"
                ACT(cbuf[:, :], oa.rearrange("p n d -> p (n d)"), AF.Square, r=["raw2"], w=["cbuf"])
                S.add("dve", lambda e: e.tensor_reduce(ssv[:, 0:NTL], cbuf[:, :].rearrange("p (n d) -> p n d", n=NTL), AX.X, ALU.add),
                      ["cbuf"], ["ssv"])
                ACT(ssv[:, 0:NTL], ssv[:, 0:NTL], AF.Ln, r=["ssv"], w=["ssv"], scale=1.0 / 128, bias=EPS)
                ACT(ssv[:, 0:NTL], ssv[:, 0:NTL], AF.Exp, r=["ssv"], w=["ssv"], scale=-0.5)
                for tl in range(NTL):
                    on = vf32(on_o[tl % 2], 128)
                    onk = "on%d" % (tl % 2)
                    b = 6 + tl % 2
                    TS("pool", on, oa[:, tl, :], ssv[:, tl:tl + 1], None, ALU.mult, r=["raw2", "ssv"], w=[onk])
                    TR(PS[b][:, 0:128], on, r=["cst", onk], w=PK(b, 0, 128))
                    TT("dve", yh[:, tl * 128:(tl + 1) * 128], PS[b][:, 0:128], gz[:, tl * 128:(tl + 1) * 128], ALU.mult,
                       r=PK(b, 0, 128) + ["raw3"], w=[yk])
                S.dma("sp", Y_d[h * 128:(h + 1) * 128, :], yh, reads=[yk], writes=["Ys"])

            def moba_head(m):
                inproj([4112 + m * 128, 5136 + m * 128, 6160 + m * 128], 3)
                rq, rk_, rv = raw[0], raw[1], raw[2]
                cosv = cbuf[0:32, 0:T]
                sinv = vf32(Kt_o, T)[0:32, :]
                pmf = cst[0:32, C_PM:C_PM + 32]
                for ci in range(2):
                    x = raw[ci]
                    xk = "raw%d" % ci
                    for tt in range(NT):
                        cs = slice(tt * 512, (tt + 1) * 512)
                        b = 4 + tt % 2
                        t1 = vf32(rt_o[0], 512)
                        t2 = vf32(rt_o[1], 512)
                        MM(PS[b][0:32, :], pmf, x[0:32, cs], r=["cst", xk], w=PK(b))
                        TT("dve", t1[0:32, :], x[0:32, cs], cosv[:, cs], ALU.mult, r=[xk, "cbuf"], w=["Vt"])
                        TT("dve", t2[0:32, :], PS[b][0:32, :], sinv[:, cs], ALU.mult, r=PK(b) + ["Kt", "Vt"], w=["Vt"])
                        TT("pool", x[0:32, cs], t1[0:32, :], t2[0:32, :], ALU.add, r=["Vt", xk], w=[xk])
                km = vf32(km_o, 8)
                S.add("dve", lambda e: e.tensor_reduce(km[:, 0:NB], rk_[:, :].rearrange("p (n t) -> p n t", n=NB), AX.X, ALU.add),
                      ["raw1"], ["km"])
                TS("dve", km[:, 0:NB], km[:, 0:NB], 1.0 / 256, None, ALU.mult, r=["km"], w=["km"])
                gate = vf32(gate_o, NTL * 8).rearrange("p (n b) -> p n b", n=NTL)
                seln = vf32(seln_o, NTL * 8).rearrange("p (n b) -> p n b", n=NTL)
                for tl in range(NTL):
                    MM(PS[5][:, tl * 8:tl * 8 + NB], rq[:, tl * 128:(tl + 1) * 128], km[:, 0:NB], r=["raw0", "km"], w=PK(5, 0, 128))
                CP("dve", gate[:, :, 0:NB], PS[5][:, 0:NTL * 8].rearrange("p (n b) -> p n b", b=8)[:, :, 0:NB], r=PK(5, 0, 128), w=["gate"])
                MEMSET("pool", seln.rearrange("p n b -> p (n b)"), 0.0, w=["seln"])
                gt = [vf32(o, 8) for o in gtmp_o]
                for tl in range(NTL):
                    own = tl // 2
                    if own <= 3:
                        continue
                    g = gate[:, tl, 0:own]
                    cur = g
                    ck = "gate"
                    for it in range(3):
                        S.add("dve", lambda e, cur=cur, it=it: e.tensor_reduce(gt[3][:, it:it + 1], cur, AX.X, ALU.max),
                              [ck, "gt3"], ["gt3"])
                        if it < 2:
                            TS("dve", gt[2][:, 0:own], cur, gt[3][:, it:it + 1], -1e30, ALU.is_ge, ALU.mult, r=[ck, "gt3"], w=["gt2"])
                            TT("dve", gt[it][:, 0:own], cur, gt[2][:, 0:own], ALU.add, r=[ck, "gt2"], w=["gt%d" % it])
                            cur = gt[it][:, 0:own]
                            ck = "gt%d" % it
                    TS("dve", seln[:, tl, 0:own], g, gt[3][:, 2:3], NEG, ALU.is_lt, ALU.mult, r=["gate", "gt3", "seln"], w=["seln"])
                selT = vbf(selT_o, T)
                for g4 in range(NTL // 4):
                    b = 6 + g4 % 2
                    for c in range(4):
                        tl = g4 * 4 + c
                        TR(PS[b][0:8, c * 128:(c + 1) * 128], seln[:, tl, :], r=["cst", "seln"], w=PK(b, c * 128, (c + 1) * 128))
                    CP("dve", selT[0:8, g4 * 512:(g4 + 1) * 512], PS[b][0:8, :], r=PK(b), w=["Vt"])
                qb, kb = vbf(qb_o, T), vbf(kb_o, T)
                CP("act", qb, rq[:, :], r=["raw0"], w=["ATa"])
                CP("pool", kb, rk_[:, :], r=["raw1"], w=["wTa"])
                vtk = vbf(vtk_o, NTL * 128).rearrange("p (n d) -> p n d", n=NTL)
                to_tokmajor(rv, "raw2", lambda g4: vtk[:, g4 * 4:(g4 + 1) * 4, :].rearrange("p n d -> p (n d)"), "ua")
                yh = vbf(yh_o[m % 2], T)
                yk = "yh"
                eb = cstb[0:8, C_EBLK:C_EBLK + 1024].rearrange("p (n k) -> p n k", n=8)
                items = []
                for i in range(NTL):
                    own = i // 2
                    keys = [(kt, "sel") for kt in range(2 * own)]
                    if i % 2:
                        keys.append((i - 1, "full"))
                    keys.append((i, "diag"))
                    for n_, (kt, kind) in enumerate(keys):
                        items.append((i, kt, kind, n_ == 0, n_ == len(keys) - 1))
                LA = 2
                ppb = (0, 1, 4)

                def stage12(idx):
                    i, kt, kind, first, last = items[idx]
                    qs = slice(i * 128, (i + 1) * 128)
                    bk = ppb[idx % 3]
                    pp = PS[bk][:, 0:128]
                    ptb, ptk = vbf(pt_o[idx % 4], 128), "pt%d" % (idx % 4)
                    MM(pp, kb[:, kt * 128:(kt + 1) * 128], qb[:, qs], start=True, stop=(kind != "sel"),
                       r=["wTa", "ATa"], w=PK(bk))
                    if kind == "sel":
                        MM(pp, eb[:, kt // 2, :], selT[0:8, qs], start=False, stop=True, r=["cstb", "Vt"], w=PK(bk))
                    ACT(ptb, pp, AF.Exp, r=PK(bk), w=[ptk], scale=128.0 ** -0.5)
                    if kind == "diag":
                        TT("pool", ptb, ptb, triu_b, ALU.mult, r=[ptk, "cstb"], w=[ptk])

                def stage3(idx):
                    i, kt, kind, first, last = items[idx]
                    qs = slice(i * 128, (i + 1) * 128)
                    ptb, ptk = vbf(pt_o[idx % 4], 128), "pt%d" % (idx % 4)
                    bo, bl = (2, 3) if i % 2 == 0 else (6, 7)
                    po, pl = PS[bo][:, 0:128], PS[bl][:, 0:128]
                    MM(po, vtk[:, kt, :], ptb, start=first, stop=last, r=["ua", ptk], w=PK(bo))
                    MM(pl, ones_b, ptb, start=first, stop=last, r=["cstb", ptk], w=PK(bl))
                    if last:
                        rl = vf32(rl_o[i % 2], 128)
                        rlk = "rl%d" % (i % 2)
                        S.add("dve", lambda e, rl=rl, pl=pl: e.reciprocal(rl, pl), PK(bl), [rlk])
                        TT("dve", yh[:, qs], po, rl, ALU.mult, r=PK(bo) + [rlk], w=[yk])

                for idx in range(len(items) + LA):
                    if idx < len(items):
                        stage12(idx)
                    if idx - LA >= 0:
                        stage3(idx - LA)
                S.dma("sp", Y_d[1024 + m * 128:1024 + (m + 1) * 128, :], yh, reads=[yk], writes=["Ys"])

            for h in range(ngh):
                if gdn_on:
                    gdn_head(h)
            if moba_on:
                S.dma("sp", cbuf[0:32, 0:T], rope_d[:, 0:T], writes=["cbuf"])
                S.dma("sp", vf32(Kt_o, T)[0:32, :], rope_d[:, T:2 * T], writes=["Kt"])
                for m in range(nmh):
                    moba_head(m)
            ring["nstg"], ring["nwb"] = NSTG, NWB

        if 0 in phases:
            phase0()
        if 1 in phases:
            phase1()
        if 2 in phases:
            S.barrier()
            phase2()
            S.barrier()
        if 3 in phases:
            phase3()
        S.finalize(st)
        S.run_block()
    return nc


def host_consts(T):
    c = np.zeros((128, NCONST), np.float32)
    i = np.arange(128)
    c[:, C_ID:C_ID + 128] = np.eye(128)
    c[:, C_ONE:C_ONE + 128] = 1.0
    c[:, C_TRIU:C_TRIU + 128] = (i[:, None] <= i[None, :])
    c[:, C_SGTJ:C_SGTJ + 128] = (i[:, None] > i[None, :])
    c[:, C_TRIL:C_TRIL + 128] = (i[:, None] >= i[None, :])
    c[:, C_STRICT:C_STRICT + 128] = (i[:, None] > i[None, :])
    for m in range(16):
        c[m + 16, C_PM + m] = -1.0
        c[m, C_PM + m + 16] = 1.0
    for n in range(8):
        c[n, C_EBLK + n * 128:C_EBLK + (n + 1) * 128] = 1.0
    inv = 500000.0 ** (-np.arange(0, 32, 2, dtype=np.float32) / 32)
    ang = np.arange(T, dtype=np.float32)[None, :] * inv[:, None].astype(np.float32)
    rope = np.zeros((32, 2 * T), np.float32)
    rope[0:16, 0:T] = np.cos(ang)
    rope[16:32, 0:T] = np.cos(ang)
    rope[0:16, T:] = np.sin(ang)
    rope[16:32, T:] = np.sin(ang)
    return c, rope


def col16(v):
    return np.ascontiguousarray(np.asarray(v, np.float32).reshape(-1, 128).T)


def host_vecs(inp, b, parity=0):
    v = np.zeros((128, NVEC), np.float32)
    v[:, V_C:V_C + 16] = col16(inp["c"][b])
    v[:, V_BADA:V_BADA + 144] = col16(inp["b_ada"][0])
    for i, nm in enumerate(("ffn1_pre_g", "ffn1_post_g", "mix_pre_g", "mix_post_g", "ffn2_pre_g", "ffn2_post_g")):
        v[:, V_G + i * 16:V_G + (i + 1) * 16] = col16(inp[nm][0])
    cw = np.asarray(inp["gdn_conv_w"][0], np.float32)
    v[:, V_CONV:V_CONV + 96] = cw.T.reshape(24, 128, 4).transpose(1, 0, 2).reshape(128, 96)
    v[:, V_ALOG:V_ALOG + 128] = np.tile(np.asarray(inp["gdn_a_log"][0], np.float32), 16)[None, :]
    v[:, V_DTB:V_DTB + 128] = np.tile(np.asarray(inp["gdn_dt_bias"][0], np.float32), 16)[None, :]
    v[:, V_NG] = np.asarray(inp["gdn_norm_g"][0], np.float32)
    v[:, V_PAR + parity] = 1.0
    return v


_NC_CACHE = {}


def kernel(**inp):
    inp = {k: np.asarray(v) for k, v in inp.items()}
    B, T, _ = inp["x"].shape
    if T not in _NC_CACHE:
        _NC_CACHE[T] = build_nc(T)
    nc = _NC_CACHE[T]
    consts, rope = host_consts(T)
    shared = dict(consts=consts, rope=rope, w_ada=np.ascontiguousarray(inp["w_ada"][0]),
                  w1g=np.ascontiguousarray(inp["ffn1_w_gate"][0]), w1u=np.ascontiguousarray(inp["ffn1_w_up"][0]),
                  w1d=np.ascontiguousarray(inp["ffn1_w_down"][0]), w2g=np.ascontiguousarray(inp["ffn2_w_gate"][0]),
                  w2u=np.ascontiguousarray(inp["ffn2_w_up"][0]), w2d=np.ascontiguousarray(inp["ffn2_w_down"][0]),
                  w_in=np.ascontiguousarray(inp["w_in"][0]), w_out=np.ascontiguousarray(inp["w_out"][0]))
    in_maps = []
    for core in range(8):
        b = core // 2
        m = dict(shared)
        m["x"] = np.ascontiguousarray(inp["x"][b])
        m["vecs"] = host_vecs(inp, b, core % 2)
        in_maps.append(m)
    res = run_bass_kernel_spmd(nc, in_maps, core_ids=list(range(8)))
    if T >= 2048:
        out = np.stack([np.concatenate([np.asarray(res.results[2 * b + p]["out"], np.float32) for p in range(2)], axis=0)
                        for b in range(B)], axis=0)
    else:
        out = np.stack([np.asarray(res.results[2 * b]["out"], np.float32) for b in range(B)], axis=0)
    return out
```

```python
import contextlib
import numpy as np
import concourse.bass as bass
import concourse.mybir as mybir
from concourse.bass_utils import run_bass_kernel_spmd

F32 = mybir.dt.float32
BF16 = mybir.dt.bfloat16
AF = mybir.ActivationFunctionType
ALU = mybir.AluOpType
AX = mybir.AxisListType

ENGS = ("pe", "act", "dve", "pool", "sp")

D = 2048
KC = 16
FF = 5632
FC = 44
NMOD = 9
INC = 7184
EPS = 1e-6
NEG = -30000.0


class Sched:
    def __init__(self, nc, n_dma_sems=48):
        self.nc = nc
        self.ops = []
        self.last_w = {}
        self.readers = {}
        self.n_dma_sems = n_dma_sems

    def add(self, eng, emit, reads=(), writes=(), dma=False):
        idx = len(self.ops)
        deps = set()
        for b in reads:
            if b in self.last_w:
                deps.add(self.last_w[b])
        for b in writes:
            if b in self.last_w:
                deps.add(self.last_w[b])
            rd = self.readers.get(b)
            if rd:
                deps.update(rd.values())
        deps.discard(idx)
        self.ops.append(dict(eng=eng, emit=emit, deps=deps, dma=dma, barrier=False))
        for b in reads:
            self.readers.setdefault(b, {})[("dma", idx) if dma else eng] = idx
        for b in writes:
            self.last_w[b] = idx
            self.readers[b] = {}
        return idx

    def barrier(self):
        self.ops.append(dict(eng=None, emit=None, deps=set(), dma=False, barrier=True))
        self.last_w = {}
        self.readers = {}

    def dma(self, q, out, in_, reads=(), writes=()):
        return self.add(q, lambda e: e.dma_start(out=out, in_=in_), reads, writes, dma=True)

    def finalize(self, stack):
        nc = self.nc
        ops = self.ops
        dma_count = 0
        for op in ops:
            op["signal"] = False
            if op["dma"]:
                op["dma_ord"] = dma_count
                dma_count += 1
        last_c = {}
        for i, op in enumerate(ops):
            if op["barrier"]:
                for e, j in last_c.items():
                    ops[j]["signal"] = True
                op["last_c"] = dict(last_c)
                continue
            for d in op["deps"]:
                od = ops[d]
                if od["dma"]:
                    continue
                if not (od["eng"] == "pe" and op["eng"] == "pe"):
                    od["signal"] = True
            if not op["dma"]:
                last_c[op["eng"]] = i
        sigcount = {e: 0 for e in ENGS}
        for op in ops:
            if op["barrier"] or op["dma"]:
                continue
            if op["signal"]:
                sigcount[op["eng"]] += 1
                op["sigval"] = sigcount[op["eng"]]
        esem = {e: stack.enter_context(nc.semaphore("s_" + e)) for e in ENGS}
        nd = min(self.n_dma_sems, max(1, dma_count))
        dsem = [stack.enter_context(nc.semaphore("d%d" % i)) for i in range(nd)]
        for op in ops:
            if op["dma"]:
                k = op["dma_ord"]
                op["dsem"] = dsem[k % nd]
                op["dval"] = 16 * (k // nd + 1)
                op["dslot"] = k % nd
        waited = {e: {} for e in ENGS}
        per_eng = {e: [] for e in ENGS}
        last_on_slot = {}
        pending = {e: [] for e in ENGS}
        for i, op in enumerate(ops):
            if op["barrier"]:
                for e in ENGS:
                    lst = []
                    for e2, j in op["last_c"].items():
                        lst.append((("e", e2), esem[e2], ops[j]["sigval"]))
                    for slot, o in last_on_slot.items():
                        lst.append((("d", slot), o["dsem"], o["dval"]))
                    pending[e] = lst
                continue
            e = op["eng"]
            waits = []

            def need(key, sem, val):
                if waited[e].get(key, 0) < val:
                    waited[e][key] = val
                    waits.append((sem, val))

            for key, sem, val in pending[e]:
                need(key, sem, val)
            pending[e] = []
            for d in sorted(op["deps"]):
                od = ops[d]
                if od["dma"]:
                    need(("d", od["dslot"]), od["dsem"], od["dval"])
                elif not (od["eng"] == "pe" and e == "pe"):
                    need(("e", od["eng"]), esem[od["eng"]], od["sigval"])
            if op["dma"]:
                prev = last_on_slot.get(op["dslot"])
                if prev is not None:
                    need(("d", op["dslot"]), prev["dsem"], prev["dval"])
                last_on_slot[op["dslot"]] = op
            op["waits"] = waits
            per_eng[e].append(op)
        self.esem = esem
        self.per_eng = per_eng
        self.final_dma_waits = [(o["dsem"], o["dval"]) for o in last_on_slot.values()]

    def emit_engine(self, e, eng, final=False):
        for op in self.per_eng[e]:
            for sem, val in op["waits"]:
                eng.wait_ge(sem, val)
            ins = op["emit"](eng)
            if op["dma"]:
                ins.then_inc(op["dsem"], 16)
            elif op["signal"]:
                ins.then_inc(self.esem[e], 1)
        if final:
            for sem, val in self.final_dma_waits:
                eng.wait_ge(sem, val)

    def run_block(self):
        nc = self.nc
        with nc.Block() as block:
            @block.tensor
            def _(eng):
                self.emit_engine("pe", eng)

            @block.scalar
            def _(eng):
                self.emit_engine("act", eng)

            @block.vector
            def _(eng):
                self.emit_engine("dve", eng)

            @block.gpsimd
            def _(eng):
                self.emit_engine("pool", eng)

            @block.sync
            def _(eng):
                self.emit_engine("sp", eng, final=True)


C_ID, C_ONE, C_TRIU, C_SGTJ, C_TRIL, C_STRICT, C_PM, C_EBLK = 0, 128, 256, 384, 512, 640, 768, 800
NCONST = 800 + 1024
V_C, V_BADA, V_G = 0, 16, 160
V_CONV = 256
V_ALOG = 352
V_DTB = 480
V_NG = 608
V_PAR = 609
NVEC = 611


def build_nc(T, phases=(0, 1, 2, 3), gdn_on=True, moba_on=True, lvl=99, ngh=8, nmh=8):
    NT = T // 512
    NTL = T // 128
    NB = T // 256
    nc = bass.Bass("TRN2", target_bir_lowering=False)
    dr = lambda name, shape, dt=F32, kind="ExternalInput": nc.dram_tensor(name, shape, dt, kind=kind).ap()
    x_d = dr("x", [T, D])
    consts_d = dr("consts", [128, NCONST])
    vecs_d = dr("vecs", [128, NVEC])
    rope_d = dr("rope", [32, 2 * T])
    wada_d = dr("w_ada", [D, NMOD * D])
    w1g_d, w1u_d, w1d_d = dr("w1g", [D, FF]), dr("w1u", [D, FF]), dr("w1d", [FF, D])
    w2g_d, w2u_d, w2d_d = dr("w2g", [D, FF]), dr("w2u", [D, FF]), dr("w2d", [FF, D])
    win_d = dr("w_in", [D, INC])
    wout_d = dr("w_out", [D, D])
    split = T >= 2048
    TO = 1024 if split else T
    out_d = dr("out", [TO, D], kind="ExternalOutput")
    X1_d = dr("X1s", [D, T], F32, "Internal")
    X0_d = dr("X0s", [D, T], F32, "Internal")
    YF_d = dr("YFs", [D, 1024], F32, "Internal")
    H2_d = dr("H2s", [D, T], BF16, "Internal")
    Y_d = dr("Ys", [D, T], BF16, "Internal")

    st = contextlib.ExitStack()
    with st:
        S = Sched(nc)
        arena = st.enter_context(nc.sbuf_tensor("arena", [128, 96 * 1024], BF16))
        cst = st.enter_context(nc.sbuf_tensor("cst", [128, NCONST], F32))
        cstb = st.enter_context(nc.sbuf_tensor("cstb", [128, NCONST], BF16))
        vec = st.enter_context(nc.sbuf_tensor("vec", [128, NVEC], F32))
        modv = st.enter_context(nc.sbuf_tensor("modv", [128, 144 + 9 * 16], F32))
        PS = [st.enter_context(nc.psum_tensor("ps%d" % i, [128, 512], F32)) for i in range(8)]

        apos = [0]

        def alloc(nbytes):
            o = apos[0]
            apos[0] += (nbytes + 63) // 64 * 32
            assert apos[0] <= 96 * 1024, apos[0]
            return o

        def vf32(off, n):
            return arena[:, off:off + 2 * n].bitcast(F32)

        def vbf(off, n):
            return arena[:, off:off + n]

        def MM(out, lhsT, rhs, start=True, stop=True, r=(), w=()):
            S.add("pe", lambda e: e.matmul(out, lhsT, rhs, start=start, stop=stop), r, w)

        def TR(out, in_, r=(), w=()):
            S.add("pe", lambda e: e.transpose(out, in_, cst[:, C_ID:C_ID + 128]), r, w)

        def ACT(out, in_, func, r=(), w=(), **kw):
            S.add("act", lambda e: e.activation(out, in_, func, **kw), r, w)

        def CP(eng, out, in_, r=(), w=()):
            if eng == "act":
                S.add("act", lambda e: e.copy(out, in_), r, w)
            else:
                S.add(eng, lambda e: e.tensor_copy(out, in_), r, w)

        def TS(eng, out, in0, s1, s2, op0, op1=None, r=(), w=()):
            if op1 is None:
                S.add(eng, lambda e: e.tensor_single_scalar(out, in0, s1, op0), r, w)
            else:
                S.add(eng, lambda e: e.tensor_scalar(out, in0, s1, s2, op0, op1), r, w)

        def STT(eng, out, in0, sc, in1, op0, op1, r=(), w=()):
            S.add(eng, lambda e: e.scalar_tensor_tensor(out, in0, sc, in1, op0, op1), r, w)

        def TT(eng, out, in0, in1, op, r=(), w=()):
            S.add(eng, lambda e: e.tensor_tensor(out, in0, in1, op), r, w)

        def MEMSET(eng, ap, val, r=(), w=()):
            S.add(eng, lambda e: e.memset(ap, val), r, w)

        def PK(b, lo=0, hi=512):
            return ["ps%d" % b]

        ident = cst[:, C_ID:C_ID + 128]
        ones_f = cst[:, C_ONE:C_ONE + 128]
        triu_f = cst[:, C_TRIU:C_TRIU + 128]
        sgtj_f = cst[:, C_SGTJ:C_SGTJ + 128]
        tril_f = cst[:, C_TRIL:C_TRIL + 128]
        strict_f = cst[:, C_STRICT:C_STRICT + 128]
        ones_b = cstb[:, C_ONE:C_ONE + 128]
        triu_b = cstb[:, C_TRIU:C_TRIU + 128]

        S.dma("sp", cst[:], consts_d, writes=["cst"])
        S.dma("sp", vec[:], vecs_d, writes=["vec"])
        CP("dve", cstb[:], cst[:], r=["cst"], w=["cstb"])

        NSTG, NWB = 2, 8
        stg_off = [alloc(8192), alloc(8192)]
        wb_off = [alloc(4096) for _ in range(8)]
        ring = dict(s=0, w=0, c=0, nstg=NSTG, nwb=NWB)
        cast_engs = ("act", "dve")

        def wblock(src, nk, ncols, engs=cast_engs):
            si = ring["s"] % ring["nstg"]
            wi = ring["w"] % ring["nwb"]
            ring["s"] += 1
            ring["w"] += 1
            sv = vf32(stg_off[si], nk * ncols).rearrange("p (k c) -> p k c", k=nk)
            wv = vbf(wb_off[wi], nk * ncols).rearrange("p (k c) -> p k c", k=nk)
            S.dma("sp", sv, src.rearrange("(k p) c -> p k c", p=128), writes=["stg%d" % si])
            eng = engs[ring["c"] % len(engs)]
            ring["c"] += 1
            CP(eng, wv, sv, r=["stg%d" % si], w=["wb%d" % wi])
            return wv, "wb%d" % wi

        main_o = apos[0]
        hT_o = alloc(32768)
        actT_o = alloc(FC * 2048)
        cx_o = [alloc(4096), alloc(4096)]
        cy_o = [alloc(4096), alloc(4096)]
        rstd_o = alloc(4096)
        sq_o = alloc(2048)
        scb_o = alloc(64)
        row_o = cy_o

        hT = vbf(hT_o, 16 * 1024).rearrange("p (k t) -> p k t", k=16)
        actT = vbf(actT_o, FC * 1024).rearrange("p (k t) -> p k t", k=FC)
        xTf = vf32(actT_o, 16 * 1024).rearrange("p (k t) -> p k t", k=16)
        xin = [vf32(actT_o + 32768, 2048), vf32(actT_o + 32768 + 4096, 2048)]
        cx = [vf32(o, 1024) for o in cx_o]
        cy = [vf32(o, 1024) for o in cy_o]
        rstdB = vf32(rstd_o, 1024)
        sqb = vbf(sq_o, 1024)

        def mcol(i):
            return modv[:, i * 16:(i + 1) * 16]
        Sv = [modv[:, 144 + i * 16:144 + (i + 1) * 16] for i in range(3)]
        coef = [modv[:, 144 + 48 + i * 16:144 + 48 + (i + 1) * 16] for i in range(3)]
        tmpv = modv[:, 144 + 96:144 + 112]

        def gain(i):
            return vec[:, V_G + i * 16:V_G + (i + 1) * 16]

        def phase0():
            scb = vbf(scb_o, 16)
            ACT(scb, vec[:, V_C:V_C + 16], AF.Silu, r=["vec"], w=["scb"])
            pm = PS[7]
            for cb in range(72):
                blks = [wblock(wada_d[kh * 1024:(kh + 1) * 1024, cb * 256:(cb + 1) * 256], 8, 256) for kh in range(2)]
                pr = PS[cb % 2]
                prk = PK(cb % 2, 0, 256)
                for kc in range(16):
                    wv, wk = blks[kc // 8]
                    MM(pr[0:1, 0:256], scb[:, kc:kc + 1], wv[:, kc % 8, :], start=(kc == 0), stop=(kc == 15),
                       r=["scb", wk], w=prk)
                rw = vf32(row_o[cb % 2], 256)
                rk = "row%d" % (cb % 2)
                CP("dve", rw[0:1, :], pr[0:1, 0:256], r=prk, w=[rk])
                for j in range(2):
                    MM(pm[:, cb * 2 + j:cb * 2 + j + 1], rw[0:1, j * 128:(j + 1) * 128], cst[0:1, C_ONE:C_ONE + 1],
                       r=[rk, "cst"], w=PK(7, 0, 256))
            TT("dve", modv[:, 0:144], pm[:, 0:144], vec[:, V_BADA:V_BADA + 144], ALU.add, r=PK(7, 0, 256) + ["vec"], w=["modv"])
            for i in range(3):
                TS("dve", tmpv, mcol(3 * i + 1), 1.0, None, ALU.add, r=["modv"], w=["tmpv"])
                TT("dve", Sv[i], tmpv, gain(2 * i), ALU.mult, r=["tmpv", "vec"], w=["Sv%d" % i])
                STT("dve", coef[i], mcol(3 * i + 2), (1.0 if i == 1 else 0.5), gain(2 * i + 1), ALU.mult, ALU.mult,
                    r=["modv", "vec"], w=["coef%d" % i])

        HK = ["H%d" % k for k in range(16)]
        AK = ["A%d" % k for k in range(FC)]
        XFK = lambda kc: [AK[2 * kc], AK[2 * kc + 1]]
        X0v = X0_d.rearrange("(k p) t -> p k t", p=128)
        X1v = X1_d.rearrange("(k p) t -> p k t", p=128)
        H2v = H2_d.rearrange("(k p) t -> p k t", p=128)
        Yv = Y_d.rearrange("(k p) t -> p k t", p=128)
        YFv = YF_d.rearrange("(k p) t -> p k t", p=128)

        def stat_acc(src, srckeys, first, last):
            ACT(sqb, src, AF.Square, r=srckeys, w=["sq"])
            for hf in range(2):
                MM(PS[6 + hf][:, :], ones_b, sqb[:, hf * 512:(hf + 1) * 512], start=first, stop=last,
                   r=["cstb", "sq"], w=PK(6 + hf))

        def stat_fin():
            for hf in range(2):
                ACT(rstdB[:, hf * 512:(hf + 1) * 512], PS[6 + hf][:, :], AF.Ln, r=PK(6 + hf), w=["rstdB"],
                    scale=1.0 / D, bias=EPS)
            ACT(rstdB, rstdB, AF.Exp, r=["rstdB"], w=["rstdB"], scale=-0.5)

        def mod_to_hT(i, kc, src, srckeys):
            t = cy[kc % 2]
            STT("dve", t, src, Sv[i][:, kc:kc + 1], rstdB, ALU.mult, ALU.mult,
                r=srckeys + ["Sv%d" % i, "rstdB"], w=["cy%d" % (kc % 2)])
            ACT(hT[:, kc, :], t, AF.Identity, r=["cy%d" % (kc % 2), "modv"], w=[HK[kc]], bias=mcol(3 * i)[:, kc:kc + 1])

        def prenorm_dram(i, Xsrc, ts, xkey):
            stat_fin()
            for kc in range(16):
                S.dma("sp", cx[kc % 2], Xsrc[:, kc, ts], reads=[xkey], writes=["cx%d" % (kc % 2)])
                mod_to_hT(i, kc, cx[kc % 2], ["cx%d" % (kc % 2)])

        def y_sink(dm, buf, bkey):
            stat_acc(buf, [bkey], dm == 0, dm == 15)
            S.dma("sp", YFv[:, dm, :], buf, reads=[bkey], writes=["YF%d" % dm])

        def proj_T(inT, inkeys, nk, W):
            groups = [(g, min(8, nk - g)) for g in range(0, nk, 8)]
            for db in range(8):
                for (g0, n) in groups:
                    wv, wk = wblock(W[g0 * 128:(g0 + n) * 128, db * 256:(db + 1) * 256], n, 256)
                    for sub in range(2):
                        for hf in range(2):
                            bk = 2 + sub * 2 + hf
                            for f in range(n):
                                k = g0 + f
                                MM(PS[bk][:, :], wv[:, f, sub * 128:(sub + 1) * 128], inT[:, k, hf * 512:(hf + 1) * 512],
                                   start=(k == 0), stop=(k == nk - 1), r=[wk, inkeys[k]], w=PK(bk))
                for sub in range(2):
                    dm = db * 2 + sub
                    buf, bkey = cy[dm % 2], "cy%d" % (dm % 2)
                    for hf in range(2):
                        bk = 2 + sub * 2 + hf
                        CP("act" if bk % 2 == 0 else "dve", buf[:, hf * 512:(hf + 1) * 512], PS[bk][:, :], r=PK(bk), w=[bkey])
                    y_sink(dm, buf, bkey)

        def ffn(Wg, Wu, Wd):
            for fb in range(22):
                blk = {}
                for mi, Wm in enumerate((Wg, Wu)):
                    for kh in range(2):
                        blk[(mi, kh)] = wblock(Wm[kh * 1024:(kh + 1) * 1024, fb * 256:(fb + 1) * 256], 8, 256,
                                               engs=("act", "dve"))
                for sub in range(2):
                    fc = fb * 2 + sub
                    for hf in range(2):
                        hs = slice(hf * 512, (hf + 1) * 512)
                        for mi in range(2):
                            bk = 2 * hf + mi
                            for kc in range(16):
                                wv, wk = blk[(mi, kc // 8)]
                                MM(PS[bk][:, :], wv[:, kc % 8, sub * 128:(sub + 1) * 128], hT[:, kc, hs],
                                   start=(kc == 0), stop=(kc == 15), r=[wk, HK[kc]], w=PK(bk))
                        tb, tk = cy[hf][:, 0:512], "cy%d" % hf
                        ACT(tb, PS[2 * hf][:, :], AF.Silu, r=PK(2 * hf), w=[tk])
                        TT("dve", actT[:, fc, hs], tb, PS[2 * hf + 1][:, :], ALU.mult, r=[tk] + PK(2 * hf + 1), w=[AK[fc]])
            proj_T(actT, AK, FC, Wd)

        m0c, m1c = vec[:, V_PAR:V_PAR + 1], vec[:, V_PAR + 1:V_PAR + 2]

        def post_res(i, Xsrc, ts, xkey, dst, need_stats, ts2=None):
            stat_fin()
            for kc in range(16):
                cb, ck = cx[kc % 2], "cx%d" % (kc % 2)
                yb, yk = cy[kc % 2], "cy%d" % (kc % 2)
                S.dma("sp", yb, YFv[:, kc, :], reads=["YF%d" % kc], writes=[yk])
                S.dma("sp", cb, Xsrc[:, kc, ts], reads=[xkey], writes=[ck])
                if ts2 is not None:
                    c2 = vf32(actT_o + 16384 + 2048 * (kc % 2), 1024)
                    c2k = [AK[16 + 2 * (kc % 2)], AK[17 + 2 * (kc % 2)]]
                    S.dma("sp", c2, Xsrc[:, kc, ts2], reads=[xkey], writes=c2k)
                    TS("dve", c2, c2, m1c, None, ALU.mult, r=c2k + ["vec"], w=c2k)
                    STT("dve", cb, cb, m0c, c2, ALU.mult, ALU.add, r=[ck, "vec"] + c2k, w=[ck])
                STT("dve", yb, yb, coef[i][:, kc:kc + 1], rstdB, ALU.mult, ALU.mult, r=[yk, "coef%d" % i, "rstdB"], w=[yk])
                TT("pool", cb, cb, yb, ALU.add, r=[ck, yk], w=[ck])
                if need_stats:
                    stat_acc(cb, [ck], kc == 0, kc == 15)
                dst(kc, cb, ck)

        def phase1():
            for t in range(T // 1024):
                ts = slice(t * 1024, (t + 1) * 1024)
                for j in range(8):
                    xb, xbk = xin[j % 2], AK[32 + 4 * (j % 2):36 + 4 * (j % 2)]
                    S.dma("sp", xb, x_d[t * 1024 + j * 128:t * 1024 + (j + 1) * 128, :], writes=xbk)
                    for q4 in range(4):
                        bk = 4 + q4 % 2
                        for c in range(4):
                            kc = q4 * 4 + c
                            TR(PS[bk][:, c * 128:(c + 1) * 128], xb[:, kc * 128:(kc + 1) * 128], r=["cst"] + xbk, w=PK(bk))
                        CP("act" if bk % 2 == 0 else "dve", xTf[:, q4 * 4:(q4 + 1) * 4, j * 128:(j + 1) * 128],
                           PS[bk][:, :].rearrange("p (c t) -> p c t", c=4), r=PK(bk), w=AK[8 * q4:8 * q4 + 8])
                S.dma("sp", X0v[:, :, ts], xTf, reads=AK[0:32], writes=["X0s"])
                for kc in range(16):
                    stat_acc(xTf[:, kc, :], XFK(kc), kc == 0, kc == 15)
                stat_fin()
                for kc in range(16):
                    mod_to_hT(0, kc, xTf[:, kc, :], XFK(kc))
                ffn(w1g_d, w1u_d, w1d_d)
                post_res(0, X0v, ts, "X0s", lambda kc, cb, ck: S.dma("sp", X1v[:, kc, ts], cb, reads=[ck], writes=["X1s"]), True)
                prenorm_dram(1, X1v, ts, "X1s")
                S.dma("sp", H2v[:, :, ts], hT, reads=HK, writes=["H2s"])

        def phase3():
            tiles = [0] if split else list(range(T // 1024))
            for t in tiles:
                ts = slice(t * 1024, (t + 1) * 1024)
                ts2 = slice(1024, 2048) if split else None
                S.dma("sp", hT, Yv[:, :, ts], reads=["Ys"], writes=HK)
                if split:
                    hB = vbf(actT_o, 16 * 1024).rearrange("p (k t) -> p k t", k=16)
                    S.dma("sp", hB, Yv[:, :, ts2], reads=["Ys"], writes=AK[0:16])
                    for kc in range(16):
                        TS("dve", hB[:, kc, :], hB[:, kc, :], m1c, None, ALU.mult, r=[AK[kc], "vec"], w=[AK[kc]])
                        STT("dve", hT[:, kc, :], hT[:, kc, :], m0c, hB[:, kc, :], ALU.mult, ALU.add,
                            r=[HK[kc], "vec", AK[kc]], w=[HK[kc]])
                proj_T(hT, HK, 16, wout_d)
                post_res(1, X1v, ts, "X1s", lambda kc, cb, ck: S.dma("sp", X0v[:, kc, ts], cb, reads=[ck], writes=["X0s"]), True,
                         ts2=ts2)
                prenorm_dram(2, X0v, ts, "X0s")
                ffn(w2g_d, w2u_d, w2d_d)
                post_res(2, X0v, ts, "X0s",
                         lambda kc, cb, ck: CP("act", xTf[:, kc, :], cb, r=[ck], w=XFK(kc)), False)
                for j in range(8):
                    xb, xbk = xin[j % 2], AK[32 + 4 * (j % 2):36 + 4 * (j % 2)]
                    for q4 in range(4):
                        bk = 4 + q4 % 2
                        for c in range(4):
                            kc = q4 * 4 + c
                            TR(PS[bk][:, c * 128:(c + 1) * 128], xTf[:, kc, j * 128:(j + 1) * 128], r=["cst"] + XFK(kc), w=PK(bk))
                        CP("act" if bk % 2 == 0 else "dve", xb[:, q4 * 512:(q4 + 1) * 512], PS[bk][:, :], r=PK(bk), w=xbk)
                    S.dma("sp", out_d[t * 1024 + j * 128:t * 1024 + (j + 1) * 128, :], xb, reads=xbk, writes=["out"])

        def phase2():
            ring["nstg"], ring["nwb"] = 2, 4
            h2_o = [wb_off[4], None]
            apos[0] = main_o
            h2_o[1] = alloc(16384)
            raw_o = [alloc(4 * T), alloc(4 * T), alloc(4 * T), alloc(4 * T)]
            cb_o = alloc(4 * T)
            Kt_o, Vt_o = alloc(4 * T), alloc(4 * T)
            AT_o, wT_o, u_o = alloc(4 * T), alloc(4 * T), alloc(4 * T)
            o_o = raw_o[2]
            G = 2
            un_o = [dict(B=alloc(512), dec=alloc(512), decS=alloc(512), decT=alloc(512), N=alloc(512), A=alloc(512),
                         XY=[alloc(512) for _ in range(4)], R=[alloc(1024), alloc(1024)]) for _ in range(G)]
            S_o = [alloc(512), alloc(512)]
            vn_o = [alloc(512), alloc(512)]
            kd_o = [alloc(512), alloc(512)]
            ot_o = [alloc(512), alloc(512)]
            on_o = [alloc(512), alloc(512)]
            yh_o = [alloc(2 * T)] * 2
            ab_o = alloc(NTL * 16 * 4)
            sm_o = {k: alloc(NTL * 8 * 4) for k in ("gcol", "beta", "nbeta", "Gc", "gl", "eG", "edec", "egl", "bEG", "t1", "t2")}
            wab_o = alloc(16 * 16 * 2)
            ss_o = alloc(64 * 4)
            km_o = alloc(64)
            gate_o = alloc(NTL * 8 * 4)
            gtmp_o = [alloc(64) for _ in range(4)]
            seln_o = alloc(NTL * 8 * 4)
            pt_o = [alloc(256) for _ in range(4)]
            rl_o = [alloc(512), alloc(512)]
            qb_o, kb_o, vtk_o = AT_o, wT_o, u_o
            if T >= 2048:
                selT_o = Vt_o + 2048
                rt_o = [Vt_o, Vt_o + 1024]
            else:
                selT_o = alloc(2 * T)
                rt_o = [alloc(2048), alloc(2048)]

            h2t = [vbf(o, 16 * 512).rearrange("p (k t) -> p k t", k=16) for o in h2_o]
            raw = [vf32(o, T) for o in raw_o]
            cbuf = vf32(cb_o, T)
            Kt = vf32(Kt_o, NTL * 128).rearrange("p (n d) -> p n d", n=NTL)
            Vt = vf32(Vt_o, NTL * 128).rearrange("p (n d) -> p n d", n=NTL)
            ATa = vf32(AT_o, NTL * 128).rearrange("p (n d) -> p n d", n=NTL)
            wTa = vf32(wT_o, NTL * 128).rearrange("p (n d) -> p n d", n=NTL)
            ua = vf32(u_o, NTL * 128).rearrange("p (n d) -> p n d", n=NTL)
            oa = vf32(o_o, NTL * 128).rearrange("p (n d) -> p n d", n=NTL)
            sm = {k: vf32(o, NTL * 8).rearrange("p (n h) -> p n h", n=NTL) for k, o in sm_o.items()}
            ab = vf32(ab_o, NTL * 16).rearrange("p (n c) -> p n c", n=NTL)
            wab = vbf(wab_o, 256).rearrange("p (k c) -> p k c", k=16)
            ssv = vf32(ss_o, 64)

            def h2load(tt, slot):
                S.dma("sp", h2t[slot], H2v[:, :, tt * 512:(tt + 1) * 512], reads=["H2s"], writes=["h2t%d" % slot])

            sv = vf32(stg_off[0], 256).rearrange("p (k c) -> p k c", k=16)
            S.dma("sp", sv, win_d[:, 4096:4112].rearrange("(k p) c -> p k c", p=128), writes=["stg0"])
            CP("dve", wab, sv, r=["stg0"], w=["wab"])
            for tt in range(NT):
                h2load(tt, tt % 2)
                for sub in range(4):
                    tl = tt * 4 + sub
                    for kc in range(16):
                        MM(PS[0][:, tl * 16:(tl + 1) * 16], h2t[tt % 2][:, kc, sub * 128:(sub + 1) * 128], wab[:, kc, :],
                           start=(kc == 0), stop=(kc == 15), r=["h2t%d" % (tt % 2), "wab"], w=PK(0, 0, 256))
            CP("dve", ab.rearrange("p n c -> p (n c)"), PS[0][:, 0:NTL * 16], r=PK(0, 0, 256), w=["ab"])
            flat = lambda k: sm[k].rearrange("p n h -> p (n h)")
            alog = vec[:, V_ALOG:V_ALOG + 128].rearrange("p (n h) -> p n h", n=16)[:, 0:NTL, :]
            dtb = vec[:, V_DTB:V_DTB + 128].rearrange("p (n h) -> p n h", n=16)[:, 0:NTL, :]
            TT("dve", sm["t1"], ab[:, :, 0:8], dtb, ALU.add, r=["ab", "vec"], w=["t1"])
            ACT(flat("t1"), flat("t1"), AF.Exp, r=["t1"], w=["t1"])
            ACT(flat("t1"), flat("t1"), AF.Ln, r=["t1"], w=["t1"], bias=1.0)
            ACT(sm["t2"], alog, AF.Exp, r=["vec"], w=["t2"])
            STT("dve", flat("gcol"), flat("t1"), -1.0, flat("t2"), ALU.mult, ALU.mult, r=["t1", "t2"], w=["gcol"])
            ACT(sm["beta"], ab[:, :, 8:16], AF.Sigmoid, r=["ab"], w=["beta"])
            TS("dve", flat("nbeta"), flat("beta"), -1.0, None, ALU.mult, r=["beta"], w=["nbeta"])
            for tl in range(NTL):
                MM(PS[1][:, tl * 8:(tl + 1) * 8], triu_f, sm["gcol"][:, tl, :], r=["cst", "gcol"], w=PK(1, 0, 128))
                MM(PS[2][:, tl * 8:(tl + 1) * 8], ones_f, sm["gcol"][:, tl, :], r=["cst", "gcol"], w=PK(2, 0, 128))
            CP("dve", flat("Gc"), PS[1][:, 0:NTL * 8], r=PK(1, 0, 128), w=["Gc"])
            CP("dve", flat("gl"), PS[2][:, 0:NTL * 8], r=PK(2, 0, 128), w=["gl"])
            ACT(flat("eG"), flat("Gc"), AF.Exp, r=["Gc"], w=["eG"])
            ACT(flat("egl"), flat("gl"), AF.Exp, r=["gl"], w=["egl"])
            TT("dve", flat("t1"), flat("gl"), flat("Gc"), ALU.subtract, r=["gl", "Gc", "t1"], w=["t1"])
            ACT(flat("edec"), flat("t1"), AF.Exp, r=["t1"], w=["edec"])
            TT("dve", flat("bEG"), flat("beta"), flat("eG"), ALU.mult, r=["beta", "eG"], w=["bEG"])

            if lvl < 1:
                ring["nstg"], ring["nwb"] = NSTG, NWB
                return

            def inproj(cols, nchunk):
                blks = [wblock(win_d[:, c0:c0 + 128], 16, 128, engs=("act",)) for c0 in cols]
                for tt in range(NT):
                    h2load(tt, tt % 2)
                    for ci in range(nchunk):
                        wv, wk = blks[ci]
                        for kc in range(16):
                            MM(PS[ci][:, :], wv[:, kc, :], h2t[tt % 2][:, kc, :], start=(kc == 0), stop=(kc == 15),
                               r=[wk, "h2t%d" % (tt % 2)], w=PK(ci))
                        CP("act" if ci % 2 else "dve", raw[ci][:, tt * 512:(tt + 1) * 512], PS[ci][:, :],
                           r=PK(ci), w=["raw%d" % ci])

            def to_tokmajor(src, sk, dstflat, dk_):
                for g4 in range(NTL // 4):
                    b = 6 + g4 % 2
                    for c in range(4):
                        tl = g4 * 4 + c
                        TR(PS[b][:, c * 128:(c + 1) * 128], src[:, tl * 128:(tl + 1) * 128], r=["cst", sk],
                           w=PK(b, c * 128, (c + 1) * 128))
                    CP("act" if g4 % 2 else "dve", dstflat(g4), PS[b][:, :], r=PK(b), w=[dk_])

            def gdn_head(h):
                inproj([h * 128, 1024 + h * 128, 2048 + h * 128, 3072 + h * 128], 4)
                for ci in range(3):
                    eng = "dve"
                    cw0 = V_CONV + (ci * 8 + h) * 4
                    x = raw[ci]
                    rk = "raw%d" % ci
                    TS(eng, cbuf[:, :], x[:, :], vec[:, cw0 + 3:cw0 + 4], None, ALU.mult, r=[rk, "vec"], w=["cbuf"])
                    for sft in (1, 2, 3):
                        STT(eng, cbuf[:, sft:T], x[:, 0:T - sft], vec[:, cw0 + 3 - sft:cw0 + 4 - sft], cbuf[:, sft:T],
                            ALU.mult, ALU.add, r=[rk, "vec", "cbuf"], w=["cbuf"])
                    ACT(x[:, :], cbuf[:, :], AF.Silu, r=["cbuf"], w=[rk])
                    if ci < 2:
                        ACT(cbuf[:, :], x[:, :], AF.Square, r=[rk], w=["cbuf"])
                        for tt in range(NT):
                            MM(PS[tt][:, :], ones_f, cbuf[:, tt * 512:(tt + 1) * 512], r=["cst", "cbuf"], w=PK(tt))
                        for tt in range(NT):
                            ACT(cbuf[:, tt * 512:(tt + 1) * 512], PS[tt][:, :], AF.Ln, r=PK(tt), w=["cbuf"], bias=EPS)
                        ACT(cbuf[:, :], cbuf[:, :], AF.Exp, r=["cbuf"], w=["cbuf"], scale=-0.5)
                        if ci == 0:
                            STT("dve", x[:, :], x[:, :], 128.0 ** -0.5, cbuf[:, :], ALU.mult, ALU.mult, r=[rk, "cbuf"], w=[rk])
                        else:
                            TT("dve", x[:, :], x[:, :], cbuf[:, :], ALU.mult, r=[rk, "cbuf"], w=[rk])
                ACT(raw[3][:, :], raw[3][:, :], AF.Silu, r=["raw3"], w=["raw3"])
                TS("pool", raw[3][:, :], raw[3][:, :], vec[:, V_NG:V_NG + 1], None, ALU.mult, r=["raw3", "vec"], w=["raw3"])
                qT, kT, vT, gz = raw
                if lvl < 2:
                    return
                to_tokmajor(kT, "raw1", lambda g4: Kt[:, g4 * 4:(g4 + 1) * 4, :].rearrange("p n d -> p (n d)"), "Kt")
                to_tokmajor(vT, "raw2", lambda g4: Vt[:, g4 * 4:(g4 + 1) * 4, :].rearrange("p n d -> p (n d)"), "Vt")
                if lvl < 2.5:
                    return
                for g0 in range(0, NTL, G):
                    units = list(range(g0, min(NTL, g0 + G)))
                    U = {}
                    for tl in units:
                        u = tl % G
                        o = un_o[u]
                        U[tl] = dict(
                            B=vf32(o["B"], 128), dec=vf32(o["dec"], 128), decS=vf32(o["decS"], 128), decT=vf32(o["decT"], 128),
                            N=vf32(o["N"], 128), A=vf32(o["A"], 128), XY=[vf32(q, 128) for q in o["XY"]],
                            R=[vf32(q, 256) for q in o["R"]], pa=PS[2 * u][:, 0:128], pb=PS[2 * u][:, 128:256], pc=PS[2 * u + 1][:, 0:256],
                            pak=PK(2 * u, 0, 128), pbk=PK(2 * u, 128, 256), pck=PK(2 * u + 1, 0, 256),
                            k=(lambda s_, u=u: "u%d%s" % (u, s_)), tsl=slice(tl * 128, (tl + 1) * 128))
                    for tl in units:
                        d = U[tl]; k = d["k"]
                        TS("pool", d["B"], sgtj_f, sm["gcol"][:, tl, h:h + 1], None, ALU.mult, r=["cst", "gcol"], w=[k("B")])
                        MM(d["pa"], triu_f, d["B"], r=["cst", k("B")], w=d["pak"])
                        ACT(d["dec"], d["pa"], AF.Exp, r=d["pak"], w=[k("dec")])
                        TS("pool", d["R"][0][:, 0:128], Vt[:, tl, :], sm["beta"][:, tl, h:h + 1], None, ALU.mult,
                           r=["Vt", "beta"], w=[k("R0")])
                        TS("pool", d["R"][0][:, 128:256], Kt[:, tl, :], sm["bEG"][:, tl, h:h + 1], None, ALU.mult,
                           r=["Kt", "bEG", k("R0")], w=[k("R0")])
                        if lvl < 2.6:
                            continue
                        MM(d["pa"], kT[:, d["tsl"]], kT[:, d["tsl"]], r=["raw1"], w=d["pak"])
                        MM(d["pb"], qT[:, d["tsl"]], kT[:, d["tsl"]], r=["raw0", "raw1"], w=d["pbk"])
                        TT("dve", d["decS"], d["dec"], strict_f, ALU.mult, r=[k("dec"), "cst"], w=[k("decS")])
                        TT("dve", d["decT"], d["dec"], tril_f, ALU.mult, r=[k("dec"), "cst"], w=[k("decT")])
                        STT("dve", d["N"], d["pa"], sm["nbeta"][:, tl, h:h + 1], d["decS"], ALU.mult, ALU.mult,
                            r=d["pak"] + ["nbeta", k("decS")], w=[k("N")])
                        TT("dve", d["A"], d["pb"], d["decT"], ALU.mult, r=d["pbk"] + [k("decT")], w=[k("A")])
                        if lvl < 2.7:
                            continue
                        TR(d["pa"], d["N"], r=["cst", k("N")], w=d["pak"])
                        TR(d["pb"], d["A"], r=["cst", k("A")], w=d["pbk"])
                        CP("act", d["XY"][1], d["pa"], r=d["pak"], w=[k("XY1")])
                        CP("act", ATa[:, tl, :], d["pb"], r=d["pbk"], w=["ATa"])
                        if lvl < 2.78:
                            continue
                    if lvl < 2.8:
                        continue
                    nlv = 7 if lvl >= 3 else int(round((lvl - 2.8) * 100))
                    for lv in range(nlv):
                        for tl in units:
                            d = U[tl]; k = d["k"]
                            if lv == 0:
                                X, Xk, Y, Yk = d["N"], k("N"), d["XY"][1], k("XY1")
                            else:
                                xi, yi = (0, 1) if lv % 2 == 0 else (2, 3)
                                X, Xk, Y, Yk = d["XY"][xi], k("XY%d" % xi), d["XY"][yi], k("XY%d" % yi)
                            Rc, Rck = d["R"][lv % 2], k("R%d" % (lv % 2))
                            cv = "rs"
                            if "r" in cv:
                                MM(d["pc"], Y, Rc, r=[Yk, Rck], w=d["pck"])
                            if lv < 6 or nlv < 7:
                                Rn, Rnk = d["R"][(lv + 1) % 2], k("R%d" % ((lv + 1) % 2))
                                if "r" in cv:
                                    TT("dve", Rn, Rc, d["pc"], ALU.add, r=[Rck] + d["pck"], w=[Rnk])
                                nxi, nyi = (0, 1) if (lv + 1) % 2 == 0 else (2, 3)
                                if "s" in cv:
                                    MM(d["pa"], Y, X, r=[Yk, Xk], w=d["pak"])
                                    MM(d["pb"], X, Y, r=[Yk, Xk], w=d["pbk"])
                                    CP("dve", d["XY"][nxi], d["pa"], r=d["pak"], w=[k("XY%d" % nxi)])
                                    CP("dve", d["XY"][nyi], d["pb"], r=d["pbk"], w=[k("XY%d" % nyi)])
                            else:
                                TT("dve", ua[:, tl, :], Rc[:, 0:128], d["pc"][:, 0:128], ALU.add, r=[Rck] + d["pck"], w=["ua"])
                                TT("dve", d["A"], Rc[:, 128:256], d["pc"][:, 128:256], ALU.add, r=[Rck] + d["pck"], w=[k("A")])
                                TR(d["pa"], d["A"], r=["cst", k("A")], w=d["pak"])
                                CP("act", wTa[:, tl, :], d["pa"], r=d["pak"], w=["wTa"])
                if lvl < 4:
                    return
                Sst = [vf32(o, 128) for o in S_o]
                MEMSET("pool", Sst[0], 0.0, w=["S0"])
                for tl in range(NTL):
                    cur, nxt = tl % 2, (tl + 1) % 2
                    b = 4 + tl % 2
                    pb_ = PS[b]
                    tsl = slice(tl * 128, (tl + 1) * 128)
                    vn, kd, ot = vf32(vn_o[tl % 2], 128), vf32(kd_o[tl % 2], 128), vf32(ot_o[tl % 2], 128)
                    vk, kk, ok_ = "vn%d" % (tl % 2), "kd%d" % (tl % 2), "ot%d" % (tl % 2)
                    MM(pb_[:, 0:128], wTa[:, tl, :], Sst[cur], r=["wTa", "S%d" % cur], w=PK(b, 0, 128))
                    TT("dve", vn, ua[:, tl, :], pb_[:, 0:128], ALU.subtract, r=["ua"] + PK(b, 0, 128), w=[vk])
                    MM(pb_[:, 128:256], qT[:, tsl], Sst[cur], r=["raw0", "S%d" % cur], w=PK(b, 128, 256))
                    MM(pb_[:, 256:384], ATa[:, tl, :], vn, r=["ATa", vk], w=PK(b, 256, 384))
                    TS("dve", ot, pb_[:, 128:256], sm["eG"][:, tl, h:h + 1], None, ALU.mult, r=PK(b, 128, 256) + ["eG"], w=[ok_])
                    TT("dve", oa[:, tl, :], ot, pb_[:, 256:384], ALU.add, r=[ok_] + PK(b, 256, 384), w=["raw2"])
                    TS("pool", kd, Kt[:, tl, :], sm["edec"][:, tl, h:h + 1], None, ALU.mult, r=["Kt", "edec"], w=[kk])
                    MM(pb_[:, 384:512], kd, vn, r=[kk, vk], w=PK(b, 384, 512))
                    STT("dve", Sst[nxt], Sst[cur], sm["egl"][:, tl, h:h + 1], pb_[:, 384:512], ALU.mult, ALU.add,
                        r=["S%d" % cur, "egl"] + PK(b, 384, 512), w=["S%d" % nxt])
                if lvl < 5:
                    return
                yh = vbf(yh_o[h % 2], T)
                yk = "yh"
                ACT(cbuf[:, :], oa.rearrange("p n d -> p (n d)"), AF.Square, r=["raw2"], w=["cbuf"])
                S.add("dve", lambda e: e.tensor_reduce(ssv[:, 0:NTL], cbuf[:, :].rearrange("p (n d) -> p n d", n=NTL), AX.X, ALU.add),
                      ["cbuf"], ["ssv"])
                ACT(ssv[:, 0:NTL], ssv[:, 0:NTL], AF.Ln, r=["ssv"], w=["ssv"], scale=1.0 / 128, bias=EPS)
                ACT(ssv[:, 0:NTL], ssv[:, 0:NTL], AF.Exp, r=["ssv"], w=["ssv"], scale=-0.5)
                for tl in range(NTL):
                    on = vf32(on_o[tl % 2], 128)
                    onk = "on%d" % (tl % 2)
                    b = 6 + tl % 2
                    TS("pool", on, oa[:, tl, :], ssv[:, tl:tl + 1], None, ALU.mult, r=["raw2", "ssv"], w=[onk])
                    TR(PS[b][:, 0:128], on, r=["cst", onk], w=PK(b, 0, 128))
                    TT("dve", yh[:, tl * 128:(tl + 1) * 128], PS[b][:, 0:128], gz[:, tl * 128:(tl + 1) * 128], ALU.mult,
                       r=PK(b, 0, 128) + ["raw3"], w=[yk])
                S.dma("sp", Y_d[h * 128:(h + 1) * 128, :], yh, reads=[yk], writes=["Ys"])

            def moba_head(m):
                inproj([4112 + m * 128, 5136 + m * 128, 6160 + m * 128], 3)
                rq, rk_, rv = raw[0], raw[1], raw[2]
                cosv = cbuf[0:32, 0:T]
                sinv = vf32(Kt_o, T)[0:32, :]
                pmf = cst[0:32, C_PM:C_PM + 32]
                for ci in range(2):
                    x = raw[ci]
                    xk = "raw%d" % ci
                    for tt in range(NT):
                        cs = slice(tt * 512, (tt + 1) * 512)
                        b = 4 + tt % 2
                        t1 = vf32(rt_o[0], 512)
                        t2 = vf32(rt_o[1], 512)
                        MM(PS[b][0:32, :], pmf, x[0:32, cs], r=["cst", xk], w=PK(b))
                        TT("dve", t1[0:32, :], x[0:32, cs], cosv[:, cs], ALU.mult, r=[xk, "cbuf"], w=["Vt"])
                        TT("dve", t2[0:32, :], PS[b][0:32, :], sinv[:, cs], ALU.mult, r=PK(b) + ["Kt", "Vt"], w=["Vt"])
                        TT("pool", x[0:32, cs], t1[0:32, :], t2[0:32, :], ALU.add, r=["Vt", xk], w=[xk])
                km = vf32(km_o, 8)
                S.add("dve", lambda e: e.tensor_reduce(km[:, 0:NB], rk_[:, :].rearrange("p (n t) -> p n t", n=NB), AX.X, ALU.add),
                      ["raw1"], ["km"])
                TS("dve", km[:, 0:NB], km[:, 0:NB], 1.0 / 256, None, ALU.mult, r=["km"], w=["km"])
                gate = vf32(gate_o, NTL * 8).rearrange("p (n b) -> p n b", n=NTL)
                seln = vf32(seln_o, NTL * 8).rearrange("p (n b) -> p n b", n=NTL)
                for tl in range(NTL):
                    MM(PS[5][:, tl * 8:tl * 8 + NB], rq[:, tl * 128:(tl + 1) * 128], km[:, 0:NB], r=["raw0", "km"], w=PK(5, 0, 128))
                CP("dve", gate[:, :, 0:NB], PS[5][:, 0:NTL * 8].rearrange("p (n b) -> p n b", b=8)[:, :, 0:NB], r=PK(5, 0, 128), w=["gate"])
                MEMSET("pool", seln.rearrange("p n b -> p (n b)"), 0.0, w=["seln"])
                gt = [vf32(o, 8) for o in gtmp_o]
                for tl in range(NTL):
                    own = tl // 2
                    if own <= 3:
                        continue
                    g = gate[:, tl, 0:own]
                    cur = g
                    ck = "gate"
                    for it in range(3):
                        S.add("dve", lambda e, cur=cur, it=it: e.tensor_reduce(gt[3][:, it:it + 1], cur, AX.X, ALU.max),
                              [ck, "gt3"], ["gt3"])
                        if it < 2:
                            TS("dve", gt[2][:, 0:own], cur, gt[3][:, it:it + 1], -1e30, ALU.is_ge, ALU.mult, r=[ck, "gt3"], w=["gt2"])
                            TT("dve", gt[it][:, 0:own], cur, gt[2][:, 0:own], ALU.add, r=[ck, "gt2"], w=["gt%d" % it])
                            cur = gt[it][:, 0:own]
                            ck = "gt%d" % it
                    TS("dve", seln[:, tl, 0:own], g, gt[3][:, 2:3], NEG, ALU.is_lt, ALU.mult, r=["gate", "gt3", "seln"], w=["seln"])
                selT = vbf(selT_o, T)
                for g4 in range(NTL // 4):
                    b = 6 + g4 % 2
                    for c in range(4):
                        tl = g4 * 4 + c
                        TR(PS[b][0:8, c * 128:(c + 1) * 128], seln[:, tl, :], r=["cst", "seln"], w=PK(b, c * 128, (c + 1) * 128))
                    CP("dve", selT[0:8, g4 * 512:(g4 + 1) * 512], PS[b][0:8, :], r=PK(b), w=["Vt"])
                qb, kb = vbf(qb_o, T), vbf(kb_o, T)
                CP("act", qb, rq[:, :], r=["raw0"], w=["ATa"])
                CP("pool", kb, rk_[:, :], r=["raw1"], w=["wTa"])
                vtk = vbf(vtk_o, NTL * 128).rearrange("p (n d) -> p n d", n=NTL)
                to_tokmajor(rv, "raw2", lambda g4: vtk[:, g4 * 4:(g4 + 1) * 4, :].rearrange("p n d -> p (n d)"), "ua")
                yh = vbf(yh_o[m % 2], T)
                yk = "yh"
                eb = cstb[0:8, C_EBLK:C_EBLK + 1024].rearrange("p (n k) -> p n k", n=8)
                items = []
                for i in range(NTL):
                    own = i // 2
                    keys = [(kt, "sel") for kt in range(2 * own)]
                    if i % 2:
                        keys.append((i - 1, "full"))
                    keys.append((i, "diag"))
                    for n_, (kt, kind) in enumerate(keys):
                        items.append((i, kt, kind, n_ == 0, n_ == len(keys) - 1))
                LA = 2
                ppb = (0, 1, 4)

                def stage12(idx):
                    i, kt, kind, first, last = items[idx]
                    qs = slice(i * 128, (i + 1) * 128)
                    bk = ppb[idx % 3]
                    pp = PS[bk][:, 0:128]
                    ptb, ptk = vbf(pt_o[idx % 4], 128), "pt%d" % (idx % 4)
                    MM(pp, kb[:, kt * 128:(kt + 1) * 128], qb[:, qs], start=True, stop=(kind != "sel"),
                       r=["wTa", "ATa"], w=PK(bk))
                    if kind == "sel":
                        MM(pp, eb[:, kt // 2, :], selT[0:8, qs], start=False, stop=True, r=["cstb", "Vt"], w=PK(bk))
                    ACT(ptb, pp, AF.Exp, r=PK(bk), w=[ptk], scale=128.0 ** -0.5)
                    if kind == "diag":
                        TT("pool", ptb, ptb, triu_b, ALU.mult, r=[ptk, "cstb"], w=[ptk])

                def stage3(idx):
                    i, kt, kind, first, last = items[idx]
                    qs = slice(i * 128, (i + 1) * 128)
                    ptb, ptk = vbf(pt_o[idx % 4], 128), "pt%d" % (idx % 4)
                    bo, bl = (2, 3) if i % 2 == 0 else (6, 7)
                    po, pl = PS[bo][:, 0:128], PS[bl][:, 0:128]
                    MM(po, vtk[:, kt, :], ptb, start=first, stop=last, r=["ua", ptk], w=PK(bo))
                    MM(pl, ones_b, ptb, start=first, stop=last, r=["cstb", ptk], w=PK(bl))
                    if last:
                        rl = vf32(rl_o[i % 2], 128)
                        rlk = "rl%d" % (i % 2)
                        S.add("dve", lambda e, rl=rl, pl=pl: e.reciprocal(rl, pl), PK(bl), [rlk])
                        TT("dve", yh[:, qs], po, rl, ALU.mult, r=PK(bo) + [rlk], w=[yk])

                for idx in range(len(items) + LA):
                    if idx < len(items):
                        stage12(idx)
                    if idx - LA >= 0:
                        stage3(idx - LA)
                S.dma("sp", Y_d[1024 + m * 128:1024 + (m + 1) * 128, :], yh, reads=[yk], writes=["Ys"])

            for h in range(ngh):
                if gdn_on:
                    gdn_head(h)
            if moba_on:
                S.dma("sp", cbuf[0:32, 0:T], rope_d[:, 0:T], writes=["cbuf"])
                S.dma("sp", vf32(Kt_o, T)[0:32, :], rope_d[:, T:2 * T], writes=["Kt"])
                for m in range(nmh):
                    moba_head(m)
            ring["nstg"], ring["nwb"] = NSTG, NWB

        if 0 in phases:
            phase0()
        if 1 in phases:
            phase1()
        if 2 in phases:
            S.barrier()
            phase2()
            S.barrier()
        if 3 in phases:
            phase3()
        S.finalize(st)
        S.run_block()
    return nc


def host_consts(T):
    c = np.zeros((128, NCONST), np.float32)
    i = np.arange(128)
    c[:, C_ID:C_ID + 128] = np.eye(128)
    c[:, C_ONE:C_ONE + 128] = 1.0
    c[:, C_TRIU:C_TRIU + 128] = (i[:, None] <= i[None, :])
    c[:, C_SGTJ:C_SGTJ + 128] = (i[:, None] > i[None, :])
    c[:, C_TRIL:C_TRIL + 128] = (i[:, None] >= i[None, :])
    c[:, C_STRICT:C_STRICT + 128] = (i[:, None] > i[None, :])
    for m in range(16):
        c[m + 16, C_PM + m] = -1.0
        c[m, C_PM + m + 16] = 1.0
    for n in range(8):
        c[n, C_EBLK + n * 128:C_EBLK + (n + 1) * 128] = 1.0
    inv = 500000.0 ** (-np.arange(0, 32, 2, dtype=np.float32) / 32)
    ang = np.arange(T, dtype=np.float32)[None, :] * inv[:, None].astype(np.float32)
    rope = np.zeros((32, 2 * T), np.float32)
    rope[0:16, 0:T] = np.cos(ang)
    rope[16:32, 0:T] = np.cos(ang)
    rope[0:16, T:] = np.sin(ang)
    rope[16:32, T:] = np.sin(ang)
    return c, rope


def col16(v):
    return np.ascontiguousarray(np.asarray(v, np.float32).reshape(-1, 128).T)


def host_vecs(inp, b, parity=0):
    v = np.zeros((128, NVEC), np.float32)
    v[:, V_C:V_C + 16] = col16(inp["c"][b])
    v[:, V_BADA:V_BADA + 144] = col16(inp["b_ada"][0])
    for i, nm in enumerate(("ffn1_pre_g", "ffn1_post_g", "mix_pre_g", "mix_post_g", "ffn2_pre_g", "ffn2_post_g")):
        v[:, V_G + i * 16:V_G + (i + 1) * 16] = col16(inp[nm][0])
    cw = np.asarray(inp["gdn_conv_w"][0], np.float32)
    v[:, V_CONV:V_CONV + 96] = cw.T.reshape(24, 128, 4).transpose(1, 0, 2).reshape(128, 96)
    v[:, V_ALOG:V_ALOG + 128] = np.tile(np.asarray(inp["gdn_a_log"][0], np.float32), 16)[None, :]
    v[:, V_DTB:V_DTB + 128] = np.tile(np.asarray(inp["gdn_dt_bias"][0], np.float32), 16)[None, :]
    v[:, V_NG] = np.asarray(inp["gdn_norm_g"][0], np.float32)
    v[:, V_PAR + parity] = 1.0
    return v


_NC_CACHE = {}


def kernel(**inp):
    inp = {k: np.asarray(v) for k, v in inp.items()}
    B, T, _ = inp["x"].shape
    if T not in _NC_CACHE:
        _NC_CACHE[T] = build_nc(T)
    nc = _NC_CACHE[T]
    consts, rope = host_consts(T)
    shared = dict(consts=consts, rope=rope, w_ada=np.ascontiguousarray(inp["w_ada"][0]),
                  w1g=np.ascontiguousarray(inp["ffn1_w_gate"][0]), w1u=np.ascontiguousarray(inp["ffn1_w_up"][0]),
                  w1d=np.ascontiguousarray(inp["ffn1_w_down"][0]), w2g=np.ascontiguousarray(inp["ffn2_w_gate"][0]),
                  w2u=np.ascontiguousarray(inp["ffn2_w_up"][0]), w2d=np.ascontiguousarray(inp["ffn2_w_down"][0]),
                  w_in=np.ascontiguousarray(inp["w_in"][0]), w_out=np.ascontiguousarray(inp["w_out"][0]))
    in_maps = []
    for core in range(8):
        b = core // 2
        m = dict(shared)
        m["x"] = np.ascontiguousarray(inp["x"][b])
        m["vecs"] = host_vecs(inp, b, core % 2)
        in_maps.append(m)
    res = run_bass_kernel_spmd(nc, in_maps, core_ids=list(range(8)))
    if T >= 2048:
        out = np.stack([np.concatenate([np.asarray(res.results[2 * b + p]["out"], np.float32) for p in range(2)], axis=0)
                        for b in range(B)], axis=0)
    else:
        out = np.stack([np.asarray(res.results[2 * b]["out"], np.float32) for b in range(B)], axis=0)
    return out
```

```python
import contextlib
import numpy as np
import concourse.bass as bass
import concourse.mybir as mybir
from concourse.bass_utils import run_bass_kernel_spmd

F32 = mybir.dt.float32
BF16 = mybir.dt.bfloat16
AF = mybir.ActivationFunctionType
ALU = mybir.AluOpType
AX = mybir.AxisListType

ENGS = ("pe", "act", "dve", "pool", "sp")

D = 2048
KC = 16
FF = 5632
FC = 44
NMOD = 9
INC = 7184
EPS = 1e-6
NEG = -30000.0


class Sched:
    def __init__(self, nc, n_dma_sems=48):
        self.nc = nc
        self.ops = []
        self.last_w = {}
        self.readers = {}
        self.n_dma_sems = n_dma_sems

    def add(self, eng, emit, reads=(), writes=(), dma=False):
        idx = len(self.ops)
        deps = set()
        for b in reads:
            if b in self.last_w:
                deps.add(self.last_w[b])
        for b in writes:
            if b in self.last_w:
                deps.add(self.last_w[b])
            rd = self.readers.get(b)
            if rd:
                deps.update(rd.values())
        deps.discard(idx)
        self.ops.append(dict(eng=eng, emit=emit, deps=deps, dma=dma, barrier=False))
        for b in reads:
            self.readers.setdefault(b, {})[("dma", idx) if dma else eng] = idx
        for b in writes:
            self.last_w[b] = idx
            self.readers[b] = {}
        return idx

    def barrier(self):
        self.ops.append(dict(eng=None, emit=None, deps=set(), dma=False, barrier=True))
        self.last_w = {}
        self.readers = {}

    def dma(self, q, out, in_, reads=(), writes=()):
        return self.add(q, lambda e: e.dma_start(out=out, in_=in_), reads, writes, dma=True)

    def finalize(self, stack):
        nc = self.nc
        ops = self.ops
        dma_count = 0
        for op in ops:
            op["signal"] = False
            if op["dma"]:
                op["dma_ord"] = dma_count
                dma_count += 1
        last_c = {}
        for i, op in enumerate(ops):
            if op["barrier"]:
                for e, j in last_c.items():
                    ops[j]["signal"] = True
                op["last_c"] = dict(last_c)
                continue
            for d in op["deps"]:
                od = ops[d]
                if od["dma"]:
                    continue
                if not (od["eng"] == "pe" and op["eng"] == "pe"):
                    od["signal"] = True
            if not op["dma"]:
                last_c[op["eng"]] = i
        sigcount = {e: 0 for e in ENGS}
        for op in ops:
            if op["barrier"] or op["dma"]:
                continue
            if op["signal"]:
                sigcount[op["eng"]] += 1
                op["sigval"] = sigcount[op["eng"]]
        esem = {e: stack.enter_context(nc.semaphore("s_" + e)) for e in ENGS}
        nd = min(self.n_dma_sems, max(1, dma_count))
        dsem = [stack.enter_context(nc.semaphore("d%d" % i)) for i in range(nd)]
        for op in ops:
            if op["dma"]:
                k = op["dma_ord"]
                op["dsem"] = dsem[k % nd]
                op["dval"] = 16 * (k // nd + 1)
                op["dslot"] = k % nd
        waited = {e: {} for e in ENGS}
        per_eng = {e: [] for e in ENGS}
        last_on_slot = {}
        pending = {e: [] for e in ENGS}
        for i, op in enumerate(ops):
            if op["barrier"]:
                for e in ENGS:
                    lst = []
                    for e2, j in op["last_c"].items():
                        lst.append((("e", e2), esem[e2], ops[j]["sigval"]))
                    for slot, o in last_on_slot.items():
                        lst.append((("d", slot), o["dsem"], o["dval"]))
                    pending[e] = lst
                continue
            e = op["eng"]
            waits = []

            def need(key, sem, val):
                if waited[e].get(key, 0) < val:
                    waited[e][key] = val
                    waits.append((sem, val))

            for key, sem, val in pending[e]:
                need(key, sem, val)
            pending[e] = []
            for d in sorted(op["deps"]):
                od = ops[d]
                if od["dma"]:
                    need(("d", od["dslot"]), od["dsem"], od["dval"])
                elif not (od["eng"] == "pe" and e == "pe"):
                    need(("e", od["eng"]), esem[od["eng"]], od["sigval"])
            if op["dma"]:
                prev = last_on_slot.get(op["dslot"])
                if prev is not None:
                    need(("d", op["dslot"]), prev["dsem"], prev["dval"])
                last_on_slot[op["dslot"]] = op
            op["waits"] = waits
            per_eng[e].append(op)
        self.esem = esem
        self.per_eng = per_eng
        self.final_dma_waits = [(o["dsem"], o["dval"]) for o in last_on_slot.values()]

    def emit_engine(self, e, eng, final=False):
        for op in self.per_eng[e]:
            for sem, val in op["waits"]:
                eng.wait_ge(sem, val)
            ins = op["emit"](eng)
            if op["dma"]:
                ins.then_inc(op["dsem"], 16)
            elif op["signal"]:
                ins.then_inc(self.esem[e], 1)
        if final:
            for sem, val in self.final_dma_waits:
                eng.wait_ge(sem, val)

    def run_block(self):
        nc = self.nc
        with nc.Block() as block:
            @block.tensor
            def _(eng):
                self.emit_engine("pe", eng)

            @block.scalar
            def _(eng):
                self.emit_engine("act", eng)

            @block.vector
            def _(eng):
                self.emit_engine("dve", eng)

            @block.gpsimd
            def _(eng):
                self.emit_engine("pool", eng)

            @block.sync
            def _(eng):
                self.emit_engine("sp", eng, final=True)


C_ID, C_ONE, C_TRIU, C_SGTJ, C_TRIL, C_STRICT, C_PM, C_EBLK = 0, 128, 256, 384, 512, 640, 768, 800
NCONST = 800 + 1024
V_C, V_BADA, V_G = 0, 16, 160
V_CONV = 256
V_ALOG = 352
V_DTB = 480
V_NG = 608
V_PAR = 609
NVEC = 611


def build_nc(T, phases=(0, 1, 2, 3), gdn_on=True, moba_on=True, lvl=99, ngh=8, nmh=8):
    NT = T // 512
    NTL = T // 128
    NB = T // 256
    nc = bass.Bass("TRN2", target_bir_lowering=False)
    dr = lambda name, shape, dt=F32, kind="ExternalInput": nc.dram_tensor(name, shape, dt, kind=kind).ap()
    x_d = dr("x", [T, D])
    consts_d = dr("consts", [128, NCONST])
    vecs_d = dr("vecs", [128, NVEC])
    rope_d = dr("rope", [32, 2 * T])
    wada_d = dr("w_ada", [D, NMOD * D])
    w1g_d, w1u_d, w1d_d = dr("w1g", [D, FF]), dr("w1u", [D, FF]), dr("w1d", [FF, D])
    w2g_d, w2u_d, w2d_d = dr("w2g", [D, FF]), dr("w2u", [D, FF]), dr("w2d", [FF, D])
    win_d = dr("w_in", [D, INC])
    wout_d = dr("w_out", [D, D])
    split = T >= 2048
    TO = 1024 if split else T
    out_d = dr("out", [TO, D], kind="ExternalOutput")
    X1_d = dr("X1s", [D, T], F32, "Internal")
    X0_d = dr("X0s", [D, T], F32, "Internal")
    YF_d = dr("YFs", [D, 1024], F32, "Internal")
    H2_d = dr("H2s", [D, T], BF16, "Internal")
    Y_d = dr("Ys", [D, T], BF16, "Internal")

    st = contextlib.ExitStack()
    with st:
        S = Sched(nc)
        arena = st.enter_context(nc.sbuf_tensor("arena", [128, 96 * 1024], BF16))
        cst = st.enter_context(nc.sbuf_tensor("cst", [128, NCONST], F32))
        cstb = st.enter_context(nc.sbuf_tensor("cstb", [128, NCONST], BF16))
        vec = st.enter_context(nc.sbuf_tensor("vec", [128, NVEC], F32))
        modv = st.enter_context(nc.sbuf_tensor("modv", [128, 144 + 9 * 16], F32))
        PS = [st.enter_context(nc.psum_tensor("ps%d" % i, [128, 512], F32)) for i in range(8)]

        apos = [0]

        def alloc(nbytes):
            o = apos[0]
            apos[0] += (nbytes + 63) // 64 * 32
            assert apos[0] <= 96 * 1024, apos[0]
            return o

        def vf32(off, n):
            return arena[:, off:off + 2 * n].bitcast(F32)

        def vbf(off, n):
            return arena[:, off:off + n]

        def MM(out, lhsT, rhs, start=True, stop=True, r=(), w=()):
            S.add("pe", lambda e: e.matmul(out, lhsT, rhs, start=start, stop=stop), r, w)

        def TR(out, in_, r=(), w=()):
            S.add("pe", lambda e: e.transpose(out, in_, cst[:, C_ID:C_ID + 128]), r, w)

        def ACT(out, in_, func, r=(), w=(), **kw):
            S.add("act", lambda e: e.activation(out, in_, func, **kw), r, w)

        def CP(eng, out, in_, r=(), w=()):
            if eng == "act":
                S.add("act", lambda e: e.copy(out, in_), r, w)
            else:
                S.add(eng, lambda e: e.tensor_copy(out, in_), r, w)

        def TS(eng, out, in0, s1, s2, op0, op1=None, r=(), w=()):
            if op1 is None:
                S.add(eng, lambda e: e.tensor_single_scalar(out, in0, s1, op0), r, w)
            else:
                S.add(eng, lambda e: e.tensor_scalar(out, in0, s1, s2, op0, op1), r, w)

        def STT(eng, out, in0, sc, in1, op0, op1, r=(), w=()):
            S.add(eng, lambda e: e.scalar_tensor_tensor(out, in0, sc, in1, op0, op1), r, w)

        def TT(eng, out, in0, in1, op, r=(), w=()):
            S.add(eng, lambda e: e.tensor_tensor(out, in0, in1, op), r, w)

        def MEMSET(eng, ap, val, r=(), w=()):
            S.add(eng, lambda e: e.memset(ap, val), r, w)

        def PK(b, lo=0, hi=512):
            return ["ps%d" % b]

        ident = cst[:, C_ID:C_ID + 128]
        ones_f = cst[:, C_ONE:C_ONE + 128]
        triu_f = cst[:, C_TRIU:C_TRIU + 128]
        sgtj_f = cst[:, C_SGTJ:C_SGTJ + 128]
        tril_f = cst[:, C_TRIL:C_TRIL + 128]
        strict_f = cst[:, C_STRICT:C_STRICT + 128]
        ones_b = cstb[:, C_ONE:C_ONE + 128]
        triu_b = cstb[:, C_TRIU:C_TRIU + 128]

        S.dma("sp", cst[:], consts_d, writes=["cst"])
        S.dma("sp", vec[:], vecs_d, writes=["vec"])
        CP("dve", cstb[:], cst[:], r=["cst"], w=["cstb"])

        NSTG, NWB = 2, 8
        stg_off = [alloc(8192), alloc(8192)]
        wb_off = [alloc(4096) for _ in range(8)]
        ring = dict(s=0, w=0, c=0, nstg=NSTG, nwb=NWB)
        cast_engs = ("act", "dve")

        def wblock(src, nk, ncols, engs=cast_engs):
            si = ring["s"] % ring["nstg"]
            wi = ring["w"] % ring["nwb"]
            ring["s"] += 1
            ring["w"] += 1
            sv = vf32(stg_off[si], nk * ncols).rearrange("p (k c) -> p k c", k=nk)
            wv = vbf(wb_off[wi], nk * ncols).rearrange("p (k c) -> p k c", k=nk)
            S.dma("sp", sv, src.rearrange("(k p) c -> p k c", p=128), writes=["stg%d" % si])
            eng = engs[ring["c"] % len(engs)]
            ring["c"] += 1
            CP(eng, wv, sv, r=["stg%d" % si], w=["wb%d" % wi])
            return wv, "wb%d" % wi

        main_o = apos[0]
        hT_o = alloc(32768)
        actT_o = alloc(FC * 2048)
        cx_o = [alloc(4096), alloc(4096)]
        cy_o = [alloc(4096), alloc(4096)]
        rstd_o = alloc(4096)
        sq_o = alloc(2048)
        scb_o = alloc(64)
        row_o = cy_o

        hT = vbf(hT_o, 16 * 1024).rearrange("p (k t) -> p k t", k=16)
        actT = vbf(actT_o, FC * 1024).rearrange("p (k t) -> p k t", k=FC)
        xTf = vf32(actT_o, 16 * 1024).rearrange("p (k t) -> p k t", k=16)
        xin = [vf32(actT_o + 32768, 2048), vf32(actT_o + 32768 + 4096, 2048)]
        cx = [vf32(o, 1024) for o in cx_o]
        cy = [vf32(o, 1024) for o in cy_o]
        rstdB = vf32(rstd_o, 1024)
        sqb = vbf(sq_o, 1024)

        def mcol(i):
            return modv[:, i * 16:(i + 1) * 16]
        Sv = [modv[:, 144 + i * 16:144 + (i + 1) * 16] for i in range(3)]
        coef = [modv[:, 144 + 48 + i * 16:144 + 48 + (i + 1) * 16] for i in range(3)]
        tmpv = modv[:, 144 + 96:144 + 112]

        def gain(i):
            return vec[:, V_G + i * 16:V_G + (i + 1) * 16]

        def phase0():
            scb = vbf(scb_o, 16)
            ACT(scb, vec[:, V_C:V_C + 16], AF.Silu, r=["vec"], w=["scb"])
            pm = PS[7]
            for cb in range(72):
                blks = [wblock(wada_d[kh * 1024:(kh + 1) * 1024, cb * 256:(cb + 1) * 256], 8, 256) for kh in range(2)]
                pr = PS[cb % 2]
                prk = PK(cb % 2, 0, 256)
                for kc in range(16):
                    wv, wk = blks[kc // 8]
                    MM(pr[0:1, 0:256], scb[:, kc:kc + 1], wv[:, kc % 8, :], start=(kc == 0), stop=(kc == 15),
                       r=["scb", wk], w=prk)
                rw = vf32(row_o[cb % 2], 256)
                rk = "row%d" % (cb % 2)
                CP("dve", rw[0:1, :], pr[0:1, 0:256], r=prk, w=[rk])
                for j in range(2):
                    MM(pm[:, cb * 2 + j:cb * 2 + j + 1], rw[0:1, j * 128:(j + 1) * 128], cst[0:1, C_ONE:C_ONE + 1],
                       r=[rk, "cst"], w=PK(7, 0, 256))
            TT("dve", modv[:, 0:144], pm[:, 0:144], vec[:, V_BADA:V_BADA + 144], ALU.add, r=PK(7, 0, 256) + ["vec"], w=["modv"])
            for i in range(3):
                TS("dve", tmpv, mcol(3 * i + 1), 1.0, None, ALU.add, r=["modv"], w=["tmpv"])
                TT("dve", Sv[i], tmpv, gain(2 * i), ALU.mult, r=["tmpv", "vec"], w=["Sv%d" % i])
                STT("dve", coef[i], mcol(3 * i + 2), (1.0 if i == 1 else 0.5), gain(2 * i + 1), ALU.mult, ALU.mult,
                    r=["modv", "vec"], w=["coef%d" % i])

        HK = ["H%d" % k for k in range(16)]
        AK = ["A%d" % k for k in range(FC)]
        XFK = lambda kc: [AK[2 * kc], AK[2 * kc + 1]]
        X0v = X0_d.rearrange("(k p) t -> p k t", p=128)
        X1v = X1_d.rearrange("(k p) t -> p k t", p=128)
        H2v = H2_d.rearrange("(k p) t -> p k t", p=128)
        Yv = Y_d.rearrange("(k p) t -> p k t", p=128)
        YFv = YF_d.rearrange("(k p) t -> p k t", p=128)

        def stat_acc(src, srckeys, first, last):
            ACT(sqb, src, AF.Square, r=srckeys, w=["sq"])
            for hf in range(2):
                MM(PS[6 + hf][:, :], ones_b, sqb[:, hf * 512:(hf + 1) * 512], start=first, stop=last,
                   r=["cstb", "sq"], w=PK(6 + hf))

        def stat_fin():
            for hf in range(2):
                ACT(rstdB[:, hf * 512:(hf + 1) * 512], PS[6 + hf][:, :], AF.Ln, r=PK(6 + hf), w=["rstdB"],
                    scale=1.0 / D, bias=EPS)
            ACT(rstdB, rstdB, AF.Exp, r=["rstdB"], w=["rstdB"], scale=-0.5)

        def mod_to_hT(i, kc, src, srckeys):
            t = cy[kc % 2]
            STT("dve", t, src, Sv[i][:, kc:kc + 1], rstdB, ALU.mult, ALU.mult,
                r=srckeys + ["Sv%d" % i, "rstdB"], w=["cy%d" % (kc % 2)])
            ACT(hT[:, kc, :], t, AF.Identity, r=["cy%d" % (kc % 2), "modv"], w=[HK[kc]], bias=mcol(3 * i)[:, kc:kc + 1])

        def prenorm_dram(i, Xsrc, ts, xkey):
            stat_fin()
            for kc in range(16):
                S.dma("sp", cx[kc % 2], Xsrc[:, kc, ts], reads=[xkey], writes=["cx%d" % (kc % 2)])
                mod_to_hT(i, kc, cx[kc % 2], ["cx%d" % (kc % 2)])

        def y_sink(dm, buf, bkey):
            stat_acc(buf, [bkey], dm == 0, dm == 15)
            S.dma("sp", YFv[:, dm, :], buf, reads=[bkey], writes=["YF%d" % dm])

        def proj_T(inT, inkeys, nk, W):
            groups = [(g, min(8, nk - g)) for g in range(0, nk, 8)]
            for db in range(8):
                for (g0, n) in groups:
                    wv, wk = wblock(W[g0 * 128:(g0 + n) * 128, db * 256:(db + 1) * 256], n, 256)
                    for sub in range(2):
                        for hf in range(2):
                            bk = 2 + sub * 2 + hf
                            for f in range(n):
                                k = g0 + f
                                MM(PS[bk][:, :], wv[:, f, sub * 128:(sub + 1) * 128], inT[:, k, hf * 512:(hf + 1) * 512],
                                   start=(k == 0), stop=(k == nk - 1), r=[wk, inkeys[k]], w=PK(bk))
                for sub in range(2):
                    dm = db * 2 + sub
                    buf, bkey = cy[dm % 2], "cy%d" % (dm % 2)
                    for hf in range(2):
                        bk = 2 + sub * 2 + hf
                        CP("act" if bk % 2 == 0 else "dve", buf[:, hf * 512:(hf + 1) * 512], PS[bk][:, :], r=PK(bk), w=[bkey])
                    y_sink(dm, buf, bkey)

        def ffn(Wg, Wu, Wd):
            for fb in range(22):
                blk = {}
                for mi, Wm in enumerate((Wg, Wu)):
                    for kh in range(2):
                        blk[(mi, kh)] = wblock(Wm[kh * 1024:(kh + 1) * 1024, fb * 256:(fb + 1) * 256], 8, 256,
                                               engs=("act", "dve"))
                for sub in range(2):
                    fc = fb * 2 + sub
                    for hf in range(2):
                        hs = slice(hf * 512, (hf + 1) * 512)
                        for mi in range(2):
                            bk = 2 * hf + mi
                            for kc in range(16):
                                wv, wk = blk[(mi, kc // 8)]
                                MM(PS[bk][:, :], wv[:, kc % 8, sub * 128:(sub + 1) * 128], hT[:, kc, hs],
                                   start=(kc == 0), stop=(kc == 15), r=[wk, HK[kc]], w=PK(bk))
                        tb, tk = cy[hf][:, 0:512], "cy%d" % hf
                        ACT(tb, PS[2 * hf][:, :], AF.Silu, r=PK(2 * hf), w=[tk])
                        TT("dve", actT[:, fc, hs], tb, PS[2 * hf + 1][:, :], ALU.mult, r=[tk] + PK(2 * hf + 1), w=[AK[fc]])
            proj_T(actT, AK, FC, Wd)

        m0c, m1c = vec[:, V_PAR:V_PAR + 1], vec[:, V_PAR + 1:V_PAR + 2]

        def post_res(i, Xsrc, ts, xkey, dst, need_stats, ts2=None):
            stat_fin()
            for kc in range(16):
                cb, ck = cx[kc % 2], "cx%d" % (kc % 2)
                yb, yk = cy[kc % 2], "cy%d" % (kc % 2)
                S.dma("sp", yb, YFv[:, kc, :], reads=["YF%d" % kc], writes=[yk])
                S.dma("sp", cb, Xsrc[:, kc, ts], reads=[xkey], writes=[ck])
                if ts2 is not None:
                    c2 = vf32(actT_o + 16384 + 2048 * (kc % 2), 1024)
                    c2k = [AK[16 + 2 * (kc % 2)], AK[17 + 2 * (kc % 2)]]
                    S.dma("sp", c2, Xsrc[:, kc, ts2], reads=[xkey], writes=c2k)
                    TS("dve", c2, c2, m1c, None, ALU.mult, r=c2k + ["vec"], w=c2k)
                    STT("dve", cb, cb, m0c, c2, ALU.mult, ALU.add, r=[ck, "vec"] + c2k, w=[ck])
                STT("dve", yb, yb, coef[i][:, kc:kc + 1], rstdB, ALU.mult, ALU.mult, r=[yk, "coef%d" % i, "rstdB"], w=[yk])
                TT("pool", cb, cb, yb, ALU.add, r=[ck, yk], w=[ck])
                if need_stats:
                    stat_acc(cb, [ck], kc == 0, kc == 15)
                dst(kc, cb, ck)

        def phase1():
            for t in range(T // 1024):
                ts = slice(t * 1024, (t + 1) * 1024)
                for j in range(8):
                    xb, xbk = xin[j % 2], AK[32 + 4 * (j % 2):36 + 4 * (j % 2)]
                    S.dma("sp", xb, x_d[t * 1024 + j * 128:t * 1024 + (j + 1) * 128, :], writes=xbk)
                    for q4 in range(4):
                        bk = 4 + q4 % 2
                        for c in range(4):
                            kc = q4 * 4 + c
                            TR(PS[bk][:, c * 128:(c + 1) * 128], xb[:, kc * 128:(kc + 1) * 128], r=["cst"] + xbk, w=PK(bk))
                        CP("act" if bk % 2 == 0 else "dve", xTf[:, q4 * 4:(q4 + 1) * 4, j * 128:(j + 1) * 128],
                           PS[bk][:, :].rearrange("p (c t) -> p c t", c=4), r=PK(bk), w=AK[8 * q4:8 * q4 + 8])
                S.dma("sp", X0v[:, :, ts], xTf, reads=AK[0:32], writes=["X0s"])
                for kc in range(16):
                    stat_acc(xTf[:, kc, :], XFK(kc), kc == 0, kc == 15)
                stat_fin()
                for kc in range(16):
                    mod_to_hT(0, kc, xTf[:, kc, :], XFK(kc))
                ffn(w1g_d, w1u_d, w1d_d)
                post_res(0, X0v, ts, "X0s", lambda kc, cb, ck: S.dma("sp", X1v[:, kc, ts], cb, reads=[ck], writes=["X1s"]), True)
                prenorm_dram(1, X1v, ts, "X1s")
                S.dma("sp", H2v[:, :, ts], hT, reads=HK, writes=["H2s"])

        def phase3():
            tiles = [0] if split else list(range(T // 1024))
            for t in tiles:
                ts = slice(t * 1024, (t + 1) * 1024)
                ts2 = slice(1024, 2048) if split else None
                S.dma("sp", hT, Yv[:, :, ts], reads=["Ys"], writes=HK)
                if split:
                    hB = vbf(actT_o, 16 * 1024).rearrange("p (k t) -> p k t", k=16)
                    S.dma("sp", hB, Yv[:, :, ts2], reads=["Ys"], writes=AK[0:16])
                    for kc in range(16):
                        TS("dve", hB[:, kc, :], hB[:, kc, :], m1c, None, ALU.mult, r=[AK[kc], "vec"], w=[AK[kc]])
                        STT("dve", hT[:, kc, :], hT[:, kc, :], m0c, hB[:, kc, :], ALU.mult, ALU.add,
                            r=[HK[kc], "vec", AK[kc]], w=[HK[kc]])
                proj_T(hT, HK, 16, wout_d)
                post_res(1, X1v, ts, "X1s", lambda kc, cb, ck: S.dma("sp", X0v[:, kc, ts], cb, reads=[ck], writes=["X0s"]), True,
                         ts2=ts2)
                prenorm_dram(2, X0v, ts, "X0s")
                ffn(w2g_d, w2u_d, w2d_d)
                post_res(2, X0v, ts, "X0s",
                         lambda kc, cb, ck: CP("act", xTf[:, kc, :], cb, r=[ck], w=XFK(kc)), False)
                for j in range(8):
                    xb, xbk = xin[j % 2], AK[32 + 4 * (j % 2):36 + 4 * (j % 2)]
                    for q4 in range(4):
                        bk = 4 + q4 % 2
                        for c in range(4):
                            kc = q4 * 4 + c
                            TR(PS[bk][:, c * 128:(c + 1) * 128], xTf[:, kc, j * 128:(j + 1) * 128], r=["cst"] + XFK(kc), w=PK(bk))
                        CP("act" if bk % 2 == 0 else "dve", xb[:, q4 * 512:(q4 + 1) * 512], PS[bk][:, :], r=PK(bk), w=xbk)
                    S.dma("sp", out_d[t * 1024 + j * 128:t * 1024 + (j + 1) * 128, :], xb, reads=xbk, writes=["out"])

        def phase2():
            ring["nstg"], ring["nwb"] = 2, 4
            h2_o = [wb_off[4], None]
            apos[0] = main_o
            h2_o[1] = alloc(16384)
            raw_o = [alloc(4 * T), alloc(4 * T), alloc(4 * T), alloc(4 * T)]
            cb_o = alloc(4 * T)
            Kt_o, Vt_o = alloc(4 * T), alloc(4 * T)
            AT_o, wT_o, u_o = alloc(4 * T), alloc(4 * T), alloc(4 * T)
            o_o = raw_o[2]
            G = 2
            un_o = [dict(B=alloc(512), dec=alloc(512), decS=alloc(512), decT=alloc(512), N=alloc(512), A=alloc(512),
                         XY=[alloc(512) for _ in range(4)], R=[alloc(1024), alloc(1024)]) for _ in range(G)]
            S_o = [alloc(512), alloc(512)]
            vn_o = [alloc(512), alloc(512)]
            kd_o = [alloc(512), alloc(512)]
            ot_o = [alloc(512), alloc(512)]
            on_o = [alloc(512), alloc(512)]
            yh_o = [alloc(2 * T)] * 2
            ab_o = alloc(NTL * 16 * 4)
            sm_o = {k: alloc(NTL * 8 * 4) for k in ("gcol", "beta", "nbeta", "Gc", "gl", "eG", "edec", "egl", "bEG", "t1", "t2")}
            wab_o = alloc(16 * 16 * 2)
            ss_o = alloc(64 * 4)
            km_o = alloc(64)
            gate_o = alloc(NTL * 8 * 4)
            gtmp_o = [alloc(64) for _ in range(4)]
            seln_o = alloc(NTL * 8 * 4)
            pt_o = [alloc(256) for _ in range(4)]
            rl_o = [alloc(512), alloc(512)]
            qb_o, kb_o, vtk_o = AT_o, wT_o, u_o
            if T >= 2048:
                selT_o = Vt_o + 2048
                rt_o = [Vt_o, Vt_o + 1024]
            else:
                selT_o = alloc(2 * T)
                rt_o = [alloc(2048), alloc(2048)]

            h2t = [vbf(o, 16 * 512).rearrange("p (k t) -> p k t", k=16) for o in h2_o]
            raw = [vf32(o, T) for o in raw_o]
            cbuf = vf32(cb_o, T)
            Kt = vf32(Kt_o, NTL * 128).rearrange("p (n d) -> p n d", n=NTL)
            Vt = vf32(Vt_o, NTL * 128).rearrange("p (n d) -> p n d", n=NTL)
            ATa = vf32(AT_o, NTL * 128).rearrange("p (n d) -> p n d", n=NTL)
            wTa = vf32(wT_o, NTL * 128).rearrange("p (n d) -> p n d", n=NTL)
            ua = vf32(u_o, NTL * 128).rearrange("p (n d) -> p n d", n=NTL)
            oa = vf32(o_o, NTL * 128).rearrange("p (n d) -> p n d", n=NTL)
            sm = {k: vf32(o, NTL * 8).rearrange("p (n h) -> p n h", n=NTL) for k, o in sm_o.items()}
            ab = vf32(ab_o, NTL * 16).rearrange("p (n c) -> p n c", n=NTL)
            wab = vbf(wab_o, 256).rearrange("p (k c) -> p k c", k=16)
            ssv = vf32(ss_o, 64)

            def h2load(tt, slot):
                S.dma("sp", h2t[slot], H2v[:, :, tt * 512:(tt + 1) * 512], reads=["H2s"], writes=["h2t%d" % slot])

            sv = vf32(stg_off[0], 256).rearrange("p (k c) -> p k c", k=16)
            S.dma("sp", sv, win_d[:, 4096:4112].rearrange("(k p) c -> p k c", p=128), writes=["stg0"])
            CP("dve", wab, sv, r=["stg0"], w=["wab"])
            for tt in range(NT):
                h2load(tt, tt % 2)
                for sub in range(4):
                    tl = tt * 4 + sub
                    for kc in range(16):
                        MM(PS[0][:, tl * 16:(tl + 1) * 16], h2t[tt % 2][:, kc, sub * 128:(sub + 1) * 128], wab[:, kc, :],
                           start=(kc == 0), stop=(kc == 15), r=["h2t%d" % (tt % 2), "wab"], w=PK(0, 0, 256))
            CP("dve", ab.rearrange("p n c -> p (n c)"), PS[0][:, 0:NTL * 16], r=PK(0, 0, 256), w=["ab"])
            flat = lambda k: sm[k].rearrange("p n h -> p (n h)")
            alog = vec[:, V_ALOG:V_ALOG + 128].rearrange("p (n h) -> p n h", n=16)[:, 0:NTL, :]
            dtb = vec[:, V_DTB:V_DTB + 128].rearrange("p (n h) -> p n h", n=16)[:, 0:NTL, :]
            TT("dve", sm["t1"], ab[:, :, 0:8], dtb, ALU.add, r=["ab", "vec"], w=["t1"])
            ACT(flat("t1"), flat("t1"), AF.Exp, r=["t1"], w=["t1"])
            ACT(flat("t1"), flat("t1"), AF.Ln, r=["t1"], w=["t1"], bias=1.0)
            ACT(sm["t2"], alog, AF.Exp, r=["vec"], w=["t2"])
            STT("dve", flat("gcol"), flat("t1"), -1.0, flat("t2"), ALU.mult, ALU.mult, r=["t1", "t2"], w=["gcol"])
            ACT(sm["beta"], ab[:, :, 8:16], AF.Sigmoid, r=["ab"], w=["beta"])
            TS("dve", flat("nbeta"), flat("beta"), -1.0, None, ALU.mult, r=["beta"], w=["nbeta"])
            for tl in range(NTL):
                MM(PS[1][:, tl * 8:(tl + 1) * 8], triu_f, sm["gcol"][:, tl, :], r=["cst", "gcol"], w=PK(1, 0, 128))
                MM(PS[2][:, tl * 8:(tl + 1) * 8], ones_f, sm["gcol"][:, tl, :], r=["cst", "gcol"], w=PK(2, 0, 128))
            CP("dve", flat("Gc"), PS[1][:, 0:NTL * 8], r=PK(1, 0, 128), w=["Gc"])
            CP("dve", flat("gl"), PS[2][:, 0:NTL * 8], r=PK(2, 0, 128), w=["gl"])
            ACT(flat("eG"), flat("Gc"), AF.Exp, r=["Gc"], w=["eG"])
            ACT(flat("egl"), flat("gl"), AF.Exp, r=["gl"], w=["egl"])
            TT("dve", flat("t1"), flat("gl"), flat("Gc"), ALU.subtract, r=["gl", "Gc", "t1"], w=["t1"])
            ACT(flat("edec"), flat("t1"), AF.Exp, r=["t1"], w=["edec"])
            TT("dve", flat("bEG"), flat("beta"), flat("eG"), ALU.mult, r=["beta", "eG"], w=["bEG"])

            if lvl < 1:
                ring["nstg"], ring["nwb"] = NSTG, NWB
                return

            def inproj(cols, nchunk):
                blks = [wblock(win_d[:, c0:c0 + 128], 16, 128, engs=("act",)) for c0 in cols]
                for tt in range(NT):
                    h2load(tt, tt % 2)
                    for ci in range(nchunk):
                        wv, wk = blks[ci]
                        for kc in range(16):
                            MM(PS[ci][:, :], wv[:, kc, :], h2t[tt % 2][:, kc, :], start=(kc == 0), stop=(kc == 15),
                               r=[wk, "h2t%d" % (tt % 2)], w=PK(ci))
                        CP("act" if ci % 2 else "dve", raw[ci][:, tt * 512:(tt + 1) * 512], PS[ci][:, :],
                           r=PK(ci), w=["raw%d" % ci])

            def to_tokmajor(src, sk, dstflat, dk_):
                for g4 in range(NTL // 4):
                    b = 6 + g4 % 2
                    for c in range(4):
                        tl = g4 * 4 + c
                        TR(PS[b][:, c * 128:(c + 1) * 128], src[:, tl * 128:(tl + 1) * 128], r=["cst", sk],
                           w=PK(b, c * 128, (c + 1) * 128))
                    CP("act" if g4 % 2 else "dve", dstflat(g4), PS[b][:, :], r=PK(b), w=[dk_])

            def gdn_head(h):
                inproj([h * 128, 1024 + h * 128, 2048 + h * 128, 3072 + h * 128], 4)
                for ci in range(3):
                    eng = "dve"
                    cw0 = V_CONV + (ci * 8 + h) * 4
                    x = raw[ci]
                    rk = "raw%d" % ci
                    TS(eng, cbuf[:, :], x[:, :], vec[:, cw0 + 3:cw0 + 4], None, ALU.mult, r=[rk, "vec"], w=["cbuf"])
                    for sft in (1, 2, 3):
                        STT(eng, cbuf[:, sft:T], x[:, 0:T - sft], vec[:, cw0 + 3 - sft:cw0 + 4 - sft], cbuf[:, sft:T],
                            ALU.mult, ALU.add, r=[rk, "vec", "cbuf"], w=["cbuf"])
                    ACT(x[:, :], cbuf[:, :], AF.Silu, r=["cbuf"], w=[rk])
                    if ci < 2:
                        ACT(cbuf[:, :], x[:, :], AF.Square, r=[rk], w=["cbuf"])
                        for tt in range(NT):
                            MM(PS[tt][:, :], ones_f, cbuf[:, tt * 512:(tt + 1) * 512], r=["cst", "cbuf"], w=PK(tt))
                        for tt in range(NT):
                            ACT(cbuf[:, tt * 512:(tt + 1) * 512], PS[tt][:, :], AF.Ln, r=PK(tt), w=["cbuf"], bias=EPS)
                        ACT(cbuf[:, :], cbuf[:, :], AF.Exp, r=["cbuf"], w=["cbuf"], scale=-0.5)
                        if ci == 0:
                            STT("dve", x[:, :], x[:, :], 128.0 ** -0.5, cbuf[:, :], ALU.mult, ALU.mult, r=[rk, "cbuf"], w=[rk])
                        else:
                            TT("dve", x[:, :], x[:, :], cbuf[:, :], ALU.mult, r=[rk, "cbuf"], w=[rk])
                ACT(raw[3][:, :], raw[3][:, :], AF.Silu, r=["raw3"], w=["raw3"])
                TS("pool", raw[3][:, :], raw[3][:, :], vec[:, V_NG:V_NG + 1], None, ALU.mult, r=["raw3", "vec"], w=["raw3"])
                qT, kT, vT, gz = raw
                if lvl < 2:
                    return
                to_tokmajor(kT, "raw1", lambda g4: Kt[:, g4 * 4:(g4 + 1) * 4, :].rearrange("p n d -> p (n d)"), "Kt")
                to_tokmajor(vT, "raw2", lambda g4: Vt[:, g4 * 4:(g4 + 1) * 4, :].rearrange("p n d -> p (n d)"), "Vt")
                if lvl < 2.5:
                    return
                for g0 in range(0, NTL, G):
                    units = list(range(g0, min(NTL, g0 + G)))
                    U = {}
                    for tl in units:
                        u = tl % G
                        o = un_o[u]
                        U[tl] = dict(
                            B=vf32(o["B"], 128), dec=vf32(o["dec"], 128), decS=vf32(o["decS"], 128), decT=vf32(o["decT"], 128),
                            N=vf32(o["N"], 128), A=vf32(o["A"], 128), XY=[vf32(q, 128) for q in o["XY"]],
                            R=[vf32(q, 256) for q in o["R"]], pa=PS[2 * u][:, 0:128], pb=PS[2 * u][:, 128:256], pc=PS[2 * u + 1][:, 0:256],
                            pak=PK(2 * u, 0, 128), pbk=PK(2 * u, 128, 256), pck=PK(2 * u + 1, 0, 256),
                            k=(lambda s_, u=u: "u%d%s" % (u, s_)), tsl=slice(tl * 128, (tl + 1) * 128))
                    for tl in units:
                        d = U[tl]; k = d["k"]
                        TS("dve", d["B"], sgtj_f, sm["gcol"][:, tl, h:h + 1], None, ALU.mult, r=["cst", "gcol"], w=[k("B")])
                        MM(d["pa"], triu_f, d["B"], r=["cst", k("B")], w=d["pak"])
                        ACT(d["dec"], d["pa"], AF.Exp, r=d["pak"], w=[k("dec")])
                        TS("pool", d["R"][0][:, 0:128], Vt[:, tl, :], sm["beta"][:, tl, h:h + 1], None, ALU.mult,
                           r=["Vt", "beta"], w=[k("R0")])
                        TS("pool", d["R"][0][:, 128:256], Kt[:, tl, :], sm["bEG"][:, tl, h:h + 1], None, ALU.mult,
                           r=["Kt", "bEG", k("R0")], w=[k("R0")])
                        if lvl < 2.6:
                            continue
                        MM(d["pa"], kT[:, d["tsl"]], kT[:, d["tsl"]], r=["raw1"], w=d["pak"])
                        MM(d["pb"], qT[:, d["tsl"]], kT[:, d["tsl"]], r=["raw0", "raw1"], w=d["pbk"])
                        TT("dve", d["decS"], d["dec"], strict_f, ALU.mult, r=[k("dec"), "cst"], w=[k("decS")])
                        TT("dve", d["decT"], d["dec"], tril_f, ALU.mult, r=[k("dec"), "cst"], w=[k("decT")])
                        STT("dve", d["N"], d["pa"], sm["nbeta"][:, tl, h:h + 1], d["decS"], ALU.mult, ALU.mult,
                            r=d["pak"] + ["nbeta", k("decS")], w=[k("N")])
                        TT("dve", d["A"], d["pb"], d["decT"], ALU.mult, r=d["pbk"] + [k("decT")], w=[k("A")])
                        if lvl < 2.7:
                            continue
                        TR(d["pa"], d["N"], r=["cst", k("N")], w=d["pak"])
                        TR(d["pb"], d["A"], r=["cst", k("A")], w=d["pbk"])
                        CP("act", d["XY"][1], d["pa"], r=d["pak"], w=[k("XY1")])
                        CP("act", ATa[:, tl, :], d["pb"], r=d["pbk"], w=["ATa"])
                        if lvl < 2.78:
                            continue
                    if lvl < 2.8:
                        continue
                    nlv = 7 if lvl >= 3 else int(round((lvl - 2.8) * 100))
                    for lv in range(nlv):
                        for tl in units:
                            d = U[tl]; k = d["k"]
                            if lv == 0:
                                X, Xk, Y, Yk = d["N"], k("N"), d["XY"][1], k("XY1")
                            else:
                                xi, yi = (0, 1) if lv % 2 == 0 else (2, 3)
                                X, Xk, Y, Yk = d["XY"][xi], k("XY%d" % xi), d["XY"][yi], k("XY%d" % yi)
                            Rc, Rck = d["R"][lv % 2], k("R%d" % (lv % 2))
                            cv = "rs"
                            if "r" in cv:
                                MM(d["pc"], Y, Rc, r=[Yk, Rck], w=d["pck"])
                            if lv < 6 or nlv < 7:
                                Rn, Rnk = d["R"][(lv + 1) % 2], k("R%d" % ((lv + 1) % 2))
                                if "r" in cv:
                                    TT("dve", Rn, Rc, d["pc"], ALU.add, r=[Rck] + d["pck"], w=[Rnk])
                                nxi, nyi = (0, 1) if (lv + 1) % 2 == 0 else (2, 3)
                                if "s" in cv:
                                    MM(d["pa"], Y, X, r=[Yk, Xk], w=d["pak"])
                                    MM(d["pb"], X, Y, r=[Yk, Xk], w=d["pbk"])
                                    CP("dve", d["XY"][nxi], d["pa"], r=d["pak"], w=[k("XY%d" % nxi)])
                                    CP("dve", d["XY"][nyi], d["pb"], r=d["pbk"], w=[k("XY%d" % nyi)])
                            else:
                                TT("dve", ua[:, tl, :], Rc[:, 0:128], d["pc"][:, 0:128], ALU.add, r=[Rck] + d["pck"], w=["ua"])
                                TT("dve", d["A"], Rc[:, 128:256], d["pc"][:, 128:256], ALU.add, r=[Rck] + d["pck"], w=[k("A")])
                                TR(d["pa"], d["A"], r=["cst", k("A")], w=d["pak"])
                                CP("act", wTa[:, tl, :], d["pa"], r=d["pak"], w=["wTa"])
                if lvl < 4:
                    return
                Sst = [vf32(o, 128) for o in S_o]
                MEMSET("pool", Sst[0], 0.0, w=["S0"])
                for tl in range(NTL):
                    cur, nxt = tl % 2, (tl + 1) % 2
                    b = 4 + tl % 2
                    pb_ = PS[b]
                    tsl = slice(tl * 128, (tl + 1) * 128)
                    vn, kd, ot = vf32(vn_o[tl % 2], 128), vf32(kd_o[tl % 2], 128), vf32(ot_o[tl % 2], 128)
                    vk, kk, ok_ = "vn%d" % (tl % 2), "kd%d" % (tl % 2), "ot%d" % (tl % 2)
                    MM(pb_[:, 0:128], wTa[:, tl, :], Sst[cur], r=["wTa", "S%d" % cur], w=PK(b, 0, 128))
                    TT("dve", vn, ua[:, tl, :], pb_[:, 0:128], ALU.subtract, r=["ua"] + PK(b, 0, 128), w=[vk])
                    MM(pb_[:, 128:256], qT[:, tsl], Sst[cur], r=["raw0", "S%d" % cur], w=PK(b, 128, 256))
                    MM(pb_[:, 256:384], ATa[:, tl, :], vn, r=["ATa", vk], w=PK(b, 256, 384))
                    TS("dve", ot, pb_[:, 128:256], sm["eG"][:, tl, h:h + 1], None, ALU.mult, r=PK(b, 128, 256) + ["eG"], w=[ok_])
                    TT("dve", oa[:, tl, :], ot, pb_[:, 256:384], ALU.add, r=[ok_] + PK(b, 256, 384), w=["raw2"])
                    TS("pool", kd, Kt[:, tl, :], sm["edec"][:, tl, h:h + 1], None, ALU.mult, r=["Kt", "edec"], w=[kk])
                    MM(pb_[:, 384:512], kd, vn, r=[kk, vk], w=PK(b, 384, 512))
                    STT("dve", Sst[nxt], Sst[cur], sm["egl"][:, tl, h:h + 1], pb_[:, 384:512], ALU.mult, ALU.add,
                        r=["S%d" % cur, "egl"] + PK(b, 384, 512), w=["S%d" % nxt])
                if lvl < 5:
                    return
                yh = vbf(yh_o[h % 2], T)
                yk = "yh"
                ACT(cbuf[:, :], oa.rearrange("p n d -> p (n d)"), AF.Square, r=["raw2"], w=["cbuf"])
                S.add("dve", lambda e: e.tensor_reduce(ssv[:, 0:NTL], cbuf[:, :].rearrange("p (n d) -> p n d", n=NTL), AX.X, ALU.add),
                      ["cbuf"], ["ssv"])
                ACT(ssv[:, 0:NTL], ssv[:, 0:NTL], AF.Ln, r=["ssv"], w=["ssv"], scale=1.0 / 128, bias=EPS)
                ACT(ssv[:, 0:NTL], ssv[:, 0:NTL], AF.Exp, r=["ssv"], w=["ssv"], scale=-0.5)
                for tl in range(NTL):
                    on = vf32(on_o[tl % 2], 128)
                    onk = "on%d" % (tl % 2)
                    b = 6 + tl % 2
                    TS("dve", on, oa[:, tl, :], ssv[:, tl:tl + 1], None, ALU.mult, r=["raw2", "ssv"], w=[onk])
                    TR(PS[b][:, 0:128], on, r=["cst", onk], w=PK(b, 0, 128))
                    TT("dve", yh[:, tl * 128:(tl + 1) * 128], PS[b][:, 0:128], gz[:, tl * 128:(tl + 1) * 128], ALU.mult,
                       r=PK(b, 0, 128) + ["raw3"], w=[yk])
                S.dma("sp", Y_d[h * 128:(h + 1) * 128, :], yh, reads=[yk], writes=["Ys"])

            def moba_head(m):
                inproj([4112 + m * 128, 5136 + m * 128, 6160 + m * 128], 3)
                rq, rk_, rv = raw[0], raw[1], raw[2]
                cosv = cbuf[0:32, 0:T]
                sinv = vf32(Kt_o, T)[0:32, :]
                pmf = cst[0:32, C_PM:C_PM + 32]
                for ci in range(2):
                    x = raw[ci]
                    xk = "raw%d" % ci
                    for tt in range(NT):
                        cs = slice(tt * 512, (tt + 1) * 512)
                        b = 4 + tt % 2
                        t1 = vf32(rt_o[0], 512)
                        t2 = vf32(rt_o[1], 512)
                        MM(PS[b][0:32, :], pmf, x[0:32, cs], r=["cst", xk], w=PK(b))
                        TT("dve", t1[0:32, :], x[0:32, cs], cosv[:, cs], ALU.mult, r=[xk, "cbuf"], w=["Vt"])
                        TT("dve", t2[0:32, :], PS[b][0:32, :], sinv[:, cs], ALU.mult, r=PK(b) + ["Kt", "Vt"], w=["Vt"])
                        TT("pool", x[0:32, cs], t1[0:32, :], t2[0:32, :], ALU.add, r=["Vt", xk], w=[xk])
                km = vf32(km_o, 8)
                S.add("dve", lambda e: e.tensor_reduce(km[:, 0:NB], rk_[:, :].rearrange("p (n t) -> p n t", n=NB), AX.X, ALU.add),
                      ["raw1"], ["km"])
                TS("dve", km[:, 0:NB], km[:, 0:NB], 1.0 / 256, None, ALU.mult, r=["km"], w=["km"])
                gate = vf32(gate_o, NTL * 8).rearrange("p (n b) -> p n b", n=NTL)
                seln = vf32(seln_o, NTL * 8).rearrange("p (n b) -> p n b", n=NTL)
                for tl in range(NTL):
                    MM(PS[5][:, tl * 8:tl * 8 + NB], rq[:, tl * 128:(tl + 1) * 128], km[:, 0:NB], r=["raw0", "km"], w=PK(5, 0, 128))
                CP("dve", gate[:, :, 0:NB], PS[5][:, 0:NTL * 8].rearrange("p (n b) -> p n b", b=8)[:, :, 0:NB], r=PK(5, 0, 128), w=["gate"])
                MEMSET("pool", seln.rearrange("p n b -> p (n b)"), 0.0, w=["seln"])
                gt = [vf32(o, 8) for o in gtmp_o]
                for tl in range(NTL):
                    own = tl // 2
                    if own <= 3:
                        continue
                    g = gate[:, tl, 0:own]
                    cur = g
                    ck = "gate"
                    for it in range(3):
                        S.add("dve", lambda e, cur=cur, it=it: e.tensor_reduce(gt[3][:, it:it + 1], cur, AX.X, ALU.max),
                              [ck, "gt3"], ["gt3"])
                        if it < 2:
                            TS("dve", gt[2][:, 0:own], cur, gt[3][:, it:it + 1], -1e30, ALU.is_ge, ALU.mult, r=[ck, "gt3"], w=["gt2"])
                            TT("dve", gt[it][:, 0:own], cur, gt[2][:, 0:own], ALU.add, r=[ck, "gt2"], w=["gt%d" % it])
                            cur = gt[it][:, 0:own]
                            ck = "gt%d" % it
                    TS("dve", seln[:, tl, 0:own], g, gt[3][:, 2:3], NEG, ALU.is_lt, ALU.mult, r=["gate", "gt3", "seln"], w=["seln"])
                selT = vbf(selT_o, T)
                for g4 in range(NTL // 4):
                    b = 6 + g4 % 2
                    for c in range(4):
                        tl = g4 * 4 + c
                        TR(PS[b][0:8, c * 128:(c + 1) * 128], seln[:, tl, :], r=["cst", "seln"], w=PK(b, c * 128, (c + 1) * 128))
                    CP("dve", selT[0:8, g4 * 512:(g4 + 1) * 512], PS[b][0:8, :], r=PK(b), w=["Vt"])
                qb, kb = vbf(qb_o, T), vbf(kb_o, T)
                CP("act", qb, rq[:, :], r=["raw0"], w=["ATa"])
                CP("dve", kb, rk_[:, :], r=["raw1"], w=["wTa"])
                vtk = vbf(vtk_o, NTL * 128).rearrange("p (n d) -> p n d", n=NTL)
                to_tokmajor(rv, "raw2", lambda g4: vtk[:, g4 * 4:(g4 + 1) * 4, :].rearrange("p n d -> p (n d)"), "ua")
                yh = vbf(yh_o[m % 2], T)
                yk = "yh"
                eb = cstb[0:8, C_EBLK:C_EBLK + 1024].rearrange("p (n k) -> p n k", n=8)
                items = []
                for i in range(NTL):
                    own = i // 2
                    keys = [(kt, "sel") for kt in range(2 * own)]
                    if i % 2:
                        keys.append((i - 1, "full"))
                    keys.append((i, "diag"))
                    for n_, (kt, kind) in enumerate(keys):
                        items.append((i, kt, kind, n_ == 0, n_ == len(keys) - 1))
                LA = 2
                ppb = (0, 1, 4)

                def stage12(idx):
                    i, kt, kind, first, last = items[idx]
                    qs = slice(i * 128, (i + 1) * 128)
                    bk = ppb[idx % 3]
                    pp = PS[bk][:, 0:128]
                    ptb, ptk = vbf(pt_o[idx % 4], 128), "pt%d" % (idx % 4)
                    MM(pp, kb[:, kt * 128:(kt + 1) * 128], qb[:, qs], start=True, stop=(kind != "sel"),
                       r=["wTa", "ATa"], w=PK(bk))
                    if kind == "sel":
                        MM(pp, eb[:, kt // 2, :], selT[0:8, qs], start=False, stop=True, r=["cstb", "Vt"], w=PK(bk))
                    ACT(ptb, pp, AF.Exp, r=PK(bk), w=[ptk], scale=128.0 ** -0.5)
                    if kind == "diag":
                        TT("pool", ptb, ptb, triu_b, ALU.mult, r=[ptk, "cstb"], w=[ptk])

                def stage3(idx):
                    i, kt, kind, first, last = items[idx]
                    qs = slice(i * 128, (i + 1) * 128)
                    ptb, ptk = vbf(pt_o[idx % 4], 128), "pt%d" % (idx % 4)
                    bo, bl = (2, 3) if i % 2 == 0 else (6, 7)
                    po, pl = PS[bo][:, 0:128], PS[bl][:, 0:128]
                    MM(po, vtk[:, kt, :], ptb, start=first, stop=last, r=["ua", ptk], w=PK(bo))
                    MM(pl, ones_b, ptb, start=first, stop=last, r=["cstb", ptk], w=PK(bl))
                    if last:
                        rl = vf32(rl_o[i % 2], 128)
                        rlk = "rl%d" % (i % 2)
                        S.add("dve", lambda e, rl=rl, pl=pl: e.reciprocal(rl, pl), PK(bl), [rlk])
                        TT("dve", yh[:, qs], po, rl, ALU.mult, r=PK(bo) + [rlk], w=[yk])

                for idx in range(len(items) + LA):
                    if idx < len(items):
                        stage12(idx)
                    if idx - LA >= 0:
                        stage3(idx - LA)
                S.dma("sp", Y_d[1024 + m * 128:1024 + (m + 1) * 128, :], yh, reads=[yk], writes=["Ys"])

            for h in range(ngh):
                if gdn_on:
                    gdn_head(h)
            if moba_on:
                S.dma("sp", cbuf[0:32, 0:T], rope_d[:, 0:T], writes=["cbuf"])
                S.dma("sp", vf32(Kt_o, T)[0:32, :], rope_d[:, T:2 * T], writes=["Kt"])
                for m in range(nmh):
                    moba_head(m)
            ring["nstg"], ring["nwb"] = NSTG, NWB

        if 0 in phases:
            phase0()
        if 1 in phases:
            phase1()
        if 2 in phases:
            S.barrier()
            phase2()
            S.barrier()
        if 3 in phases:
            phase3()
        S.finalize(st)
        S.run_block()
    return nc


def host_consts(T):
    c = np.zeros((128, NCONST), np.float32)
    i = np.arange(128)
    c[:, C_ID:C_ID + 128] = np.eye(128)
    c[:, C_ONE:C_ONE + 128] = 1.0
    c[:, C_TRIU:C_TRIU + 128] = (i[:, None] <= i[None, :])
    c[:, C_SGTJ:C_SGTJ + 128] = (i[:, None] > i[None, :])
    c[:, C_TRIL:C_TRIL + 128] = (i[:, None] >= i[None, :])
    c[:, C_STRICT:C_STRICT + 128] = (i[:, None] > i[None, :])
    for m in range(16):
        c[m + 16, C_PM + m] = -1.0
        c[m, C_PM + m + 16] = 1.0
    for n in range(8):
        c[n, C_EBLK + n * 128:C_EBLK + (n + 1) * 128] = 1.0
    inv = 500000.0 ** (-np.arange(0, 32, 2, dtype=np.float32) / 32)
    ang = np.arange(T, dtype=np.float32)[None, :] * inv[:, None].astype(np.float32)
    rope = np.zeros((32, 2 * T), np.float32)
    rope[0:16, 0:T] = np.cos(ang)
    rope[16:32, 0:T] = np.cos(ang)
    rope[0:16, T:] = np.sin(ang)
    rope[16:32, T:] = np.sin(ang)
    return c, rope


def col16(v):
    return np.ascontiguousarray(np.asarray(v, np.float32).reshape(-1, 128).T)


def host_vecs(inp, b, parity=0):
    v = np.zeros((128, NVEC), np.float32)
    v[:, V_C:V_C + 16] = col16(inp["c"][b])
    v[:, V_BADA:V_BADA + 144] = col16(inp["b_ada"][0])
    for i, nm in enumerate(("ffn1_pre_g", "ffn1_post_g", "mix_pre_g", "mix_post_g", "ffn2_pre_g", "ffn2_post_g")):
        v[:, V_G + i * 16:V_G + (i + 1) * 16] = col16(inp[nm][0])
    cw = np.asarray(inp["gdn_conv_w"][0], np.float32)
    v[:, V_CONV:V_CONV + 96] = cw.T.reshape(24, 128, 4).transpose(1, 0, 2).reshape(128, 96)
    v[:, V_ALOG:V_ALOG + 128] = np.tile(np.asarray(inp["gdn_a_log"][0], np.float32), 16)[None, :]
    v[:, V_DTB:V_DTB + 128] = np.tile(np.asarray(inp["gdn_dt_bias"][0], np.float32), 16)[None, :]
    v[:, V_NG] = np.asarray(inp["gdn_norm_g"][0], np.float32)
    v[:, V_PAR + parity] = 1.0
    return v


_NC_CACHE = {}


def kernel(**inp):
    inp = {k: np.asarray(v) for k, v in inp.items()}
    B, T, _ = inp["x"].shape
    if T not in _NC_CACHE:
        _NC_CACHE[T] = build_nc(T)
    nc = _NC_CACHE[T]
    consts, rope = host_consts(T)
    shared = dict(consts=consts, rope=rope, w_ada=np.ascontiguousarray(inp["w_ada"][0]),
                  w1g=np.ascontiguousarray(inp["ffn1_w_gate"][0]), w1u=np.ascontiguousarray(inp["ffn1_w_up"][0]),
                  w1d=np.ascontiguousarray(inp["ffn1_w_down"][0]), w2g=np.ascontiguousarray(inp["ffn2_w_gate"][0]),
                  w2u=np.ascontiguousarray(inp["ffn2_w_up"][0]), w2d=np.ascontiguousarray(inp["ffn2_w_down"][0]),
                  w_in=np.ascontiguousarray(inp["w_in"][0]), w_out=np.ascontiguousarray(inp["w_out"][0]))
    in_maps = []
    for core in range(8):
        b = core // 2
        m = dict(shared)
        m["x"] = np.ascontiguousarray(inp["x"][b])
        m["vecs"] = host_vecs(inp, b, core % 2)
        in_maps.append(m)
    res = run_bass_kernel_spmd(nc, in_maps, core_ids=list(range(8)))
    if T >= 2048:
        out = np.stack([np.concatenate([np.asarray(res.results[2 * b + p]["out"], np.float32) for p in range(2)], axis=0)
                        for b in range(B)], axis=0)
    else:
        out = np.stack([np.asarray(res.results[2 * b]["out"], np.float32) for b in range(B)], axis=0)
    return out
```

```python
import contextlib
import numpy as np
import concourse.bass as bass
import concourse.mybir as mybir
from concourse.bass_utils import run_bass_kernel_spmd

F32 = mybir.dt.float32
BF16 = mybir.dt.bfloat16
AF = mybir.ActivationFunctionType
ALU = mybir.AluOpType
AX = mybir.AxisListType

ENGS = ("pe", "act", "dve", "pool", "sp")

D = 2048
KC = 16
FF = 5632
FC = 44
NMOD = 9
INC = 7184
EPS = 1e-6
NEG = -30000.0


class Sched:
    def __init__(self, nc, n_dma_sems=48):
        self.nc = nc
        self.ops = []
        self.last_w = {}
        self.readers = {}
        self.n_dma_sems = n_dma_sems

    def add(self, eng, emit, reads=(), writes=(), dma=False):
        idx = len(self.ops)
        deps = set()
        for b in reads:
            if b in self.last_w:
                deps.add(self.last_w[b])
        for b in writes:
            if b in self.last_w:
                deps.add(self.last_w[b])
            rd = self.readers.get(b)
            if rd:
                deps.update(rd.values())
        deps.discard(idx)
        self.ops.append(dict(eng=eng, emit=emit, deps=deps, dma=dma, barrier=False))
        for b in reads:
            self.readers.setdefault(b, {})[("dma", idx) if dma else eng] = idx
        for b in writes:
            self.last_w[b] = idx
            self.readers[b] = {}
        return idx

    def barrier(self):
        self.ops.append(dict(eng=None, emit=None, deps=set(), dma=False, barrier=True))
        self.last_w = {}
        self.readers = {}

    def dma(self, q, out, in_, reads=(), writes=()):
        return self.add(q, lambda e: e.dma_start(out=out, in_=in_), reads, writes, dma=True)

    def finalize(self, stack):
        nc = self.nc
        ops = self.ops
        dma_count = 0
        for op in ops:
            op["signal"] = False
            if op["dma"]:
                op["dma_ord"] = dma_count
                dma_count += 1
        last_c = {}
        for i, op in enumerate(ops):
            if op["barrier"]:
                for e, j in last_c.items():
                    ops[j]["signal"] = True
                op["last_c"] = dict(last_c)
                continue
            for d in op["deps"]:
                od = ops[d]
                if od["dma"]:
                    continue
                if not (od["eng"] == "pe" and op["eng"] == "pe"):
                    od["signal"] = True
            if not op["dma"]:
                last_c[op["eng"]] = i
        sigcount = {e: 0 for e in ENGS}
        for op in ops:
            if op["barrier"] or op["dma"]:
                continue
            if op["signal"]:
                sigcount[op["eng"]] += 1
                op["sigval"] = sigcount[op["eng"]]
        esem = {e: stack.enter_context(nc.semaphore("s_" + e)) for e in ENGS}
        nd = min(self.n_dma_sems, max(1, dma_count))
        dsem = [stack.enter_context(nc.semaphore("d%d" % i)) for i in range(nd)]
        for op in ops:
            if op["dma"]:
                k = op["dma_ord"]
                op["dsem"] = dsem[k % nd]
                op["dval"] = 16 * (k // nd + 1)
                op["dslot"] = k % nd
        waited = {e: {} for e in ENGS}
        per_eng = {e: [] for e in ENGS}
        last_on_slot = {}
        pending = {e: [] for e in ENGS}
        for i, op in enumerate(ops):
            if op["barrier"]:
                for e in ENGS:
                    lst = []
                    for e2, j in op["last_c"].items():
                        lst.append((("e", e2), esem[e2], ops[j]["sigval"]))
                    for slot, o in last_on_slot.items():
                        lst.append((("d", slot), o["dsem"], o["dval"]))
                    pending[e] = lst
                continue
            e = op["eng"]
            waits = []

            def need(key, sem, val):
                if waited[e].get(key, 0) < val:
                    waited[e][key] = val
                    waits.append((sem, val))

            for key, sem, val in pending[e]:
                need(key, sem, val)
            pending[e] = []
            for d in sorted(op["deps"]):
                od = ops[d]
                if od["dma"]:
                    need(("d", od["dslot"]), od["dsem"], od["dval"])
                elif not (od["eng"] == "pe" and e == "pe"):
                    need(("e", od["eng"]), esem[od["eng"]], od["sigval"])
            if op["dma"]:
                prev = last_on_slot.get(op["dslot"])
                if prev is not None:
                    need(("d", op["dslot"]), prev["dsem"], prev["dval"])
                last_on_slot[op["dslot"]] = op
            op["waits"] = waits
            per_eng[e].append(op)
        self.esem = esem
        self.per_eng = per_eng
        self.final_dma_waits = [(o["dsem"], o["dval"]) for o in last_on_slot.values()]

    def emit_engine(self, e, eng, final=False):
        for op in self.per_eng[e]:
            for sem, val in op["waits"]:
                eng.wait_ge(sem, val)
            ins = op["emit"](eng)
            if op["dma"]:
                ins.then_inc(op["dsem"], 16)
            elif op["signal"]:
                ins.then_inc(self.esem[e], 1)
        if final:
            for sem, val in self.final_dma_waits:
                eng.wait_ge(sem, val)

    def run_block(self):
        nc = self.nc
        with nc.Block() as block:
            @block.tensor
            def _(eng):
                self.emit_engine("pe", eng)

            @block.scalar
            def _(eng):
                self.emit_engine("act", eng)

            @block.vector
            def _(eng):
                self.emit_engine("dve", eng)

            @block.gpsimd
            def _(eng):
                self.emit_engine("pool", eng)

            @block.sync
            def _(eng):
                self.emit_engine("sp", eng, final=True)


C_ID, C_ONE, C_TRIU, C_SGTJ, C_TRIL, C_STRICT, C_PM, C_EBLK = 0, 128, 256, 384, 512, 640, 768, 800
NCONST = 800 + 1024
V_C, V_BADA, V_G = 0, 16, 160
V_CONV = 256
V_ALOG = 352
V_DTB = 480
V_NG = 608
V_PAR = 609
NVEC = 611


def build_nc(T, phases=(0, 1, 2, 3), gdn_on=True, moba_on=True, lvl=99, ngh=8, nmh=8):
    NT = T // 512
    NTL = T // 128
    NB = T // 256
    nc = bass.Bass("TRN2", target_bir_lowering=False)
    dr = lambda name, shape, dt=F32, kind="ExternalInput": nc.dram_tensor(name, shape, dt, kind=kind).ap()
    x_d = dr("x", [T, D])
    consts_d = dr("consts", [128, NCONST])
    vecs_d = dr("vecs", [128, NVEC])
    rope_d = dr("rope", [32, 2 * T])
    wada_d = dr("w_ada", [D, NMOD * D])
    w1g_d, w1u_d, w1d_d = dr("w1g", [D, FF]), dr("w1u", [D, FF]), dr("w1d", [FF, D])
    w2g_d, w2u_d, w2d_d = dr("w2g", [D, FF]), dr("w2u", [D, FF]), dr("w2d", [FF, D])
    win_d = dr("w_in", [D, INC])
    wout_d = dr("w_out", [D, D])
    split = T >= 2048
    TO = 1024 if split else T
    out_d = dr("out", [TO, D], kind="ExternalOutput")
    X1_d = dr("X1s", [D, T], F32, "Internal")
    X0_d = dr("X0s", [D, T], F32, "Internal")
    YF_d = dr("YFs", [D, 1024], F32, "Internal")
    H2_d = dr("H2s", [D, T], BF16, "Internal")
    Y_d = dr("Ys", [D, T], BF16, "Internal")

    st = contextlib.ExitStack()
    with st:
        S = Sched(nc)
        arena = st.enter_context(nc.sbuf_tensor("arena", [128, 96 * 1024], BF16))
        cst = st.enter_context(nc.sbuf_tensor("cst", [128, NCONST], F32))
        cstb = st.enter_context(nc.sbuf_tensor("cstb", [128, NCONST], BF16))
        vec = st.enter_context(nc.sbuf_tensor("vec", [128, NVEC], F32))
        modv = st.enter_context(nc.sbuf_tensor("modv", [128, 144 + 9 * 16], F32))
        PS = [st.enter_context(nc.psum_tensor("ps%d" % i, [128, 512], F32)) for i in range(8)]

        apos = [0]

        def alloc(nbytes):
            o = apos[0]
            apos[0] += (nbytes + 63) // 64 * 32
            assert apos[0] <= 96 * 1024, apos[0]
            return o

        def vf32(off, n):
            return arena[:, off:off + 2 * n].bitcast(F32)

        def vbf(off, n):
            return arena[:, off:off + n]

        def MM(out, lhsT, rhs, start=True, stop=True, r=(), w=()):
            S.add("pe", lambda e: e.matmul(out, lhsT, rhs, start=start, stop=stop), r, w)

        def TR(out, in_, r=(), w=()):
            S.add("pe", lambda e: e.transpose(out, in_, cst[:, C_ID:C_ID + 128]), r, w)

        def ACT(out, in_, func, r=(), w=(), **kw):
            S.add("act", lambda e: e.activation(out, in_, func, **kw), r, w)

        def CP(eng, out, in_, r=(), w=()):
            if eng == "act":
                S.add("act", lambda e: e.copy(out, in_), r, w)
            else:
                S.add(eng, lambda e: e.tensor_copy(out, in_), r, w)

        def TS(eng, out, in0, s1, s2, op0, op1=None, r=(), w=()):
            if op1 is None:
                S.add(eng, lambda e: e.tensor_single_scalar(out, in0, s1, op0), r, w)
            else:
                S.add(eng, lambda e: e.tensor_scalar(out, in0, s1, s2, op0, op1), r, w)

        def STT(eng, out, in0, sc, in1, op0, op1, r=(), w=()):
            S.add(eng, lambda e: e.scalar_tensor_tensor(out, in0, sc, in1, op0, op1), r, w)

        def TT(eng, out, in0, in1, op, r=(), w=()):
            S.add(eng, lambda e: e.tensor_tensor(out, in0, in1, op), r, w)

        def MEMSET(eng, ap, val, r=(), w=()):
            S.add(eng, lambda e: e.memset(ap, val), r, w)

        def PK(b, lo=0, hi=512):
            return ["ps%d" % b]

        ident = cst[:, C_ID:C_ID + 128]
        ones_f = cst[:, C_ONE:C_ONE + 128]
        triu_f = cst[:, C_TRIU:C_TRIU + 128]
        sgtj_f = cst[:, C_SGTJ:C_SGTJ + 128]
        tril_f = cst[:, C_TRIL:C_TRIL + 128]
        strict_f = cst[:, C_STRICT:C_STRICT + 128]
        ones_b = cstb[:, C_ONE:C_ONE + 128]
        triu_b = cstb[:, C_TRIU:C_TRIU + 128]

        S.dma("sp", cst[:], consts_d, writes=["cst"])
        S.dma("sp", vec[:], vecs_d, writes=["vec"])
        CP("dve", cstb[:], cst[:], r=["cst"], w=["cstb"])

        NSTG, NWB = 2, 8
        stg_off = [alloc(8192), alloc(8192)]
        wb_off = [alloc(4096) for _ in range(8)]
        ring = dict(s=0, w=0, c=0, nstg=NSTG, nwb=NWB)
        cast_engs = ("act", "dve")

        def wblock(src, nk, ncols, engs=cast_engs):
            si = ring["s"] % ring["nstg"]
            wi = ring["w"] % ring["nwb"]
            ring["s"] += 1
            ring["w"] += 1
            sv = vf32(stg_off[si], nk * ncols).rearrange("p (k c) -> p k c", k=nk)
            wv = vbf(wb_off[wi], nk * ncols).rearrange("p (k c) -> p k c", k=nk)
            S.dma("sp", sv, src.rearrange("(k p) c -> p k c", p=128), writes=["stg%d" % si])
            eng = engs[ring["c"] % len(engs)]
            ring["c"] += 1
            CP(eng, wv, sv, r=["stg%d" % si], w=["wb%d" % wi])
            return wv, "wb%d" % wi

        main_o = apos[0]
        hT_o = alloc(32768)
        actT_o = alloc(FC * 2048)
        cx_o = [alloc(4096), alloc(4096)]
        cy_o = [alloc(4096), alloc(4096)]
        rstd_o = alloc(4096)
        sq_o = alloc(2048)
        scb_o = alloc(64)
        row_o = cy_o

        hT = vbf(hT_o, 16 * 1024).rearrange("p (k t) -> p k t", k=16)
        actT = vbf(actT_o, FC * 1024).rearrange("p (k t) -> p k t", k=FC)
        xTf = vf32(actT_o, 16 * 1024).rearrange("p (k t) -> p k t", k=16)
        xin = [vf32(actT_o + 32768, 2048), vf32(actT_o + 32768 + 4096, 2048)]
        cx = [vf32(o, 1024) for o in cx_o]
        cy = [vf32(o, 1024) for o in cy_o]
        rstdB = vf32(rstd_o, 1024)
        sqb = vbf(sq_o, 1024)

        def mcol(i):
            return modv[:, i * 16:(i + 1) * 16]
        Sv = [modv[:, 144 + i * 16:144 + (i + 1) * 16] for i in range(3)]
        coef = [modv[:, 144 + 48 + i * 16:144 + 48 + (i + 1) * 16] for i in range(3)]
        tmpv = modv[:, 144 + 96:144 + 112]

        def gain(i):
            return vec[:, V_G + i * 16:V_G + (i + 1) * 16]

        def phase0():
            scb = vbf(scb_o, 16)
            ACT(scb, vec[:, V_C:V_C + 16], AF.Silu, r=["vec"], w=["scb"])
            pm = PS[7]
            for cb in range(72):
                blks = [wblock(wada_d[kh * 1024:(kh + 1) * 1024, cb * 256:(cb + 1) * 256], 8, 256) for kh in range(2)]
                pr = PS[cb % 2]
                prk = PK(cb % 2, 0, 256)
                for kc in range(16):
                    wv, wk = blks[kc // 8]
                    MM(pr[0:1, 0:256], scb[:, kc:kc + 1], wv[:, kc % 8, :], start=(kc == 0), stop=(kc == 15),
                       r=["scb", wk], w=prk)
                rw = vf32(row_o[cb % 2], 256)
                rk = "row%d" % (cb % 2)
                CP("dve", rw[0:1, :], pr[0:1, 0:256], r=prk, w=[rk])
                for j in range(2):
                    MM(pm[:, cb * 2 + j:cb * 2 + j + 1], rw[0:1, j * 128:(j + 1) * 128], cst[0:1, C_ONE:C_ONE + 1],
                       r=[rk, "cst"], w=PK(7, 0, 256))
            TT("dve", modv[:, 0:144], pm[:, 0:144], vec[:, V_BADA:V_BADA + 144], ALU.add, r=PK(7, 0, 256) + ["vec"], w=["modv"])
            for i in range(3):
                TS("dve", tmpv, mcol(3 * i + 1), 1.0, None, ALU.add, r=["modv"], w=["tmpv"])
                TT("dve", Sv[i], tmpv, gain(2 * i), ALU.mult, r=["tmpv", "vec"], w=["Sv%d" % i])
                STT("dve", coef[i], mcol(3 * i + 2), (1.0 if i == 1 else 0.5), gain(2 * i + 1), ALU.mult, ALU.mult,
                    r=["modv", "vec"], w=["coef%d" % i])

        HK = ["H%d" % k for k in range(16)]
        AK = ["A%d" % k for k in range(FC)]
        XFK = lambda kc: [AK[2 * kc], AK[2 * kc + 1]]
        X0v = X0_d.rearrange("(k p) t -> p k t", p=128)
        X1v = X1_d.rearrange("(k p) t -> p k t", p=128)
        H2v = H2_d.rearrange("(k p) t -> p k t", p=128)
        Yv = Y_d.rearrange("(k p) t -> p k t", p=128)
        YFv = YF_d.rearrange("(k p) t -> p k t", p=128)

        def stat_acc(src, srckeys, first, last):
            ACT(sqb, src, AF.Square, r=srckeys, w=["sq"])
            for hf in range(2):
                MM(PS[6 + hf][:, :], ones_b, sqb[:, hf * 512:(hf + 1) * 512], start=first, stop=last,
                   r=["cstb", "sq"], w=PK(6 + hf))

        def stat_fin():
            for hf in range(2):
                ACT(rstdB[:, hf * 512:(hf + 1) * 512], PS[6 + hf][:, :], AF.Ln, r=PK(6 + hf), w=["rstdB"],
                    scale=1.0 / D, bias=EPS)
            ACT(rstdB, rstdB, AF.Exp, r=["rstdB"], w=["rstdB"], scale=-0.5)

        def mod_to_hT(i, kc, src, srckeys):
            t = cy[kc % 2]
            STT("dve", t, src, Sv[i][:, kc:kc + 1], rstdB, ALU.mult, ALU.mult,
                r=srckeys + ["Sv%d" % i, "rstdB"], w=["cy%d" % (kc % 2)])
            ACT(hT[:, kc, :], t, AF.Identity, r=["cy%d" % (kc % 2), "modv"], w=[HK[kc]], bias=mcol(3 * i)[:, kc:kc + 1])

        def prenorm_dram(i, Xsrc, ts, xkey):
            stat_fin()
            for kc in range(16):
                S.dma("sp", cx[kc % 2], Xsrc[:, kc, ts], reads=[xkey], writes=["cx%d" % (kc % 2)])
                mod_to_hT(i, kc, cx[kc % 2], ["cx%d" % (kc % 2)])

        def y_sink(dm, buf, bkey):
            stat_acc(buf, [bkey], dm == 0, dm == 15)
            S.dma("sp", YFv[:, dm, :], buf, reads=[bkey], writes=["YF%d" % dm])

        def proj_T(inT, inkeys, nk, W):
            groups = [(g, min(8, nk - g)) for g in range(0, nk, 8)]
            for db in range(8):
                for (g0, n) in groups:
                    wv, wk = wblock(W[g0 * 128:(g0 + n) * 128, db * 256:(db + 1) * 256], n, 256)
                    for sub in range(2):
                        for hf in range(2):
                            bk = 2 + sub * 2 + hf
                            for f in range(n):
                                k = g0 + f
                                MM(PS[bk][:, :], wv[:, f, sub * 128:(sub + 1) * 128], inT[:, k, hf * 512:(hf + 1) * 512],
                                   start=(k == 0), stop=(k == nk - 1), r=[wk, inkeys[k]], w=PK(bk))
                for sub in range(2):
                    dm = db * 2 + sub
                    buf, bkey = cy[dm % 2], "cy%d" % (dm % 2)
                    for hf in range(2):
                        bk = 2 + sub * 2 + hf
                        CP("act" if bk % 2 == 0 else "dve", buf[:, hf * 512:(hf + 1) * 512], PS[bk][:, :], r=PK(bk), w=[bkey])
                    y_sink(dm, buf, bkey)

        def ffn(Wg, Wu, Wd):
            for fb in range(22):
                blk = {}
                for mi, Wm in enumerate((Wg, Wu)):
                    for kh in range(2):
                        blk[(mi, kh)] = wblock(Wm[kh * 1024:(kh + 1) * 1024, fb * 256:(fb + 1) * 256], 8, 256,
                                               engs=("act", "dve"))
                for sub in range(2):
                    fc = fb * 2 + sub
                    for hf in range(2):
                        hs = slice(hf * 512, (hf + 1) * 512)
                        for mi in range(2):
                            bk = 2 * hf + mi
                            for kc in range(16):
                                wv, wk = blk[(mi, kc // 8)]
                                MM(PS[bk][:, :], wv[:, kc % 8, sub * 128:(sub + 1) * 128], hT[:, kc, hs],
                                   start=(kc == 0), stop=(kc == 15), r=[wk, HK[kc]], w=PK(bk))
                        tb, tk = cy[hf][:, 0:512], "cy%d" % hf
                        ACT(tb, PS[2 * hf][:, :], AF.Silu, r=PK(2 * hf), w=[tk])
                        TT("dve", actT[:, fc, hs], tb, PS[2 * hf + 1][:, :], ALU.mult, r=[tk] + PK(2 * hf + 1), w=[AK[fc]])
            proj_T(actT, AK, FC, Wd)

        m0c, m1c = vec[:, V_PAR:V_PAR + 1], vec[:, V_PAR + 1:V_PAR + 2]

        def post_res(i, Xsrc, ts, xkey, dst, need_stats, ts2=None):
            stat_fin()
            for kc in range(16):
                cb, ck = cx[kc % 2], "cx%d" % (kc % 2)
                yb, yk = cy[kc % 2], "cy%d" % (kc % 2)
                S.dma("sp", yb, YFv[:, kc, :], reads=["YF%d" % kc], writes=[yk])
                S.dma("sp", cb, Xsrc[:, kc, ts], reads=[xkey], writes=[ck])
                if ts2 is not None:
                    c2 = vf32(actT_o + 16384 + 2048 * (kc % 2), 1024)
                    c2k = [AK[16 + 2 * (kc % 2)], AK[17 + 2 * (kc % 2)]]
                    S.dma("sp", c2, Xsrc[:, kc, ts2], reads=[xkey], writes=c2k)
                    TS("dve", c2, c2, m1c, None, ALU.mult, r=c2k + ["vec"], w=c2k)
                    STT("dve", cb, cb, m0c, c2, ALU.mult, ALU.add, r=[ck, "vec"] + c2k, w=[ck])
                STT("dve", yb, yb, coef[i][:, kc:kc + 1], rstdB, ALU.mult, ALU.mult, r=[yk, "coef%d" % i, "rstdB"], w=[yk])
                TT("pool", cb, cb, yb, ALU.add, r=[ck, yk], w=[ck])
                if need_stats:
                    stat_acc(cb, [ck], kc == 0, kc == 15)
                dst(kc, cb, ck)

        def phase1():
            for t in range(T // 1024):
                ts = slice(t * 1024, (t + 1) * 1024)
                for j in range(8):
                    xb, xbk = xin[j % 2], AK[32 + 4 * (j % 2):36 + 4 * (j % 2)]
                    S.dma("sp", xb, x_d[t * 1024 + j * 128:t * 1024 + (j + 1) * 128, :], writes=xbk)
                    for q4 in range(4):
                        bk = 4 + q4 % 2
                        for c in range(4):
                            kc = q4 * 4 + c
                            TR(PS[bk][:, c * 128:(c + 1) * 128], xb[:, kc * 128:(kc + 1) * 128], r=["cst"] + xbk, w=PK(bk))
                        CP("act" if bk % 2 == 0 else "dve", xTf[:, q4 * 4:(q4 + 1) * 4, j * 128:(j + 1) * 128],
                           PS[bk][:, :].rearrange("p (c t) -> p c t", c=4), r=PK(bk), w=AK[8 * q4:8 * q4 + 8])
                S.dma("sp", X0v[:, :, ts], xTf, reads=AK[0:32], writes=["X0s"])
                for kc in range(16):
                    stat_acc(xTf[:, kc, :], XFK(kc), kc == 0, kc == 15)
                stat_fin()
                for kc in range(16):
                    mod_to_hT(0, kc, xTf[:, kc, :], XFK(kc))
                ffn(w1g_d, w1u_d, w1d_d)
                post_res(0, X0v, ts, "X0s", lambda kc, cb, ck: S.dma("sp", X1v[:, kc, ts], cb, reads=[ck], writes=["X1s"]), True)
                prenorm_dram(1, X1v, ts, "X1s")
                S.dma("sp", H2v[:, :, ts], hT, reads=HK, writes=["H2s"])

        def phase3():
            tiles = [0] if split else list(range(T // 1024))
            for t in tiles:
                ts = slice(t * 1024, (t + 1) * 1024)
                ts2 = slice(1024, 2048) if split else None
                S.dma("sp", hT, Yv[:, :, ts], reads=["Ys"], writes=HK)
                if split:
                    hB = vbf(actT_o, 16 * 1024).rearrange("p (k t) -> p k t", k=16)
                    S.dma("sp", hB, Yv[:, :, ts2], reads=["Ys"], writes=AK[0:16])
                    for kc in range(16):
                        TS("dve", hB[:, kc, :], hB[:, kc, :], m1c, None, ALU.mult, r=[AK[kc], "vec"], w=[AK[kc]])
                        STT("dve", hT[:, kc, :], hT[:, kc, :], m0c, hB[:, kc, :], ALU.mult, ALU.add,
                            r=[HK[kc], "vec", AK[kc]], w=[HK[kc]])
                proj_T(hT, HK, 16, wout_d)
                post_res(1, X1v, ts, "X1s", lambda kc, cb, ck: S.dma("sp", X0v[:, kc, ts], cb, reads=[ck], writes=["X0s"]), True,
                         ts2=ts2)
                prenorm_dram(2, X0v, ts, "X0s")
                ffn(w2g_d, w2u_d, w2d_d)
                post_res(2, X0v, ts, "X0s",
                         lambda kc, cb, ck: CP("act", xTf[:, kc, :], cb, r=[ck], w=XFK(kc)), False)
                for j in range(8):
                    xb, xbk = xin[j % 2], AK[32 + 4 * (j % 2):36 + 4 * (j % 2)]
                    for q4 in range(4):
                        bk = 4 + q4 % 2
                        for c in range(4):
                            kc = q4 * 4 + c
                            TR(PS[bk][:, c * 128:(c + 1) * 128], xTf[:, kc, j * 128:(j + 1) * 128], r=["cst"] + XFK(kc), w=PK(bk))
                        CP("act" if bk % 2 == 0 else "dve", xb[:, q4 * 512:(q4 + 1) * 512], PS[bk][:, :], r=PK(bk), w=xbk)
                    S.dma("sp", out_d[t * 1024 + j * 128:t * 1024 + (j + 1) * 128, :], xb, reads=xbk, writes=["out"])

        def phase2():
            ring["nstg"], ring["nwb"] = 2, 4
            h2_o = [wb_off[4], None]
            apos[0] = main_o
            h2_o[1] = alloc(16384)
            raw_o = [alloc(4 * T), alloc(4 * T), alloc(4 * T), alloc(4 * T)]
            cb_o = alloc(4 * T)
            Kt_o, Vt_o = alloc(4 * T), alloc(4 * T)
            AT_o, wT_o, u_o = alloc(4 * T), alloc(4 * T), alloc(4 * T)
            o_o = raw_o[2]
            G = 2
            un_o = [dict(B=alloc(512), dec=alloc(512), decS=alloc(512), decT=alloc(512), N=alloc(512), A=alloc(512),
                         XY=[alloc(512) for _ in range(4)], R=[alloc(1024), alloc(1024)]) for _ in range(G)]
            S_o = [alloc(512), alloc(512)]
            vn_o = [alloc(512), alloc(512)]
            kd_o = [alloc(512), alloc(512)]
            ot_o = [alloc(512), alloc(512)]
            on_o = [alloc(512), alloc(512)]
            yh_o = [alloc(2 * T)] * 2
            ab_o = alloc(NTL * 16 * 4)
            sm_o = {k: alloc(NTL * 8 * 4) for k in ("gcol", "beta", "nbeta", "Gc", "gl", "eG", "edec", "egl", "bEG", "t1", "t2")}
            wab_o = alloc(16 * 16 * 2)
            ss_o = alloc(64 * 4)
            km_o = alloc(64)
            gate_o = alloc(NTL * 8 * 4)
            gtmp_o = [alloc(64) for _ in range(4)]
            seln_o = alloc(NTL * 8 * 4)
            pt_o = [alloc(256) for _ in range(4)]
            rl_o = [alloc(512), alloc(512)]
            qb_o, kb_o, vtk_o = AT_o, wT_o, u_o
            if T >= 2048:
                selT_o = Vt_o + 2048
                rt_o = [Vt_o, Vt_o + 1024]
            else:
                selT_o = alloc(2 * T)
                rt_o = [alloc(2048), alloc(2048)]

            h2t = [vbf(o, 16 * 512).rearrange("p (k t) -> p k t", k=16) for o in h2_o]
            raw = [vf32(o, T) for o in raw_o]
            cbuf = vf32(cb_o, T)
            Kt = vf32(Kt_o, NTL * 128).rearrange("p (n d) -> p n d", n=NTL)
            Vt = vf32(Vt_o, NTL * 128).rearrange("p (n d) -> p n d", n=NTL)
            ATa = vf32(AT_o, NTL * 128).rearrange("p (n d) -> p n d", n=NTL)
            wTa = vf32(wT_o, NTL * 128).rearrange("p (n d) -> p n d", n=NTL)
            ua = vf32(u_o, NTL * 128).rearrange("p (n d) -> p n d", n=NTL)
            oa = vf32(o_o, NTL * 128).rearrange("p (n d) -> p n d", n=NTL)
            sm = {k: vf32(o, NTL * 8).rearrange("p (n h) -> p n h", n=NTL) for k, o in sm_o.items()}
            ab = vf32(ab_o, NTL * 16).rearrange("p (n c) -> p n c", n=NTL)
            wab = vbf(wab_o, 256).rearrange("p (k c) -> p k c", k=16)
            ssv = vf32(ss_o, 64)

            def h2load(tt, slot):
                S.dma("sp", h2t[slot], H2v[:, :, tt * 512:(tt + 1) * 512], reads=["H2s"], writes=["h2t%d" % slot])

            sv = vf32(stg_off[0], 256).rearrange("p (k c) -> p k c", k=16)
            S.dma("sp", sv, win_d[:, 4096:4112].rearrange("(k p) c -> p k c", p=128), writes=["stg0"])
            CP("dve", wab, sv, r=["stg0"], w=["wab"])
            for tt in range(NT):
                h2load(tt, tt % 2)
                for sub in range(4):
                    tl = tt * 4 + sub
                    for kc in range(16):
                        MM(PS[0][:, tl * 16:(tl + 1) * 16], h2t[tt % 2][:, kc, sub * 128:(sub + 1) * 128], wab[:, kc, :],
                           start=(kc == 0), stop=(kc == 15), r=["h2t%d" % (tt % 2), "wab"], w=PK(0, 0, 256))
            CP("dve", ab.rearrange("p n c -> p (n c)"), PS[0][:, 0:NTL * 16], r=PK(0, 0, 256), w=["ab"])
            flat = lambda k: sm[k].rearrange("p n h -> p (n h)")
            alog = vec[:, V_ALOG:V_ALOG + 128].rearrange("p (n h) -> p n h", n=16)[:, 0:NTL, :]
            dtb = vec[:, V_DTB:V_DTB + 128].rearrange("p (n h) -> p n h", n=16)[:, 0:NTL, :]
            TT("dve", sm["t1"], ab[:, :, 0:8], dtb, ALU.add, r=["ab", "vec"], w=["t1"])
            ACT(flat("t1"), flat("t1"), AF.Exp, r=["t1"], w=["t1"])
            ACT(flat("t1"), flat("t1"), AF.Ln, r=["t1"], w=["t1"], bias=1.0)
            ACT(sm["t2"], alog, AF.Exp, r=["vec"], w=["t2"])
            STT("dve", flat("gcol"), flat("t1"), -1.0, flat("t2"), ALU.mult, ALU.mult, r=["t1", "t2"], w=["gcol"])
            ACT(sm["beta"], ab[:, :, 8:16], AF.Sigmoid, r=["ab"], w=["beta"])
            TS("dve", flat("nbeta"), flat("beta"), -1.0, None, ALU.mult, r=["beta"], w=["nbeta"])
            for tl in range(NTL):
                MM(PS[1][:, tl * 8:(tl + 1) * 8], triu_f, sm["gcol"][:, tl, :], r=["cst", "gcol"], w=PK(1, 0, 128))
                MM(PS[2][:, tl * 8:(tl + 1) * 8], ones_f, sm["gcol"][:, tl, :], r=["cst", "gcol"], w=PK(2, 0, 128))
            CP("dve", flat("Gc"), PS[1][:, 0:NTL * 8], r=PK(1, 0, 128), w=["Gc"])
            CP("dve", flat("gl"), PS[2][:, 0:NTL * 8], r=PK(2, 0, 128), w=["gl"])
            ACT(flat("eG"), flat("Gc"), AF.Exp, r=["Gc"], w=["eG"])
            ACT(flat("egl"), flat("gl"), AF.Exp, r=["gl"], w=["egl"])
            TT("dve", flat("t1"), flat("gl"), flat("Gc"), ALU.subtract, r=["gl", "Gc", "t1"], w=["t1"])
            ACT(flat("edec"), flat("t1"), AF.Exp, r=["t1"], w=["edec"])
            TT("dve", flat("bEG"), flat("beta"), flat("eG"), ALU.mult, r=["beta", "eG"], w=["bEG"])

            if lvl < 1:
                ring["nstg"], ring["nwb"] = NSTG, NWB
                return

            def inproj(cols, nchunk):
                blks = [wblock(win_d[:, c0:c0 + 128], 16, 128, engs=("act",)) for c0 in cols]
                for tt in range(NT):
                    h2load(tt, tt % 2)
                    for ci in range(nchunk):
                        wv, wk = blks[ci]
                        for kc in range(16):
                            MM(PS[ci][:, :], wv[:, kc, :], h2t[tt % 2][:, kc, :], start=(kc == 0), stop=(kc == 15),
                               r=[wk, "h2t%d" % (tt % 2)], w=PK(ci))
                        CP("act" if ci % 2 else "dve", raw[ci][:, tt * 512:(tt + 1) * 512], PS[ci][:, :],
                           r=PK(ci), w=["raw%d" % ci])

            def to_tokmajor(src, sk, dstflat, dk_):
                for g4 in range(NTL // 4):
                    b = 6 + g4 % 2
                    for c in range(4):
                        tl = g4 * 4 + c
                        TR(PS[b][:, c * 128:(c + 1) * 128], src[:, tl * 128:(tl + 1) * 128], r=["cst", sk],
                           w=PK(b, c * 128, (c + 1) * 128))
                    CP("act" if g4 % 2 else "dve", dstflat(g4), PS[b][:, :], r=PK(b), w=[dk_])

            def gdn_head(h):
                inproj([h * 128, 1024 + h * 128, 2048 + h * 128, 3072 + h * 128], 4)
                for ci in range(3):
                    eng = "dve"
                    cw0 = V_CONV + (ci * 8 + h) * 4
                    x = raw[ci]
                    rk = "raw%d" % ci
                    TS(eng, cbuf[:, :], x[:, :], vec[:, cw0 + 3:cw0 + 4], None, ALU.mult, r=[rk, "vec"], w=["cbuf"])
                    for sft in (1, 2, 3):
                        STT(eng, cbuf[:, sft:T], x[:, 0:T - sft], vec[:, cw0 + 3 - sft:cw0 + 4 - sft], cbuf[:, sft:T],
                            ALU.mult, ALU.add, r=[rk, "vec", "cbuf"], w=["cbuf"])
                    ACT(x[:, :], cbuf[:, :], AF.Silu, r=["cbuf"], w=[rk])
                    if ci < 2:
                        ACT(cbuf[:, :], x[:, :], AF.Square, r=[rk], w=["cbuf"])
                        for tt in range(NT):
                            MM(PS[tt][:, :], ones_f, cbuf[:, tt * 512:(tt + 1) * 512], r=["cst", "cbuf"], w=PK(tt))
                        for tt in range(NT):
                            ACT(cbuf[:, tt * 512:(tt + 1) * 512], PS[tt][:, :], AF.Ln, r=PK(tt), w=["cbuf"], bias=EPS)
                        ACT(cbuf[:, :], cbuf[:, :], AF.Exp, r=["cbuf"], w=["cbuf"], scale=-0.5)
                        if ci == 0:
                            STT("dve", x[:, :], x[:, :], 128.0 ** -0.5, cbuf[:, :], ALU.mult, ALU.mult, r=[rk, "cbuf"], w=[rk])
                        else:
                            TT("dve", x[:, :], x[:, :], cbuf[:, :], ALU.mult, r=[rk, "cbuf"], w=[rk])
                ACT(raw[3][:, :], raw[3][:, :], AF.Silu, r=["raw3"], w=["raw3"])
                TS("dve", raw[3][:, :], raw[3][:, :], vec[:, V_NG:V_NG + 1], None, ALU.mult, r=["raw3", "vec"], w=["raw3"])
                qT, kT, vT, gz = raw
                if lvl < 2:
                    return
                to_tokmajor(kT, "raw1", lambda g4: Kt[:, g4 * 4:(g4 + 1) * 4, :].rearrange("p n d -> p (n d)"), "Kt")
                to_tokmajor(vT, "raw2", lambda g4: Vt[:, g4 * 4:(g4 + 1) * 4, :].rearrange("p n d -> p (n d)"), "Vt")
                if lvl < 2.5:
                    return
                for g0 in range(0, NTL, G):
                    units = list(range(g0, min(NTL, g0 + G)))
                    U = {}
                    for tl in units:
                        u = tl % G
                        o = un_o[u]
                        U[tl] = dict(
                            B=vf32(o["B"], 128), dec=vf32(o["dec"], 128), decS=vf32(o["decS"], 128), decT=vf32(o["decT"], 128),
                            N=vf32(o["N"], 128), A=vf32(o["A"], 128), XY=[vf32(q, 128) for q in o["XY"]],
                            R=[vf32(q, 256) for q in o["R"]], pa=PS[2 * u][:, 0:128], pb=PS[2 * u][:, 128:256], pc=PS[2 * u + 1][:, 0:256],
                            pak=PK(2 * u, 0, 128), pbk=PK(2 * u, 128, 256), pck=PK(2 * u + 1, 0, 256),
                            k=(lambda s_, u=u: "u%d%s" % (u, s_)), tsl=slice(tl * 128, (tl + 1) * 128))
                    for tl in units:
                        d = U[tl]; k = d["k"]
                        TS("dve", d["B"], sgtj_f, sm["gcol"][:, tl, h:h + 1], None, ALU.mult, r=["cst", "gcol"], w=[k("B")])
                        MM(d["pa"], triu_f, d["B"], r=["cst", k("B")], w=d["pak"])
                        ACT(d["dec"], d["pa"], AF.Exp, r=d["pak"], w=[k("dec")])
                        TS("pool", d["R"][0][:, 0:128], Vt[:, tl, :], sm["beta"][:, tl, h:h + 1], None, ALU.mult,
                           r=["Vt", "beta"], w=[k("R0")])
                        TS("dve", d["R"][0][:, 128:256], Kt[:, tl, :], sm["bEG"][:, tl, h:h + 1], None, ALU.mult,
                           r=["Kt", "bEG", k("R0")], w=[k("R0")])
                        if lvl < 2.6:
                            continue
                        MM(d["pa"], kT[:, d["tsl"]], kT[:, d["tsl"]], r=["raw1"], w=d["pak"])
                        MM(d["pb"], qT[:, d["tsl"]], kT[:, d["tsl"]], r=["raw0", "raw1"], w=d["pbk"])
                        TT("dve", d["decS"], d["dec"], strict_f, ALU.mult, r=[k("dec"), "cst"], w=[k("decS")])
                        TT("dve", d["decT"], d["dec"], tril_f, ALU.mult, r=[k("dec"), "cst"], w=[k("decT")])
                        STT("dve", d["N"], d["pa"], sm["nbeta"][:, tl, h:h + 1], d["decS"], ALU.mult, ALU.mult,
                            r=d["pak"] + ["nbeta", k("decS")], w=[k("N")])
                        TT("dve", d["A"], d["pb"], d["decT"], ALU.mult, r=d["pbk"] + [k("decT")], w=[k("A")])
                        if lvl < 2.7:
                            continue
                        TR(d["pa"], d["N"], r=["cst", k("N")], w=d["pak"])
                        TR(d["pb"], d["A"], r=["cst", k("A")], w=d["pbk"])
                        CP("act", d["XY"][1], d["pa"], r=d["pak"], w=[k("XY1")])
                        CP("act", ATa[:, tl, :], d["pb"], r=d["pbk"], w=["ATa"])
                        if lvl < 2.78:
                            continue
                    if lvl < 2.8:
                        continue
                    nlv = 7 if lvl >= 3 else int(round((lvl - 2.8) * 100))
                    for lv in range(nlv):
                        for tl in units:
                            d = U[tl]; k = d["k"]
                            if lv == 0:
                                X, Xk, Y, Yk = d["N"], k("N"), d["XY"][1], k("XY1")
                            else:
                                xi, yi = (0, 1) if lv % 2 == 0 else (2, 3)
                                X, Xk, Y, Yk = d["XY"][xi], k("XY%d" % xi), d["XY"][yi], k("XY%d" % yi)
                            Rc, Rck = d["R"][lv % 2], k("R%d" % (lv % 2))
                            cv = "rs"
                            if "r" in cv:
                                MM(d["pc"], Y, Rc, r=[Yk, Rck], w=d["pck"])
                            if lv < 6 or nlv < 7:
                                Rn, Rnk = d["R"][(lv + 1) % 2], k("R%d" % ((lv + 1) % 2))
                                if "r" in cv:
                                    TT("dve", Rn, Rc, d["pc"], ALU.add, r=[Rck] + d["pck"], w=[Rnk])
                                nxi, nyi = (0, 1) if (lv + 1) % 2 == 0 else (2, 3)
                                if "s" in cv:
                                    MM(d["pa"], Y, X, r=[Yk, Xk], w=d["pak"])
                                    MM(d["pb"], X, Y, r=[Yk, Xk], w=d["pbk"])
                                    CP("dve", d["XY"][nxi], d["pa"], r=d["pak"], w=[k("XY%d" % nxi)])
                                    CP("dve", d["XY"][nyi], d["pb"], r=d["pbk"], w=[k("XY%d" % nyi)])
                            else:
                                TT("dve", ua[:, tl, :], Rc[:, 0:128], d["pc"][:, 0:128], ALU.add, r=[Rck] + d["pck"], w=["ua"])
                                TT("dve", d["A"], Rc[:, 128:256], d["pc"][:, 128:256], ALU.add, r=[Rck] + d["pck"], w=[k("A")])
                                TR(d["pa"], d["A"], r=["cst", k("A")], w=d["pak"])
                                CP("act", wTa[:, tl, :], d["pa"], r=d["pak"], w=["wTa"])
                if lvl < 4:
                    return
                Sst = [vf32(o, 128) for o in S_o]
                MEMSET("pool", Sst[0], 0.0, w=["S0"])
                for tl in range(NTL):
                    cur, nxt = tl % 2, (tl + 1) % 2
                    b = 4 + tl % 2
                    pb_ = PS[b]
                    tsl = slice(tl * 128, (tl + 1) * 128)
                    vn, kd, ot = vf32(vn_o[tl % 2], 128), vf32(kd_o[tl % 2], 128), vf32(ot_o[tl % 2], 128)
                    vk, kk, ok_ = "vn%d" % (tl % 2), "kd%d" % (tl % 2), "ot%d" % (tl % 2)
                    MM(pb_[:, 0:128], wTa[:, tl, :], Sst[cur], r=["wTa", "S%d" % cur], w=PK(b, 0, 128))
                    TT("dve", vn, ua[:, tl, :], pb_[:, 0:128], ALU.subtract, r=["ua"] + PK(b, 0, 128), w=[vk])
                    MM(pb_[:, 128:256], qT[:, tsl], Sst[cur], r=["raw0", "S%d" % cur], w=PK(b, 128, 256))
                    MM(pb_[:, 256:384], ATa[:, tl, :], vn, r=["ATa", vk], w=PK(b, 256, 384))
                    TS("dve", ot, pb_[:, 128:256], sm["eG"][:, tl, h:h + 1], None, ALU.mult, r=PK(b, 128, 256) + ["eG"], w=[ok_])
                    TT("dve", oa[:, tl, :], ot, pb_[:, 256:384], ALU.add, r=[ok_] + PK(b, 256, 384), w=["raw2"])
                    TS("pool", kd, Kt[:, tl, :], sm["edec"][:, tl, h:h + 1], None, ALU.mult, r=["Kt", "edec"], w=[kk])
                    MM(pb_[:, 384:512], kd, vn, r=[kk, vk], w=PK(b, 384, 512))
                    STT("dve", Sst[nxt], Sst[cur], sm["egl"][:, tl, h:h + 1], pb_[:, 384:512], ALU.mult, ALU.add,
                        r=["S%d" % cur, "egl"] + PK(b, 384, 512), w=["S%d" % nxt])
                if lvl < 5:
                    return
                yh = vbf(yh_o[h % 2], T)
                yk = "yh"
                ACT(cbuf[:, :], oa.rearrange("p n d -> p (n d)"), AF.Square, r=["raw2"], w=["cbuf"])
                S.add("dve", lambda e: e.tensor_reduce(ssv[:, 0:NTL], cbuf[:, :].rearrange("p (n d) -> p n d", n=NTL), AX.X, ALU.add),
                      ["cbuf"], ["ssv"])
                ACT(ssv[:, 0:NTL], ssv[:, 0:NTL], AF.Ln, r=["ssv"], w=["ssv"], scale=1.0 / 128, bias=EPS)
                ACT(ssv[:, 0:NTL], ssv[:, 0:NTL], AF.Exp, r=["ssv"], w=["ssv"], scale=-0.5)
                for tl in range(NTL):
                    on = vf32(on_o[tl % 2], 128)
                    onk = "on%d" % (tl % 2)
                    b = 6 + tl % 2
                    TS("dve", on, oa[:, tl, :], ssv[:, tl:tl + 1], None, ALU.mult, r=["raw2", "ssv"], w=[onk])
                    TR(PS[b][:, 0:128], on, r=["cst", onk], w=PK(b, 0, 128))
                    TT("dve", yh[:, tl * 128:(tl + 1) * 128], PS[b][:, 0:128], gz[:, tl * 128:(tl + 1) * 128], ALU.mult,
                       r=PK(b, 0, 128) + ["raw3"], w=[yk])
                S.dma("sp", Y_d[h * 128:(h + 1) * 128, :], yh, reads=[yk], writes=["Ys"])

            def moba_head(m):
                inproj([4112 + m * 128, 5136 + m * 128, 6160 + m * 128], 3)
                rq, rk_, rv = raw[0], raw[1], raw[2]
                cosv = cbuf[0:32, 0:T]
                sinv = vf32(Kt_o, T)[0:32, :]
                pmf = cst[0:32, C_PM:C_PM + 32]
                for ci in range(2):
                    x = raw[ci]
                    xk = "raw%d" % ci
                    for tt in range(NT):
                        cs = slice(tt * 512, (tt + 1) * 512)
                        b = 4 + tt % 2
                        t1 = vf32(rt_o[0], 512)
                        t2 = vf32(rt_o[1], 512)
                        MM(PS[b][0:32, :], pmf, x[0:32, cs], r=["cst", xk], w=PK(b))
                        TT("dve", t1[0:32, :], x[0:32, cs], cosv[:, cs], ALU.mult, r=[xk, "cbuf"], w=["Vt"])
                        TT("dve", t2[0:32, :], PS[b][0:32, :], sinv[:, cs], ALU.mult, r=PK(b) + ["Kt", "Vt"], w=["Vt"])
                        TT("pool", x[0:32, cs], t1[0:32, :], t2[0:32, :], ALU.add, r=["Vt", xk], w=[xk])
                km = vf32(km_o, 8)
                S.add("dve", lambda e: e.tensor_reduce(km[:, 0:NB], rk_[:, :].rearrange("p (n t) -> p n t", n=NB), AX.X, ALU.add),
                      ["raw1"], ["km"])
                TS("dve", km[:, 0:NB], km[:, 0:NB], 1.0 / 256, None, ALU.mult, r=["km"], w=["km"])
                gate = vf32(gate_o, NTL * 8).rearrange("p (n b) -> p n b", n=NTL)
                seln = vf32(seln_o, NTL * 8).rearrange("p (n b) -> p n b", n=NTL)
                for tl in range(NTL):
                    MM(PS[5][:, tl * 8:tl * 8 + NB], rq[:, tl * 128:(tl + 1) * 128], km[:, 0:NB], r=["raw0", "km"], w=PK(5, 0, 128))
                CP("dve", gate[:, :, 0:NB], PS[5][:, 0:NTL * 8].rearrange("p (n b) -> p n b", b=8)[:, :, 0:NB], r=PK(5, 0, 128), w=["gate"])
                MEMSET("pool", seln.rearrange("p n b -> p (n b)"), 0.0, w=["seln"])
                gt = [vf32(o, 8) for o in gtmp_o]
                for tl in range(NTL):
                    own = tl // 2
                    if own <= 3:
                        continue
                    g = gate[:, tl, 0:own]
                    cur = g
                    ck = "gate"
                    for it in range(3):
                        S.add("dve", lambda e, cur=cur, it=it: e.tensor_reduce(gt[3][:, it:it + 1], cur, AX.X, ALU.max),
                              [ck, "gt3"], ["gt3"])
                        if it < 2:
                            TS("dve", gt[2][:, 0:own], cur, gt[3][:, it:it + 1], -1e30, ALU.is_ge, ALU.mult, r=[ck, "gt3"], w=["gt2"])
                            TT("dve", gt[it][:, 0:own], cur, gt[2][:, 0:own], ALU.add, r=[ck, "gt2"], w=["gt%d" % it])
                            cur = gt[it][:, 0:own]
                            ck = "gt%d" % it
                    TS("dve", seln[:, tl, 0:own], g, gt[3][:, 2:3], NEG, ALU.is_lt, ALU.mult, r=["gate", "gt3", "seln"], w=["seln"])
                selT = vbf(selT_o, T)
                for g4 in range(NTL // 4):
                    b = 6 + g4 % 2
                    for c in range(4):
                        tl = g4 * 4 + c
                        TR(PS[b][0:8, c * 128:(c + 1) * 128], seln[:, tl, :], r=["cst", "seln"], w=PK(b, c * 128, (c + 1) * 128))
                    CP("dve", selT[0:8, g4 * 512:(g4 + 1) * 512], PS[b][0:8, :], r=PK(b), w=["Vt"])
                qb, kb = vbf(qb_o, T), vbf(kb_o, T)
                CP("act", qb, rq[:, :], r=["raw0"], w=["ATa"])
                CP("dve", kb, rk_[:, :], r=["raw1"], w=["wTa"])
                vtk = vbf(vtk_o, NTL * 128).rearrange("p (n d) -> p n d", n=NTL)
                to_tokmajor(rv, "raw2", lambda g4: vtk[:, g4 * 4:(g4 + 1) * 4, :].rearrange("p n d -> p (n d)"), "ua")
                yh = vbf(yh_o[m % 2], T)
                yk = "yh"
                eb = cstb[0:8, C_EBLK:C_EBLK + 1024].rearrange("p (n k) -> p n k", n=8)
                items = []
                for i in range(NTL):
                    own = i // 2
                    keys = [(kt, "sel") for kt in range(2 * own)]
                    if i % 2:
                        keys.append((i - 1, "full"))
                    keys.append((i, "diag"))
                    for n_, (kt, kind) in enumerate(keys):
                        items.append((i, kt, kind, n_ == 0, n_ == len(keys) - 1))
                LA = 2
                ppb = (0, 1, 4)

                def stage12(idx):
                    i, kt, kind, first, last = items[idx]
                    qs = slice(i * 128, (i + 1) * 128)
                    bk = ppb[idx % 3]
                    pp = PS[bk][:, 0:128]
                    ptb, ptk = vbf(pt_o[idx % 4], 128), "pt%d" % (idx % 4)
                    MM(pp, kb[:, kt * 128:(kt + 1) * 128], qb[:, qs], start=True, stop=(kind != "sel"),
                       r=["wTa", "ATa"], w=PK(bk))
                    if kind == "sel":
                        MM(pp, eb[:, kt // 2, :], selT[0:8, qs], start=False, stop=True, r=["cstb", "Vt"], w=PK(bk))
                    ACT(ptb, pp, AF.Exp, r=PK(bk), w=[ptk], scale=128.0 ** -0.5)
                    if kind == "diag":
                        TT("pool", ptb, ptb, triu_b, ALU.mult, r=[ptk, "cstb"], w=[ptk])

                def stage3(idx):
                    i, kt, kind, first, last = items[idx]
                    qs = slice(i * 128, (i + 1) * 128)
                    ptb, ptk = vbf(pt_o[idx % 4], 128), "pt%d" % (idx % 4)
                    bo, bl = (2, 3) if i % 2 == 0 else (6, 7)
                    po, pl = PS[bo][:, 0:128], PS[bl][:, 0:128]
                    MM(po, vtk[:, kt, :], ptb, start=first, stop=last, r=["ua", ptk], w=PK(bo))
                    MM(pl, ones_b, ptb, start=first, stop=last, r=["cstb", ptk], w=PK(bl))
                    if last:
                        rl = vf32(rl_o[i % 2], 128)
                        rlk = "rl%d" % (i % 2)
                        S.add("dve", lambda e, rl=rl, pl=pl: e.reciprocal(rl, pl), PK(bl), [rlk])
                        TT("dve", yh[:, qs], po, rl, ALU.mult, r=PK(bo) + [rlk], w=[yk])

                for idx in range(len(items) + LA):
                    if idx < len(items):
                        stage12(idx)
                    if idx - LA >= 0:
                        stage3(idx - LA)
                S.dma("sp", Y_d[1024 + m * 128:1024 + (m + 1) * 128, :], yh, reads=[yk], writes=["Ys"])

            for h in range(ngh):
                if gdn_on:
                    gdn_head(h)
            if moba_on:
                S.dma("sp", cbuf[0:32, 0:T], rope_d[:, 0:T], writes=["cbuf"])
                S.dma("sp", vf32(Kt_o, T)[0:32, :], rope_d[:, T:2 * T], writes=["Kt"])
                for m in range(nmh):
                    moba_head(m)
            ring["nstg"], ring["nwb"] = NSTG, NWB

        if 0 in phases:
            phase0()
        if 1 in phases:
            phase1()
        if 2 in phases:
            S.barrier()
            phase2()
            S.barrier()
        if 3 in phases:
            phase3()
        S.finalize(st)
        S.run_block()
    return nc


def host_consts(T):
    c = np.zeros((128, NCONST), np.float32)
    i = np.arange(128)
    c[:, C_ID:C_ID + 128] = np.eye(128)
    c[:, C_ONE:C_ONE + 128] = 1.0
    c[:, C_TRIU:C_TRIU + 128] = (i[:, None] <= i[None, :])
    c[:, C_SGTJ:C_SGTJ + 128] = (i[:, None] > i[None, :])
    c[:, C_TRIL:C_TRIL + 128] = (i[:, None] >= i[None, :])
    c[:, C_STRICT:C_STRICT + 128] = (i[:, None] > i[None, :])
    for m in range(16):
        c[m + 16, C_PM + m] = -1.0
        c[m, C_PM + m + 16] = 1.0
    for n in range(8):
        c[n, C_EBLK + n * 128:C_EBLK + (n + 1) * 128] = 1.0
    inv = 500000.0 ** (-np.arange(0, 32, 2, dtype=np.float32) / 32)
    ang = np.arange(T, dtype=np.float32)[None, :] * inv[:, None].astype(np.float32)
    rope = np.zeros((32, 2 * T), np.float32)
    rope[0:16, 0:T] = np.cos(ang)
    rope[16:32, 0:T] = np.cos(ang)
    rope[0:16, T:] = np.sin(ang)
    rope[16:32, T:] = np.sin(ang)
    return c, rope


def col16(v):
    return np.ascontiguousarray(np.asarray(v, np.float32).reshape(-1, 128).T)


def host_vecs(inp, b, parity=0):
    v = np.zeros((128, NVEC), np.float32)
    v[:, V_C:V_C + 16] = col16(inp["c"][b])
    v[:, V_BADA:V_BADA + 144] = col16(inp["b_ada"][0])
    for i, nm in enumerate(("ffn1_pre_g", "ffn1_post_g", "mix_pre_g", "mix_post_g", "ffn2_pre_g", "ffn2_post_g")):
        v[:, V_G + i * 16:V_G + (i + 1) * 16] = col16(inp[nm][0])
    cw = np.asarray(inp["gdn_conv_w"][0], np.float32)
    v[:, V_CONV:V_CONV + 96] = cw.T.reshape(24, 128, 4).transpose(1, 0, 2).reshape(128, 96)
    v[:, V_ALOG:V_ALOG + 128] = np.tile(np.asarray(inp["gdn_a_log"][0], np.float32), 16)[None, :]
    v[:, V_DTB:V_DTB + 128] = np.tile(np.asarray(inp["gdn_dt_bias"][0], np.float32), 16)[None, :]
    v[:, V_NG] = np.asarray(inp["gdn_norm_g"][0], np.float32)
    v[:, V_PAR + parity] = 1.0
    return v


_NC_CACHE = {}


def kernel(**inp):
    inp = {k: np.asarray(v) for k, v in inp.items()}
    B, T, _ = inp["x"].shape
    if T not in _NC_CACHE:
        _NC_CACHE[T] = build_nc(T)
    nc = _NC_CACHE[T]
    consts, rope = host_consts(T)
    shared = dict(consts=consts, rope=rope, w_ada=np.ascontiguousarray(inp["w_ada"][0]),
                  w1g=np.ascontiguousarray(inp["ffn1_w_gate"][0]), w1u=np.ascontiguousarray(inp["ffn1_w_up"][0]),
                  w1d=np.ascontiguousarray(inp["ffn1_w_down"][0]), w2g=np.ascontiguousarray(inp["ffn2_w_gate"][0]),
                  w2u=np.ascontiguousarray(inp["ffn2_w_up"][0]), w2d=np.ascontiguousarray(inp["ffn2_w_down"][0]),
                  w_in=np.ascontiguousarray(inp["w_in"][0]), w_out=np.ascontiguousarray(inp["w_out"][0]))
    in_maps = []
    for core in range(8):
        b = core // 2
        m = dict(shared)
        m["x"] = np.ascontiguousarray(inp["x"][b])
        m["vecs"] = host_vecs(inp, b, core % 2)
        in_maps.append(m)
    res = run_bass_kernel_spmd(nc, in_maps, core_ids=list(range(8)))
    if T >= 2048:
        out = np.stack([np.concatenate([np.asarray(res.results[2 * b + p]["out"], np.float32) for p in range(2)], axis=0)
                        for b in range(B)], axis=0)
    else:
        out = np.stack([np.asarray(res.results[2 * b]["out"], np.float32) for b in range(B)], axis=0)
    return out
```
